# Optimizing a Trainium2 kernel written in Bass

```python
import jax, jax.numpy as jnp
from jax import lax
import numpy as np

D_MODEL = 2048
BATCH = 2
SEQ = 4096
DEPTH = 1
DEC_BATCH = 8
DEC_SEQ = 16
PAST_LEN = 4096

CHUNK = 64
QBLOCK = 128
HEAD_DIM = 128
SB_HEADS = 6
BAND_HEADS = 6
MEM_HEADS = 4
SB_WIDTH = SB_HEADS * HEAD_DIM
BAND_WIDTH = BAND_HEADS * HEAD_DIM
MEM_WIDTH = MEM_HEADS * HEAD_DIM
N_MEM = 256
BAND_LEFT_CHUNKS = 8
BAND_ROWS = BAND_LEFT_CHUNKS * CHUNK
MAX_REL = 256
IN_WIDTH = 4 * SB_WIDTH + 4 * BAND_WIDTH + 2 * MEM_WIDTH + 3 * D_MODEL
RMS_EPS = 1e-6
NEG_INF = -1e30

kernel_name = "sandwich_gated_stickbreak_chunkband_stream_step"


def rms_norm(x, g):
    xf = x.astype(jnp.float32)
    xf = xf * lax.rsqrt(jnp.mean(xf * xf, axis=-1, keepdims=True) + RMS_EPS)
    return (xf * g.astype(jnp.float32)).astype(x.dtype)


def in_projection(x, g_pre, w_in):
    b, t, _ = x.shape
    widths = [SB_WIDTH] * 4 + [BAND_WIDTH] * 4 + [MEM_WIDTH] * 2 + [D_MODEL] * 3
    points = np.cumsum(widths)[:-1].tolist()
    parts = jnp.split(rms_norm(x, g_pre) @ w_in, points, axis=-1)
    sb_q, sb_k, sb_v, sb_g, bd_q, bd_k, bd_v, bd_g, mm_q, mm_g, mg_sb, mg_bd, mg_mm = parts
    hs = lambda a, n: a.reshape(b, t, n, HEAD_DIM)
    return (hs(sb_q, SB_HEADS), hs(sb_k, SB_HEADS), hs(sb_v, SB_HEADS), sb_g,
            hs(bd_q, BAND_HEADS), hs(bd_k, BAND_HEADS), hs(bd_v, BAND_HEADS), bd_g,
            hs(mm_q, MEM_HEADS), mm_g, mg_sb, mg_bd, mg_mm)


def memory_kv(mem, g_mem, w_mem_kv):
    b, m, _ = mem.shape
    mk, mv = jnp.split(rms_norm(mem, g_mem) @ w_mem_kv, 2, axis=-1)
    return (mk.reshape(b, m, MEM_HEADS, HEAD_DIM), mv.reshape(b, m, MEM_HEADS, HEAD_DIM))


def stick_breaking(q, k, v, q_pos, k_pos):
    z = jnp.einsum('bqhd,bkhd->bhqk', q, k).astype(jnp.float32) * (HEAD_DIM ** -0.5)
    mask = k_pos[None, :] < q_pos[:, None]
    log_1m = jnp.where(mask, jax.nn.log_sigmoid(-z), 0.0)
    suffix = lax.cumsum(log_1m, axis=3, reverse=True) - log_1m
    w = jnp.where(mask, jnp.exp(jax.nn.log_sigmoid(z) + suffix), 0.0)
    return jnp.einsum('bhqk,bkhd->bqhd', w.astype(v.dtype), v)


def stick_breaking_prompt(q, k, v):
    t = q.shape[1]
    pos = jnp.arange(t)
    outs = []
    for i in range(t // QBLOCK):
        lo, hi = i * QBLOCK, (i + 1) * QBLOCK
        outs.append(stick_breaking(q[:, lo:hi], k[:, :hi], v[:, :hi], pos[lo:hi], pos[:hi]))
    return jnp.concatenate(outs, axis=1)


def band_attention(q, k, v, q_pos, k_pos, rel_bias):
    s = jnp.einsum('bnqhd,bnkhd->bnhqk', q, k).astype(jnp.float32) * (HEAD_DIM ** -0.5)
    rel = jnp.clip(q_pos[:, :, None] - k_pos[:, None, :], -MAX_REL, MAX_REL) + MAX_REL
    bias = jnp.moveaxis(rel_bias.astype(jnp.float32)[:, rel], 0, 1)
    s = jnp.where((k_pos >= 0)[None, :, None, None, :], s + bias[None], NEG_INF)
    p = jax.nn.softmax(s, axis=-1)
    return jnp.einsum('bnhqk,bnkhd->bnqhd', p.astype(v.dtype), v)


def band_prompt(q, k, v, rel_bias):
    b, t, h, d = q.shape
    nc = t // CHUNK
    lc = BAND_LEFT_CHUNKS

    def gather_band(a):
        ac = jnp.pad(a.reshape(b, nc, CHUNK, h, d), ((0, 0), (lc, 0), (0, 0), (0, 0), (0, 0)))
        return jnp.concatenate([ac[:, j:j + nc] for j in range(lc + 1)], axis=2)

    c0 = jnp.arange(nc)[:, None] * CHUNK
    q_pos = c0 + jnp.arange(CHUNK)[None, :]
    k_pos = c0 - lc * CHUNK + jnp.arange((lc + 1) * CHUNK)[None, :]
    out = band_attention(q.reshape(b, nc, CHUNK, h, d), gather_band(k), gather_band(v),
                         q_pos, k_pos, rel_bias)
    return out.reshape(b, t, h, d)


def memory_attention(q, mk, mv):
    s = jnp.einsum('bthd,bmhd->bhtm', q, mk).astype(jnp.float32) * (HEAD_DIM ** -0.5)
    p = jax.nn.softmax(s, axis=-1)
    return jnp.einsum('bhtm,bmhd->bthd', p.astype(mv.dtype), mv)


def merge_output(x, o_sb, g_sb, o_bd, g_bd, o_mm, g_mm, mg_sb, mg_bd, mg_mm,
                 w_up_sb, w_up_band, w_up_mem, w_out, g_post):
    b, t, _ = x.shape

    def branch(o, g, w):
        return (o.reshape(b, t, -1) * jax.nn.silu(g)) @ w

    merged = (jax.nn.sigmoid(mg_sb) * branch(o_sb, g_sb, w_up_sb)
              + jax.nn.sigmoid(mg_bd) * branch(o_bd, g_bd, w_up_band)
              + jax.nn.sigmoid(mg_mm) * branch(o_mm, g_mm, w_up_mem))
    return x + rms_norm(merged @ w_out, g_post)


def setup_inputs(seed: int = 0) -> dict:
    key = jax.random.key(seed)
    ks = jax.random.split(key, 24)
    nrm = lambda k, shape, scale: jax.random.normal(k, shape, jnp.float32) * scale
    band_rows = min(BAND_ROWS, PAST_LEN)
    return {
        'x_prompt': nrm(ks[0], (BATCH, SEQ, D_MODEL), 1.0),
        'x_sample': nrm(ks[1], (DEC_BATCH, DEC_SEQ, D_MODEL), 1.0),
        'cache_sb_k': nrm(ks[2], (DEPTH, DEC_BATCH, PAST_LEN, SB_HEADS, HEAD_DIM), 1.0),
        'cache_sb_v': nrm(ks[3], (DEPTH, DEC_BATCH, PAST_LEN, SB_HEADS, HEAD_DIM), 1.0),
        'cache_band_k': nrm(ks[4], (DEPTH, DEC_BATCH, band_rows, BAND_HEADS, HEAD_DIM), 1.0),
        'cache_band_v': nrm(ks[5], (DEPTH, DEC_BATCH, band_rows, BAND_HEADS, HEAD_DIM), 1.0),
        'cache_mem_k': nrm(ks[6], (DEPTH, DEC_BATCH, N_MEM, MEM_HEADS, HEAD_DIM), 1.0),
        'cache_mem_v': nrm(ks[7], (DEPTH, DEC_BATCH, N_MEM, MEM_HEADS, HEAD_DIM), 1.0),
        'mem_prompt': nrm(ks[8], (BATCH, N_MEM, D_MODEL), 1.0),
        'g_pre': 1.0 + nrm(ks[9], (DEPTH, D_MODEL), 0.02),
        'w_in': nrm(ks[10], (DEPTH, D_MODEL, IN_WIDTH), D_MODEL ** -0.5),
        'rel_bias': nrm(ks[11], (DEPTH, BAND_HEADS, 2 * MAX_REL + 1), 0.1),
        'g_mem': 1.0 + nrm(ks[12], (DEPTH, D_MODEL), 0.02),
        'w_mem_kv': nrm(ks[13], (DEPTH, D_MODEL, 2 * MEM_WIDTH), D_MODEL ** -0.5),
        'w_up_sb': nrm(ks[14], (DEPTH, SB_WIDTH, D_MODEL), SB_WIDTH ** -0.5),
        'w_up_band': nrm(ks[15], (DEPTH, BAND_WIDTH, D_MODEL), BAND_WIDTH ** -0.5),
        'w_up_mem': nrm(ks[16], (DEPTH, MEM_WIDTH, D_MODEL), MEM_WIDTH ** -0.5),
        'w_out': nrm(ks[17], (DEPTH, D_MODEL, D_MODEL), D_MODEL ** -0.5),
        'g_post': 1.0 + nrm(ks[18], (DEPTH, D_MODEL), 0.02),
    }


def reference(x_prompt, x_sample, cache_sb_k, cache_sb_v, cache_band_k, cache_band_v,
              cache_mem_k, cache_mem_v, mem_prompt, g_pre, w_in, rel_bias, g_mem, w_mem_kv,
              w_up_sb, w_up_band, w_up_mem, w_out, g_post):
    past = cache_sb_k.shape[2]
    r_band = cache_band_k.shape[2]
    n_new = x_sample.shape[1]
    q_pos_s = past + jnp.arange(n_new)
    k_pos_sb = jnp.arange(past + n_new)
    k_pos_bd = past - r_band + jnp.arange(r_band + n_new)

    xp, xs = x_prompt, x_sample
    sbk_p, sbv_p, bdk_p, bdv_p, mk_p, mv_p = [], [], [], [], [], []
    sbk_s, sbv_s, bdk_s, bdv_s = [], [], [], []
    for l in range(DEPTH):
        (sq, sk, sv, sg, bq, bk, bv, bg, mq, mg, gsb, gbd, gmm) = in_projection(xp, g_pre[l], w_in[l])
        mk, mv = memory_kv(mem_prompt, g_mem[l], w_mem_kv[l])
        o_sb = stick_breaking_prompt(sq, sk, sv)
        o_bd = band_prompt(bq, bk, bv, rel_bias[l])
        o_mm = memory_attention(mq, mk, mv)
        xp = merge_output(xp, o_sb, sg, o_bd, bg, o_mm, mg, gsb, gbd, gmm,
                          w_up_sb[l], w_up_band[l], w_up_mem[l], w_out[l], g_post[l])
        sbk_p.append(sk); sbv_p.append(sv)
        bdk_p.append(bk[:, -BAND_ROWS:]); bdv_p.append(bv[:, -BAND_ROWS:])
        mk_p.append(mk); mv_p.append(mv)

        (sq2, sk2, sv2, sg2, bq2, bk2, bv2, bg2, mq2, mg2, gsb2, gbd2, gmm2) = in_projection(xs, g_pre[l], w_in[l])
        k_all = jnp.concatenate([cache_sb_k[l], sk2], axis=1)
        v_all = jnp.concatenate([cache_sb_v[l], sv2], axis=1)
        o_sb2 = stick_breaking(sq2, k_all, v_all, q_pos_s, k_pos_sb)
        kb = jnp.concatenate([cache_band_k[l], bk2], axis=1)[:, None]
        vb = jnp.concatenate([cache_band_v[l], bv2], axis=1)[:, None]
        o_bd2 = band_attention(bq2[:, None], kb, vb, q_pos_s[None], k_pos_bd[None], rel_bias[l])[:, 0]
        o_mm2 = memory_attention(mq2, cache_mem_k[l], cache_mem_v[l])
        xs = merge_output(xs, o_sb2, sg2, o_bd2, bg2, o_mm2, mg2, gsb2, gbd2, gmm2,
                          w_up_sb[l], w_up_band[l], w_up_mem[l], w_out[l], g_post[l])
        sbk_s.append(sk2); sbv_s.append(sv2)
        bdk_s.append(bk2); bdv_s.append(bv2)

    return (xp, xs,
            jnp.stack(sbk_p), jnp.stack(sbv_p), jnp.stack(bdk_p), jnp.stack(bdv_p),
            jnp.stack(mk_p), jnp.stack(mv_p),
            jnp.stack(sbk_s), jnp.stack(sbv_s), jnp.stack(bdk_s), jnp.stack(bdv_s))
```

```python
import numpy as np
from contextlib import ExitStack
import concourse.bass as bass
import concourse.mybir as mybir
from concourse.bass_utils import run_bass_kernel_spmd

F32 = mybir.dt.float32
BF16 = mybir.dt.bfloat16
AF = mybir.ActivationFunctionType
ALU = mybir.AluOpType
AX = mybir.AxisListType

D = 2048
NEG = -30000.0
QS = 128 ** -0.5
IN_W = 13312
OWN0 = 512
SMP0 = 1536
NTOK = 1552
NQ = 1040
EPS = 1e-6


class Tk:
    __slots__ = ("w", "r", "x")

    def __init__(self, x=False):
        self.w = {}
        self.r = {}
        self.x = x


class Sched:
    def __init__(self, nc, es):
        self.nc = nc
        self.es = es
        self.E = dict(pe=nc.tensor, act=nc.scalar, dve=nc.vector, pool=nc.gpsimd, sp=nc.sync)
        self.sem = {}
        self.cnt = {}
        self.waited = {e: {} for e in self.E}
        for e in self.E:
            self._sem(e)

    def _sem(self, key):
        if key not in self.sem:
            self.sem[key] = self.es.enter_context(self.nc.semaphore("sem_" + key))
            self.cnt[key] = 0
        return self.sem[key]

    def _wait(self, e, key, val):
        if key == "pe" and e == "pe":
            return
        if self.waited[e].get(key, 0) >= val:
            return
        self.E[e].wait_ge(self.sem[key], val)
        self.waited[e][key] = val

    def _deps(self, e, reads, writes):
        for t in reads:
            for k, v in t.w.items():
                self._wait(e, k, v)
            if t.x:
                for k, v in t.r.items():
                    if k != e:
                        self._wait(e, k, v)
        for t in writes:
            for k, v in t.w.items():
                self._wait(e, k, v)
            for k, v in t.r.items():
                self._wait(e, k, v)

    def op(self, e, fn, reads=(), writes=(), signal=True):
        self._deps(e, reads, writes)
        ins = fn()
        if signal:
            self.cnt[e] += 1
            ins.then_inc(self.sem[e], 1)
            v = self.cnt[e]
        else:
            v = self.cnt[e] + 1
        for t in reads:
            t.r[e] = v
        for t in writes:
            t.w = {e: v}
            t.r = {}
        return ins

    def dma(self, q, out, in_, key, reads=(), writes=()):
        self._sem(key)
        self._deps(q, reads, writes)
        ins = self.E[q].dma_start(out=out, in_=in_)
        self.cnt[key] += 16
        ins.then_inc(self.sem[key], 16)
        v = self.cnt[key]
        for t in reads:
            t.r[key] = v
        for t in writes:
            t.w = {key: v}
            t.r = {}

    def barrier(self):
        keys = [k for k in self.sem if self.cnt[k] > 0]
        for e in self.E:
            for k in keys:
                if k != e:
                    self._wait(e, k, self.cnt[k])

    def finish(self, e="sp"):
        for k in self.sem:
            if self.cnt[k] > 0 and k != e:
                self._wait(e, k, self.cnt[k])


def split512(width, step=384):
    out = []
    c = 0
    while c < width:
        w = min(step, width - c)
        out.append((c, w))
        c += w
    return out


class _Stop(Exception):
    pass


def build_program(stop=None):
    st = {}
    try:
        return _build(stop, st)
    except _Stop:
        return st["nc"]


def _build(stop, st):
    nc = bass.Bass("TRN2", target_bir_lowering=False)
    es = ExitStack()
    st["nc"] = nc; st["es"] = es

    def maybe_stop(tag):
        if stop == tag:
            S.finish("sp")
            S.barrier()
            raise _Stop()

    def din(name, shape):
        return nc.dram_tensor(name, list(shape), F32, kind="ExternalInput").ap()

    def dout(name, shape):
        return nc.dram_tensor(name, list(shape), F32, kind="ExternalOutput").ap()

    xctx = din("xctx", (4096, D))
    xs = din("xs", (16, D))
    memx = din("memx", (256, D))
    csk = din("csk", (4096, 768)); csv = din("csv", (4096, 768))
    cbk = din("cbk", (512, 768)); cbv = din("cbv", (512, 768))
    cmk = din("cmk", (256, 512)); cmv = din("cmv", (256, 512))
    w_in = din("w_in", (D, IN_W))
    w_mkv = din("w_mkv", (D, 1024))
    w_up = [din("w_up_sb", (768, D)), din("w_up_bd", (768, D)), din("w_up_mm", (512, D))]
    w_out = din("w_out", (D, D))
    gpreT = din("gpreT", (128, 16)); gmemT = din("gmemT", (128, 16))
    gpost = din("gpost", (128, D))
    gpre_b = din("gpre_b", (128, D)); gmem_b = din("gmem_b", (128, D))
    c_ident = din("c_ident", (128, 128)); c_tri = din("c_tri", (128, 128)); c_ones = din("c_ones", (128, 128))
    c_maskB = din("c_maskB", (128, 4 * 512))
    c_maskS = din("c_maskS", (128, 96))
    c_bc = din("c_bc", (128, 32))
    c_bT = din("c_bT", (128, 2 * 6 * 5 * 64))
    c_bTs = din("c_bTs", (128, 6 * 5 * 16))
    c_negrow = din("c_negrow", (1, 12 * 128))

    o_y = dout("o_y", (1024, D)); o_ys = dout("o_ys", (16, D))
    o_sbk = dout("o_sbk", (1024, 768)); o_sbv = dout("o_sbv", (1024, 768))
    o_bdk = dout("o_bdk", (512, 768)); o_bdv = dout("o_bdv", (512, 768))
    o_mk = dout("o_mk", (256, 512)); o_mv = dout("o_mv", (256, 512))
    o_sbks = dout("o_sbks", (16, 768)); o_sbvs = dout("o_sbvs", (16, 768))
    o_bdks = dout("o_bdks", (16, 768)); o_bdvs = dout("o_bdvs", (16, 768))

    kT_scr = nc.dram_tensor("kT_scr", [128, 6, 4096], BF16, kind="Internal").ap()
    v_scr = nc.dram_tensor("v_scr", [128, 32, 768], BF16, kind="Internal").ap()
    t_kscr = Tk(); t_vscr = Tk()

    S = Sched(nc, es)

    def sb(name, shape, dt):
        return es2.enter_context(nc.sbuf_tensor(name, list(shape), dt))

    es2 = es
    ps = es.enter_context(nc.psum_tensor("ps", [128, 8, 512], F32))
    pst = [Tk(True) for _ in range(8)]
    ident = sb("ident", (128, 128), BF16); tri = sb("tri", (128, 128), BF16); ones = sb("ones", (128, 128), BF16)
    maskB = sb("maskB", (128, 4, 512), BF16); maskS = sb("maskS", (128, 96), BF16)
    bc = sb("bc", (128, 32), F32)
    bT = sb("bT", (128, 2, 6, 5, 64), BF16); bTs = sb("bTs", (128, 6, 5, 16), BF16)
    negrow = sb("negrow", (1, 12, 128), BF16)
    gpre = sb("gpre", (128, 16), F32); gmem = sb("gmem", (128, 16), F32)
    t_const = Tk()
    stat = sb("stat", (128, 16), F32); t_stat = Tk()
    mkT = sb("mkT", (128, 4, 256), BF16); t_mkT = Tk()
    mvb = sb("mvb", (128, 2, 512), BF16); t_mvb = Tk()
    mg_scr = nc.dram_tensor("mg_scr", [128, 16, NQ], BF16, kind="Internal").ap()
    t_mgscr = Tk()
    esH = ExitStack()
    st["esH"] = esH
    es2 = esH
    hT = sb("hT", (128, 16, NTOK), BF16)
    t_hT = [Tk() for _ in range(4)]
    es2 = es
    t_xst = [Tk(), Tk()]; t_junk = Tk()
    xst = [None, None]; junk = [None]

    cast_i = [0]

    def cast_load(dst, src, writes, key=None):
        cast_i[0] += 1
        S.dma("pool", dst, src, key or ("cst%d" % cast_i[0]), writes=writes)

    def emit_const_loads():
        _ct = [Tk() for _ in range(12)]
        cast_load(ident[:], c_ident, [_ct[0]]); cast_load(tri[:], c_tri, [_ct[1]]); cast_load(ones[:], c_ones, [_ct[2]])
        cast_load(maskB[:].rearrange("p a b -> p (a b)"), c_maskB, [_ct[3]]); cast_load(maskS[:], c_maskS, [_ct[4]])
        cast_load(bT[:].rearrange("p a b c d -> p (a b c d)"), c_bT, [_ct[5]])
        cast_load(bTs[:].rearrange("p a b c -> p (a b c)"), c_bTs, [_ct[6]])
        cast_load(negrow[:].rearrange("p a b -> p (a b)"), c_negrow, [_ct[7]])
        S.dma("sp", bc[:], c_bc, "cf0", writes=[_ct[8]])
        S.dma("sp", gpre[:], gpreT, "cf1", writes=[_ct[9]])
        S.dma("sp", gmem[:], gmemT, "cf2", writes=[_ct[10]])
        for _t in _ct:
            for _k, _v in _t.w.items():
                t_const.w[_k] = max(t_const.w.get(_k, 0), _v)


    evac_i = [0]

    def evac(out, in_, reads, writes, scale=None, eng=None):
        if eng is None:
            evac_i[0] += 1
            eng = "act" if evac_i[0] % 2 else "dve"
        if eng == "act":
            if scale is None:
                S.op("act", lambda: nc.scalar.activation(out=out, in_=in_, func=AF.Copy), reads, writes)
            else:
                S.op("act", lambda: nc.scalar.activation(out=out, in_=in_, func=AF.Copy, scale=scale), reads, writes)
        else:
            if scale is None:
                S.op("dve", lambda: nc.vector.tensor_copy(out=out, in_=in_), reads, writes)
            else:
                S.op("dve", lambda: nc.vector.tensor_scalar(out=out, in0=in_, scalar1=scale, scalar2=None, op0=ALU.mult), reads, writes)

    bank_i = [0]

    def bank(lo=0, hi=8):
        bank_i[0] += 1
        return lo + bank_i[0] % (hi - lo)

    def mm(out, lhsT, rhs, start, stop, reads, writes):
        S.op("pe", lambda: nc.tensor.matmul(out, lhsT, rhs, start=start, stop=stop), reads, writes, signal=stop)

    xi = [0]
    xn = [None] * 4
    t_xn = [Tk() for _ in range(4)]
    t_stat4 = [Tk() for _ in range(4)]
    gb_ref = [None]
    t_gpb = Tk()

    def prep_block(src_rows, nrows, gb, slot, q="pool"):
        xi[0] += 1
        sl = xi[0] % 2
        c = slot * 4
        S.dma(q, xst[sl][0:nrows, :], src_rows, "xst%d" % sl, writes=[t_xst[sl]])
        S.op("act", lambda: nc.scalar.activation(out=xn[slot][0:nrows, :], in_=xst[sl][0:nrows, :], func=AF.Square,
                                                 accum_out=stat[0:nrows, c:c + 1]), [t_xst[sl]], [t_xn[slot], t_stat4[slot]])
        S.op("act", lambda: nc.scalar.activation(out=stat[0:nrows, c + 1:c + 2], in_=stat[0:nrows, c:c + 1], func=AF.Ln,
                                                 scale=1.0 / D, bias=EPS), [t_stat4[slot]], [t_stat4[slot]])
        S.op("act", lambda: nc.scalar.activation(out=stat[0:nrows, c + 2:c + 3], in_=stat[0:nrows, c + 1:c + 2], func=AF.Exp,
                                                 scale=-0.5), [t_stat4[slot]], [t_stat4[slot]])
        S.op("dve", lambda: nc.vector.scalar_tensor_tensor(out=xn[slot][0:nrows, :], in0=xst[sl][0:nrows, :],
                                                           scalar=stat[0:nrows, c + 2:c + 3], in1=gb[0:nrows, :],
                                                           op0=ALU.mult, op1=ALU.mult), [t_xst[sl], t_stat4[slot], t_gpb], [t_xn[slot]])

    def transpose_block(slot, nrows, dst3, t_dst):
        for j in range(4):
            b = bank(0, 3)
            for i in range(4):
                kc = 4 * j + i
                mm(ps[:, b, i * nrows:(i + 1) * nrows], xn[slot][0:nrows, kc * 128:(kc + 1) * 128], ident[0:nrows, 0:nrows],
                   True, True, [t_xn[slot], t_const], [pst[b]])
            evac(dst3[:, 4 * j:4 * j + 4, :], ps[:, b, 0:4 * nrows].rearrange("p (a b) -> p a b", a=4), [pst[b]], [t_dst])

    wsrc = w_in.rearrange("(kc p) c -> p kc c", p=128)
    with ExitStack() as es1:
        es2 = es1
        xst[0] = sb("xst0", (128, D), F32); xst[1] = sb("xst1", (128, D), F32)
        for i in range(4):
            xn[i] = sb("xn%d" % i, (128, D), BF16)
        gpb = sb("gpb", (128, D), F32)
        S.dma("sp", gpb[:], gpre_b, "gpb", writes=[t_gpb])
        wk = sb("wk", (128, 16, 768), BF16); wv = sb("wv", (128, 16, 768), BF16); t_wkv = Tk()
        t_wv2 = [Tk(), Tk()]
        chunks_v = split512(768)
        cast_load(wv[:, :, 0:chunks_v[0][1]], wsrc[:, :, 1536:1536 + chunks_v[0][1]], [t_wv2[0]], key="wvld0")
        prep_block(xctx[0:128, :], 128, gpb, 0, q="sp")
        prep_block(xctx[128:256, :], 128, gpb, 1, q="sp")
        emit_const_loads()
        for ci, (c0, w) in enumerate(chunks_v):
            if ci > 0:
                cast_load(wv[:, :, c0:c0 + w], wsrc[:, :, 1536 + c0:1536 + c0 + w], [t_wv2[ci]], key="wvld%d" % ci)
        for (c0, w) in split512(768):
            cast_load(wk[:, :, c0:c0 + w], wsrc[:, :, 768 + c0:768 + c0 + w], [t_wkv], key="wkld")
        es2 = es1
        hTt = [sb("hTt%d" % i, (128, 16, 512), BF16) for i in range(2)]
        kT_st = sb("kT_st", (128, 6, 512), BF16); t_kst = Tk()
        v_st = [sb("v_st%d" % i, (128, 768), BF16) for i in range(2)]; t_vst = [Tk(), Tk()]
        ko_st = sb("ko_st", (128, 768), F32); t_kost = Tk()
        vo_st = sb("vo_st", (128, 768), F32); t_vost = Tk()

        t_blk = [[Tk() for _ in range(4)] for _ in range(2)]

        def tile_ap(ti):
            return hT[:, :, (ti - 5) * 512:(ti - 4) * 512] if ti >= 5 else hTt[ti % 2][:, :, :]

        def do_T(n):
            ti, blk = divmod(n, 4)
            transpose_block(n % 4, 128, tile_ap(ti)[:, :, blk * 128:(blk + 1) * 128], t_blk[ti % 2][blk])

        def do_prep(n):
            prep_block(xctx[n * 128:(n + 1) * 128, :], 128, gpb, n % 4)

        do_prep(2)
        do_T(0)
        for n in range(32):
            ti, blk = divmod(n, 4)
            dst = tile_ap(ti)
            if n + 1 < 32:
                do_T(n + 1)
            if n + 3 < 32:
                do_prep(n + 3)
            lt = lambda kc, blk=blk, dst=dst: dst[:, kc, blk * 128:(blk + 1) * 128]
            rd = [t_wkv, t_blk[ti % 2][blk]]
            vs = n % 2
            b1 = bank(5, 8); b2 = bank(5, 8)
            for kc in range(16):
                mm(ps[:, b1, 0:384], lt(kc), wv[:, kc, 0:384], kc == 0, kc == 15, [t_wv2[0], t_blk[ti % 2][blk]], [pst[b1]])
            for kc in range(16):
                mm(ps[:, b2, 0:384], lt(kc), wv[:, kc, 384:768], kc == 0, kc == 15, [t_wv2[1], t_blk[ti % 2][blk]], [pst[b2]])
            evac(v_st[vs][:, 0:384], ps[:, b1, 0:384], [pst[b1]], [t_vst[vs]])
            evac(v_st[vs][:, 384:768], ps[:, b2, 0:384], [pst[b2]], [t_vst[vs]])
            S.dma("sp", v_scr[:, n, :], v_st[vs][:], "vst%d" % vs, reads=[t_vst[vs]], writes=[t_vscr])
            if ti >= 6:
                evac(vo_st[:, 0:384], ps[:, b1, 0:384], [pst[b1]], [t_vost])
                evac(vo_st[:, 384:768], ps[:, b2, 0:384], [pst[b2]], [t_vost])
                orow = (ti - 6) * 512 + blk * 128
                S.dma("sp", o_sbv[orow:orow + 128, :], vo_st[:], "vost", reads=[t_vost])
                b1 = bank(5, 8); b2 = bank(5, 8)
                for kc in range(16):
                    mm(ps[:, b1, :], lt(kc), wk[:, kc, 0:512], kc == 0, kc == 15, rd, [pst[b1]])
                for kc in range(16):
                    mm(ps[:, b2, 0:256], lt(kc), wk[:, kc, 512:768], kc == 0, kc == 15, rd, [pst[b2]])
                evac(ko_st[:, 0:512], ps[:, b1, :], [pst[b1]], [t_kost])
                evac(ko_st[:, 512:768], ps[:, b2, 0:256], [pst[b2]], [t_kost])
                S.dma("sp", o_sbk[orow:orow + 128, :], ko_st[:], "kost", reads=[t_kost])
            if blk == 3:
                for h in range(6):
                    b = bank(3, 5)
                    for kc in range(16):
                        mm(ps[:, b, :], wk[:, kc, h * 128:(h + 1) * 128], dst[:, kc, :], kc == 0, kc == 15, [t_wkv] + t_blk[ti % 2], [pst[b]])
                    evac(kT_st[:, h, :], ps[:, b, :], [pst[b]], [t_kst])
                S.dma("sp", kT_scr[:, :, ti * 512:(ti + 1) * 512], kT_st[:], "kst", reads=[t_kst], writes=[t_kscr])
        prep_block(xs, 16, gpb, 3)
        transpose_block(3, 16, hT[:, :, SMP0:SMP0 + 16], t_hT[3])
        S.barrier()
        maybe_stop('p1')
    es2 = es

    with ExitStack() as esM:
        es2 = esM
        ZT = sb("ZT", (128, 16, NQ), BF16); t_ZT = [Tk() for _ in range(16)]
        wbuf = [sb("wbuf%d" % i, (128, 16, 384), BF16) for i in range(2)]; t_wb = [Tk(), Tk()]
        QT = sb("QT", (128, 6, NQ), BF16); t_QT = Tk()
        ost = [sb("ost%d" % i, (128, 384), F32) for i in range(2)]; t_ost = [Tk(), Tk()]
        rden = sb("rden", (128, 512), F32); t_rden = Tk()
        otmp = sb("otmp", (128, 512), F32); t_otmp = Tk()
        KTn = sb("KTn", (128, 6, 128), BF16); t_KTn = Tk()
        Vn = sb("Vn", (128, 768), BF16); t_Vn = Tk()
        wi = [0]
        oi = [0]

        def load_w(src3, w):
            wi[0] += 1
            sl = wi[0] % 2
            S.dma("pool", wbuf[sl][:, :, 0:w], src3, "wb%d" % sl, writes=[t_wb[sl]])
            return sl

        def out_rows(dst_rows, nrows, b_list, widths):
            oi[0] += 1
            sl = oi[0] % 2
            c = 0
            for b, w in zip(b_list, widths):
                evac(ost[sl][0:nrows, c:c + w], ps[0:nrows, b, 0:w], [pst[b]], [t_ost[sl]])
                c += w
            S.dma("sp", dst_rows, ost[sl][0:nrows, 0:c], "ost%d" % sl, reads=[t_ost[sl]])

        OWN_TILES = [(OWN0, 512, 1), (OWN0 + 512, 512, 2), (SMP0, 16, 3)]
        ALL_TILES = [(0, 512, 0)] + OWN_TILES

        def proj_fm(sl, j, tiles, fn):
            for (t0, n, ht) in tiles:
                b = bank(0, 4)
                for kc in range(16):
                    mm(ps[:, b, 0:n], wbuf[sl][:, kc, j * 128:(j + 1) * 128], hT[:, kc, t0:t0 + n], kc == 0, kc == 15,
                       [t_wb[sl], t_hT[ht]], [pst[b]])
                fn(b, t0, n)

        def proj_tm(sl, w, t0, nrows, ht, lo=4, hi=8):
            b = bank(lo, hi)
            for kc in range(16):
                mm(ps[0:nrows, b, 0:w], hT[:, kc, t0:t0 + nrows], wbuf[sl][:, kc, 0:w], kc == 0, kc == 15,
                   [t_wb[sl], t_hT[ht]], [pst[b]])
            return b

        prefetched = {}

        def prefetch_w(col0, w):
            prefetched[col0] = load_w(wsrc[:, :, col0:col0 + w], w)

        def q_seg(col0, width, nheads):
            for (c0, w) in split512(width):
                if c0 == 0 and col0 in prefetched:
                    sl = prefetched.pop(col0)
                else:
                    sl = load_w(wsrc[:, :, col0 + c0:col0 + c0 + w], w)
                for j in range(w // 128):
                    head = c0 // 128 + j
                    proj_fm(sl, j, OWN_TILES, lambda b, t0, n, head=head: evac(
                        QT[:, head, t0 - OWN0:t0 - OWN0 + n], ps[:, b, 0:n], [pst[b]], [t_QT], scale=QS))

        def g_seg(col0, width, zoff):
            for (c0, w) in split512(width):
                sl = load_w(wsrc[:, :, col0 + c0:col0 + c0 + w], w)
                for j in range(w // 128):
                    zc = zoff + c0 // 128 + j
                    proj_fm(sl, j, OWN_TILES, lambda b, t0, n, zc=zc: S.op(
                        "act", lambda: nc.scalar.activation(out=ZT[:, zc, t0 - OWN0:t0 - OWN0 + n], in_=ps[:, b, 0:n], func=AF.Silu),
                        [pst[b]], [t_ZT[zc]]))

        def epilogue(zc, q0, n, o_ap, den_ap, b_o, b_d):
            if den_ap is not None:
                S.op("dve", lambda: nc.vector.reciprocal(out=rden[:, 0:n], in_=den_ap), [pst[b_d]], [t_rden])
                S.op("dve", lambda: nc.vector.tensor_tensor(out=otmp[:, 0:n], in0=o_ap, in1=rden[:, 0:n], op=ALU.mult),
                     [pst[b_o], t_rden], [t_otmp])
            else:
                S.op("dve", lambda: nc.vector.tensor_copy(out=otmp[:, 0:n], in_=o_ap), [pst[b_o]], [t_otmp])
            S.op("dve", lambda: nc.vector.tensor_tensor(out=ZT[:, zc, q0:q0 + n], in0=ZT[:, zc, q0:q0 + n], in1=otmp[:, 0:n],
                                                        op=ALU.mult), [t_otmp, t_ZT[zc]], [t_ZT[zc]])

        S.op("pool", lambda: nc.gpsimd.memset(KTn[:], 0.0), [], [t_KTn])
        S.op("pool", lambda: nc.gpsimd.memset(Vn[:], 0.0), [], [t_Vn])

        q_seg(0, 768, 6)
        for (c0, w) in split512(768):
            sl = load_w(wsrc[:, :, 768 + c0:768 + c0 + w], w)
            for j in range(w // 128):
                head = c0 // 128 + j
                proj_fm(sl, j, [OWN_TILES[2]], lambda b, t0, n, head=head: evac(KTn[:, head, 0:16], ps[:, b, 0:16], [pst[b]], [t_KTn]))
            b = proj_tm(sl, w, SMP0, 16, 3)
            out_rows(o_sbks[:, c0:c0 + w], 16, [b], [w])
        for (c0, w) in split512(768):
            sl = load_w(wsrc[:, :, 1536 + c0:1536 + c0 + w], w)
            b = proj_tm(sl, w, SMP0, 16, 3)
            evac(Vn[0:16, c0:c0 + w], ps[0:16, b, 0:w], [pst[b]], [t_Vn])
            out_rows(o_sbvs[:, c0:c0 + w], 16, [b], [w])
        g_seg(2304, 768, 0)
        maybe_stop('p2')

        with ExitStack() as esS:
            es2 = esS
            KTh = sb("KTh", (128, 4096), BF16); t_KTh = Tk()
            Vh = sb("Vh", (128, 32, 128), BF16); t_Vh = Tk()
            e_t = [sb("e_t%d" % i, (128, 2, 512), BF16) for i in range(3)]; t_e = [Tk(), Tk(), Tk()]
            sp_t = [sb("sp_t%d" % i, (128, 2, 512), BF16) for i in range(2)]; t_sp = [Tk(), Tk()]
            tt_t = [sb("tt_t%d" % i, (128, 2, 512), BF16) for i in range(2)]; t_tt = [Tk(), Tk()]
            w_t = [sb("w_t%d" % i, (128, 2, 512), BF16) for i in range(2)]; t_w = [Tk(), Tk()]
            P_t = [sb("P_t%d" % i, (128, 2, 512), BF16) for i in range(2)]; t_P = [Tk(), Tk()]
            Kb = [sb("Kb%d" % i, (128, 768), BF16) for i in range(2)]; t_Kb = [Tk(), Tk()]
            Vb = [sb("Vb%d" % i, (128, 768), BF16) for i in range(4)]; t_Vb = [Tk() for _ in range(4)]
            KTb = [sb("KTb%d" % i, (128, 768), BF16) for i in range(2)]; t_KTb = [Tk(), Tk()]
            oacc = sb("oacc", (128, 96), F32); t_oacc = Tk()

            def sweep(n, zmm, e_op, sp_op, s_mm, p_op, t_op, w_op, pv, pre=None):
                if pre is not None:
                    pre(0); pre(1)
                zmm(0)
                for k in range(n):
                    e_op(k)
                    if k > 0:
                        t_op(k - 1); w_op(k - 1)
                    sp_op(k)
                    if pre is not None and k + 2 < n:
                        pre(k + 2)
                    if k < n - 1:
                        zmm(k + 1)
                    s_mm(k); p_op(k)
                    if k > 0:
                        pv(k - 1)
                t_op(n - 1); w_op(n - 1); pv(n - 1)

            def flat(t, c0, c1):
                return t[:].rearrange("p a b -> p (a b)")[:, c0:c1]

            for h in range(6):
                S.dma("sp", KTh[:], kT_scr[:, h, :], "kth", reads=[t_kscr], writes=[t_KTh])
                S.dma("sp", Vh[:], v_scr[:, :, h * 128:(h + 1) * 128], "vh", reads=[t_vscr], writes=[t_Vh])
                S.op("pool", lambda: nc.gpsimd.memset(P_t[0][:], 0.0), [], [t_P[0]])
                S.op("pool", lambda: nc.gpsimd.memset(P_t[1][:], 0.0), [], [t_P[1]])
                nB = lambda k: 2 if k >= 4 else 1

                def zmm(k, h=h):
                    r = 31 - k
                    zb = (k % 2) * 2
                    dA = r - 28
                    mm(ps[:, zb, :], KTh[:, r * 128:(r + 1) * 128], QT[:, h, 512:1024], True, dA < 0, [t_KTh, t_QT], [pst[zb]])
                    if dA >= 0:
                        mm(ps[:, zb, :], ident[:], maskB[:, dA, :], False, True, [t_const], [pst[zb]])
                    if k >= 4:
                        dB = r - 24
                        mm(ps[:, zb + 1, :], KTh[:, r * 128:(r + 1) * 128], QT[:, h, 0:512], True, dB < 0, [t_KTh, t_QT], [pst[zb + 1]])
                        if dB >= 0:
                            mm(ps[:, zb + 1, :], ident[:], maskB[:, dB, :], False, True, [t_const], [pst[zb + 1]])

                def e_op(k):
                    r = 31 - k; zb = (k % 2) * 2; s3 = k % 3; nb = nB(k)
                    S.op("act", lambda: nc.scalar.activation(out=e_t[s3][:, 0:nb, :], in_=ps[:, zb:zb + nb, :], func=AF.Exp, bias=bc[:, r:r + 1]),
                         [pst[zb + i] for i in range(nb)] + [t_const], [t_e[s3]])

                def sp_op(k):
                    s_ = k % 2; nb = nB(k)
                    S.op("act", lambda: nc.scalar.activation(out=sp_t[s_][:, 0:nb, :], in_=e_t[k % 3][:, 0:nb, :], func=AF.Ln, bias=1.0),
                         [t_e[k % 3]], [t_sp[s_]])

                def s_mm(k):
                    s_ = k % 2; pc = k % 2
                    for i in range(nB(k)):
                        mm(ps[:, 4 + i, :], ones[:], P_t[pc][:, i, :], True, False, [t_const, t_P[pc]], [pst[4 + i]])
                    for i in range(nB(k)):
                        mm(ps[:, 4 + i, :], tri[:], sp_t[s_][:, i, :], False, True, [t_const, t_sp[s_]], [pst[4 + i]])

                def p_op(k):
                    s_ = k % 2; pc = k % 2; pn = 1 - pc; nb = nB(k)
                    if k == 31:
                        return
                    S.op("pool", lambda: nc.gpsimd.tensor_tensor(out=flat(P_t[pn], 0, nb * 512), in0=flat(P_t[pc], 0, nb * 512),
                                                                 in1=flat(sp_t[s_], 0, nb * 512), op=ALU.add), [t_sp[s_], t_P[pc]], [t_P[pn]])

                def t_op(k):
                    s_ = k % 2; nb = nB(k)
                    S.op("act", lambda: nc.scalar.activation(out=tt_t[s_][:, 0:nb, :], in_=ps[:, 4:4 + nb, :], func=AF.Exp, scale=-1.0),
                         [pst[4 + i] for i in range(nb)], [t_tt[s_]])

                def w_op(k):
                    s_ = k % 2; nb = nB(k)
                    S.op("dve", lambda: nc.vector.tensor_tensor(out=flat(w_t[s_], 0, nb * 512), in0=flat(e_t[k % 3], 0, nb * 512),
                                                                in1=flat(tt_t[s_], 0, nb * 512), op=ALU.mult), [t_e[k % 3], t_tt[s_]], [t_w[s_]])

                def pv(k):
                    r = 31 - k; s_ = k % 2
                    mm(ps[:, 6, :], Vh[:, r, :], w_t[s_][:, 0, :], k == 0, k == 31, [t_Vh, t_w[s_]], [pst[6]])
                    if k >= 4:
                        mm(ps[:, 7, :], Vh[:, r, :], w_t[s_][:, 1, :], k == 4, k == 31, [t_Vh, t_w[s_]], [pst[7]])

                sweep(32, zmm, e_op, sp_op, s_mm, p_op, t_op, w_op, pv)
                epilogue(h, 512, 512, ps[:, 6, :], None, 6, None)
                epilogue(h, 0, 512, ps[:, 7, :], None, 7, None)
            maybe_stop('sbp')

            S.op("pool", lambda: nc.gpsimd.memset(P_t[0][:], 0.0), [t_P[0]], [t_P[0]])
            S.op("pool", lambda: nc.gpsimd.memset(P_t[1][:], 0.0), [t_P[1]], [t_P[1]])
            NS = 33

            t_KTb2 = [[Tk(), Tk()], [Tk(), Tk()]]

            def pre(k):
                if k == 0:
                    return
                s_ = k % 2
                r = 32 - k
                S.dma("pool", Kb[s_][:], csk[r * 128:(r + 1) * 128, :], "kb%d" % s_, writes=[t_Kb[s_]])
                S.dma("pool", Vb[k % 4][:], csv[r * 128:(r + 1) * 128, :], "vb%d" % (k % 4), writes=[t_Vb[k % 4]])
                tb0 = 2 * s_
                for h in range(6):
                    bb = tb0 + h // 4
                    mm(ps[:, bb, (h % 4) * 128:(h % 4 + 1) * 128], Kb[s_][:, h * 128:(h + 1) * 128], ident[:], True, True,
                       [t_Kb[s_], t_const], [pst[bb]])
                evac(KTb[s_][:, 0:512], ps[:, tb0, :], [pst[tb0]], [t_KTb2[s_][0]], eng="act")
                evac(KTb[s_][:, 512:768], ps[:, tb0 + 1, 0:256], [pst[tb0 + 1]], [t_KTb2[s_][1]], eng="dve")

            def zmm(k):
                s_ = k % 2
                zb = 4 + s_
                if k == 0:
                    for h in range(6):
                        mm(ps[:, zb, h * 16:(h + 1) * 16], KTn[:, h, :], QT[:, h, 1024:1040], True, False, [t_KTn, t_QT], [pst[zb]])
                        mm(ps[:, zb, h * 16:(h + 1) * 16], ident[:], maskS[:, h * 16:(h + 1) * 16], False, True, [t_const], [pst[zb]])
                    return
                for h in range(6):
                    mm(ps[:, zb, h * 16:(h + 1) * 16], KTb[s_][:, h * 128:(h + 1) * 128], QT[:, h, 1024:1040], True, True,
                       [t_KTb2[s_][h // 4], t_QT], [pst[zb]])

            def e_op(k):
                s_ = k % 2; zb = 4 + s_
                S.op("act", lambda: nc.scalar.activation(out=flat(e_t[k % 3], 0, 96), in_=ps[:, zb, 0:96], func=AF.Exp), [pst[zb]], [t_e[k % 3]])

            def sp_op(k):
                s_ = k % 2
                S.op("act", lambda: nc.scalar.activation(out=flat(sp_t[s_], 0, 96), in_=flat(e_t[k % 3], 0, 96), func=AF.Ln, bias=1.0),
                     [t_e[k % 3]], [t_sp[s_]])

            def s_mm(k):
                s_ = k % 2; pc = k % 2
                mm(ps[:, 6, 0:96], ones[:], flat(P_t[pc], 0, 96), True, False, [t_const, t_P[pc]], [pst[6]])
                mm(ps[:, 6, 0:96], tri[:], flat(sp_t[s_], 0, 96), False, True, [t_const, t_sp[s_]], [pst[6]])

            def p_op(k):
                s_ = k % 2; pc = k % 2; pn = 1 - pc
                if k == NS - 1:
                    return
                S.op("pool", lambda: nc.gpsimd.tensor_tensor(out=flat(P_t[pn], 0, 96), in0=flat(P_t[pc], 0, 96), in1=flat(sp_t[s_], 0, 96),
                                                             op=ALU.add), [t_sp[s_], t_P[pc]], [t_P[pn]])

            def t_op(k):
                s_ = k % 2
                S.op("act", lambda: nc.scalar.activation(out=flat(tt_t[s_], 0, 96), in_=ps[:, 6, 0:96], func=AF.Exp, scale=-1.0),
                     [pst[6]], [t_tt[s_]])

            def w_op(k):
                s_ = k % 2
                S.op("dve", lambda: nc.vector.tensor_tensor(out=flat(w_t[s_], 0, 96), in0=flat(e_t[k % 3], 0, 96), in1=flat(tt_t[s_], 0, 96),
                                                            op=ALU.mult), [t_e[k % 3], t_tt[s_]], [t_w[s_]])

            def pv(k):
                s_ = k % 2
                for h in range(6):
                    if k == 0:
                        v_ap = Vn[:, h * 128:(h + 1) * 128]; rv = [t_Vn]
                    else:
                        v_ap = Vb[k % 4][:, h * 128:(h + 1) * 128]; rv = [t_Vb[k % 4]]
                    S.op("pe", lambda: nc.tensor.matmul(ps[:, 7, h * 16:(h + 1) * 16], v_ap, flat(w_t[s_], h * 16, (h + 1) * 16),
                                                        start=True, stop=True), rv + [t_w[s_]], [pst[7]], signal=(h == 5))
                if k == 0:
                    S.op("dve", lambda: nc.vector.tensor_copy(out=oacc[:, :], in_=ps[:, 7, 0:96]), [pst[7]], [t_oacc])
                else:
                    S.op("dve", lambda: nc.vector.tensor_tensor(out=oacc[:, :], in0=oacc[:, :], in1=ps[:, 7, 0:96], op=ALU.add),
                         [pst[7], t_oacc], [t_oacc])

            sweep(NS, zmm, e_op, sp_op, s_mm, p_op, t_op, w_op, pv, pre=pre)
            for h in range(6):
                S.op("dve", lambda: nc.vector.tensor_tensor(out=ZT[:, h, 1024:1040], in0=ZT[:, h, 1024:1040], in1=oacc[:, h * 16:(h + 1) * 16],
                                                            op=ALU.mult), [t_oacc, t_ZT[h]], [t_ZT[h]])
            prefetch_w(3072, 384)
            S.barrier()
            maybe_stop('sb')
        es2 = esM

        with ExitStack() as esB:
            es2 = esB
            KTbd = sb("KTbd", (128, 6, NTOK), BF16); t_KTbd = Tk()
            Vbd = sb("Vbd", (128, 12, 768), BF16); t_Vbd = Tk()
            cK = sb("cK", (128, 4, 768), BF16); t_cK = Tk()
            cV = sb("cV", (128, 4, 768), BF16); t_cV = Tk()
            cKT = sb("cKT", (128, 4, 768), BF16); t_cKT = Tk()
            onesrow = sb("onesrow", (1, 64), BF16); t_or = Tk()
            S.op("pool", lambda: nc.gpsimd.memset(onesrow[:], 1.0), [], [t_or])
            S.op("pool", lambda: nc.gpsimd.memset(KTn[:], 0.0), [t_KTn], [t_KTn])
            S.op("pool", lambda: nc.gpsimd.memset(Vn[:], 0.0), [t_Vn], [t_Vn])
            S.dma("pool", cK[:], cbk.rearrange("(b p) c -> p b c", p=128), "ck", writes=[t_cK])
            S.dma("pool", cV[:], cbv.rearrange("(b p) c -> p b c", p=128), "cv", writes=[t_cV])
            q_seg(3072, 768, 6)
            for (c0, w) in split512(768):
                sl = load_w(wsrc[:, :, 3840 + c0:3840 + c0 + w], w)
                for j in range(w // 128):
                    head = c0 // 128 + j

                    def fn(b, t0, n, head=head):
                        evac(KTbd[:, head, t0:t0 + n], ps[:, b, 0:n], [pst[b]], [t_KTbd])
                        if t0 == SMP0:
                            evac(KTn[:, head, 0:16], ps[:, b, 0:16], [pst[b]], [t_KTn])
                    proj_fm(sl, j, ALL_TILES, fn)
                for blk in range(8, 12):
                    b = proj_tm(sl, w, blk * 128, 128, 2)
                    out_rows(o_bdk[(blk - 8) * 128:(blk - 7) * 128, c0:c0 + w], 128, [b], [w])
                b = proj_tm(sl, w, SMP0, 16, 3)
                out_rows(o_bdks[:, c0:c0 + w], 16, [b], [w])
            for (c0, w) in split512(768):
                sl = load_w(wsrc[:, :, 4608 + c0:4608 + c0 + w], w)
                for blk in range(12):
                    b = proj_tm(sl, w, blk * 128, 128, blk // 4)
                    evac(Vbd[:, blk, c0:c0 + w], ps[:, b, 0:w], [pst[b]], [t_Vbd])
                    if blk >= 8:
                        out_rows(o_bdv[(blk - 8) * 128:(blk - 7) * 128, c0:c0 + w], 128, [b], [w])
                b = proj_tm(sl, w, SMP0, 16, 3)
                evac(Vn[0:16, c0:c0 + w], ps[0:16, b, 0:w], [pst[b]], [t_Vn])
                out_rows(o_bdvs[:, c0:c0 + w], 16, [b], [w])
            g_seg(5376, 768, 6)

            ebT = bT; ebTs = bTs; t_eb = Tk()
            S.op("act", lambda: nc.scalar.activation(out=ebT[:].rearrange("p a b c d -> p (a b c d)"),
                                                     in_=bT[:].rearrange("p a b c d -> p (a b c d)"), func=AF.Exp), [t_const], [t_eb])
            S.op("act", lambda: nc.scalar.activation(out=ebTs[:].rearrange("p a b c -> p (a b c)"),
                                                     in_=bTs[:].rearrange("p a b c -> p (a b c)"), func=AF.Exp), [t_const], [t_eb])
            p3 = [sb("p3_%d" % i, (128, 320), BF16) for i in range(3)]; t_p3 = [Tk(), Tk(), Tk()]
            units = []

            def band_S(u):
                nq, blocks, q_ap, eb_ap, zc, q0 = units[u]
                zb = 2 + u % 3
                for i, (kT_ap, rk, v_ap, rv, neg_ap) in enumerate(blocks):
                    mm(ps[:, zb, i * nq:(i + 1) * nq], kT_ap, q_ap, True, neg_ap is None, rk + [t_QT], [pst[zb]])
                    if neg_ap is not None:
                        mm(ps[:, zb, i * nq:(i + 1) * nq], neg_ap, onesrow[0:1, 0:nq], False, True, [t_const, t_or], [pst[zb]])

            def band_P(u):
                nq, blocks, q_ap, eb_ap, zc, q0 = units[u]
                zb = 2 + u % 3; sl = u % 3; nb = len(blocks) * nq
                S.op("act", lambda: nc.scalar.activation(out=p3[sl][:, 0:nb], in_=ps[:, zb, 0:nb], func=AF.Exp), [pst[zb]], [t_p3[sl]])
                S.op("dve", lambda: nc.vector.tensor_tensor(out=p3[sl][:, 0:nb], in0=p3[sl][:, 0:nb], in1=eb_ap, op=ALU.mult),
                     [t_p3[sl], t_eb], [t_p3[sl]])

            def band_O(u):
                nq, blocks, q_ap, eb_ap, zc, q0 = units[u]
                ob = 5 + u % 3; sl = u % 3; nbk = len(blocks)
                for i, (kT_ap, rk, v_ap, rv, neg_ap) in enumerate(blocks):
                    S.op("pe", lambda: nc.tensor.matmul(ps[:, ob, 0:nq], v_ap, p3[sl][:, i * nq:(i + 1) * nq], start=(i == 0),
                                                        stop=(i == nbk - 1)), rv + [t_p3[sl]], [pst[ob]], signal=False)
                for i in range(nbk):
                    mm(ps[:, ob, 64:64 + nq], ones[:], p3[sl][:, i * nq:(i + 1) * nq], i == 0, i == nbk - 1, [t_const, t_p3[sl]], [pst[ob]])
                epilogue(zc, q0, nq, ps[:, ob, 0:nq], ps[:, ob, 64:64 + nq], ob, ob)

            for n in range(16):
                g0 = n // 2; par = n % 2
                for h in range(6):
                    blocks = []
                    for i in range(5):
                        g = g0 + i
                        blocks.append((KTbd[:, h, g * 128:(g + 1) * 128], [t_KTbd], Vbd[:, g, h * 128:(h + 1) * 128], [t_Vbd],
                                       negrow[0:1, g, :] if g < 4 else None))
                    units.append((64, blocks, QT[:, h, n * 64:(n + 1) * 64], ebT[:, par, h, :, :].rearrange("p a b -> p (a b)"), 6 + h, n * 64))
            for blk in range(4):
                for h in range(6):
                    bb = h // 4
                    mm(ps[:, bb, (h % 4) * 128:(h % 4 + 1) * 128], cK[:, blk, h * 128:(h + 1) * 128], ident[:], True, True,
                       [t_cK, t_const], [pst[bb]])
                evac(cKT[:, blk, 0:512], ps[:, 0, :], [pst[0]], [t_cKT])
                evac(cKT[:, blk, 512:768], ps[:, 1, 0:256], [pst[1]], [t_cKT])
            for h in range(6):
                blocks = []
                for i in range(4):
                    blocks.append((cKT[:, i, h * 128:(h + 1) * 128], [t_cKT], cV[:, i, h * 128:(h + 1) * 128], [t_cV], None))
                blocks.append((KTn[:, h, :], [t_KTn], Vn[:, h * 128:(h + 1) * 128], [t_Vn], None))
                units.append((16, blocks, QT[:, h, 1024:1040], ebTs[:, h, :, :].rearrange("p a b -> p (a b)"), 6 + h, 1024))
            NU = len(units)
            band_S(0); band_S(1); band_P(0)
            for u in range(NU):
                if u + 2 < NU:
                    band_S(u + 2)
                if u + 1 < NU:
                    band_P(u + 1)
                band_O(u)
            prefetch_w(6144, 384)
            S.barrier()
            maybe_stop('band')
        es2 = esM

        with ExitStack() as esQ:
            es2 = esQ
            pm = sb("pm", (128, 2, 512), BF16); t_pm = Tk()
            cmK = sb("cmK", (128, 2, 512), BF16); t_cmK = Tk()
            cmV = sb("cmV", (128, 2, 512), BF16); t_cmV = Tk()
            cmKT = sb("cmKT", (128, 4, 256), BF16); t_cmKT = Tk()
            S.dma("pool", cmK[:], cmk.rearrange("(b p) c -> p b c", p=128), "ck", writes=[t_cmK])
            S.dma("pool", cmV[:], cmv.rearrange("(b p) c -> p b c", p=128), "cv", writes=[t_cmV])
            xst[0] = sb("xstM0", (128, D), F32); xst[1] = sb("xstM1", (128, D), F32)
            xn[0] = sb("xnM0", (128, D), BF16); xn[1] = sb("xnM1", (128, D), BF16)
            gmb = sb("gmb", (128, D), F32)
            hmT = sb("hmT", (128, 16, 256), BF16); t_hmT = Tk()
            S.dma("sp", gmb[:], gmem_b, "gpb", writes=[t_gpb])
            for mb in range(2):
                prep_block(memx[mb * 128:(mb + 1) * 128, :], 128, gmb, mb)
            q_seg(6144, 512, 4)
            g_seg(6656, 512, 12)
            for mb in range(2):
                transpose_block(mb, 128, hmT[:, :, mb * 128:(mb + 1) * 128], t_hmT)
            wm = w_mkv.rearrange("(kc p) c -> p kc c", p=128)
            for (c0, w) in split512(512):
                sl = load_w(wm[:, :, c0:c0 + w], w)
                for j in range(w // 128):
                    head = c0 // 128 + j
                    b = bank(0, 4)
                    for kc in range(16):
                        mm(ps[:, b, 0:256], wbuf[sl][:, kc, j * 128:(j + 1) * 128], hmT[:, kc, :], kc == 0, kc == 15, [t_wb[sl], t_hmT], [pst[b]])
                    evac(mkT[:, head, :], ps[:, b, 0:256], [pst[b]], [t_mkT])
                for mb in range(2):
                    b = bank(4, 8)
                    for kc in range(16):
                        mm(ps[:, b, 0:w], hmT[:, kc, mb * 128:(mb + 1) * 128], wbuf[sl][:, kc, 0:w], kc == 0, kc == 15, [t_wb[sl], t_hmT], [pst[b]])
                    out_rows(o_mk[mb * 128:(mb + 1) * 128, c0:c0 + w], 128, [b], [w])
            for (c0, w) in split512(512):
                sl = load_w(wm[:, :, 512 + c0:512 + c0 + w], w)
                for mb in range(2):
                    b = bank(4, 8)
                    for kc in range(16):
                        mm(ps[:, b, 0:w], hmT[:, kc, mb * 128:(mb + 1) * 128], wbuf[sl][:, kc, 0:w], kc == 0, kc == 15, [t_wb[sl], t_hmT], [pst[b]])
                    evac(mvb[:, mb, c0:c0 + w], ps[:, b, 0:w], [pst[b]], [t_mvb])
                    out_rows(o_mv[mb * 128:(mb + 1) * 128, c0:c0 + w], 128, [b], [w])
            for mb in range(2):
                for h in range(4):
                    mm(ps[:, 0, h * 128:(h + 1) * 128], cmK[:, mb, h * 128:(h + 1) * 128], ident[:], True, True, [t_cmK, t_const], [pst[0]])
                for h in range(4):
                    evac(cmKT[:, h, mb * 128:(mb + 1) * 128], ps[:, 0, h * 128:(h + 1) * 128], [pst[0]], [t_cmKT])

            pm2 = [pm, sb("pmB", (128, 2, 512), BF16)]; t_pm2 = [t_pm, Tk()]
            mu = [0]

            def mem_unit(n, q_ap, kT_of, rk, v_of, rv, zc, q0):
                mu[0] += 1
                u = mu[0] % 2
                zb = 4 * u; pmx = pm2[u]; tpm = t_pm2[u]
                for mb in range(2):
                    mm(ps[:, zb + mb, 0:n], kT_of(mb), q_ap, True, True, rk + [t_QT], [pst[zb + mb]])
                for mb in range(2):
                    S.op("act", lambda: nc.scalar.activation(out=pmx[:, mb, 0:n], in_=ps[:, zb + mb, 0:n], func=AF.Exp), [pst[zb + mb]], [tpm])
                for mb in range(2):
                    mm(ps[:, zb + 2, 0:n], v_of(mb), pmx[:, mb, 0:n], mb == 0, mb == 1, rv + [tpm], [pst[zb + 2]])
                for mb in range(2):
                    mm(ps[:, zb + 3, 0:n], ones[:], pmx[:, mb, 0:n], mb == 0, mb == 1, [t_const, tpm], [pst[zb + 3]])
                epilogue(zc, q0, n, ps[:, zb + 2, 0:n], ps[:, zb + 3, 0:n], zb + 2, zb + 3)

            for h in range(4):
                for qt in range(2):
                    mem_unit(512, QT[:, h, qt * 512:(qt + 1) * 512], lambda mb, h=h: mkT[:, h, mb * 128:(mb + 1) * 128], [t_mkT],
                             lambda mb, h=h: mvb[:, mb, h * 128:(h + 1) * 128], [t_mvb], 12 + h, qt * 512)
                mem_unit(16, QT[:, h, 1024:1040], lambda mb, h=h: cmKT[:, h, mb * 128:(mb + 1) * 128], [t_cmKT],
                         lambda mb, h=h: cmV[:, mb, h * 128:(h + 1) * 128], [t_cmV], 12 + h, 1024)
            S.barrier()
            maybe_stop('mem')
        es2 = esM

        with ExitStack() as esG:
            es2 = esG
            acc = sb("acc", (128, 2, NQ), F32); t_acc = Tk()
            mst = sb("mst", (128, 2, NQ), BF16); t_mst = Tk()
            sg = [sb("sg%d" % i, (128, 512), BF16) for i in range(2)]; t_sg = [Tk(), Tk()]
            tmp = [sb("tmp%d" % i, (128, 512), F32) for i in range(2)]; t_tmp = [Tk(), Tk()]
            wup = [sb("wup%d" % i, (128, 6, 256), BF16) for i in range(2)]; t_wup = [Tk(), Tk()]
            zoffs = [0, 6, 12]; nkcs = [6, 6, 4]
            gi = 0
            groups = [(cq, br) for cq in range(8) for br in range(3)]

            def merge_load(g):
                cq, br = groups[g]
                col = 7168 + br * 2048 + cq * 256
                sl_ = load_w(wsrc[:, :, col:col + 256], 256)
                us_ = g % 2
                S.dma("pool", wup[us_][:, 0:nkcs[br], :], w_up[br].rearrange("(kc p) c -> p kc c", p=128)[:, :, cq * 256:(cq + 1) * 256],
                      "wup%d" % us_, writes=[t_wup[us_]])
                return sl_, us_

            nxt = merge_load(0)
            for g, (cq, br) in enumerate(groups):
                    sl, us = nxt
                    if g + 1 < len(groups):
                        nxt = merge_load(g + 1)
                    for j in range(2):
                        for (t0, n, ht) in OWN_TILES:
                            gi += 1
                            s2 = gi % 2
                            q0 = t0 - OWN0
                            b1 = bank(0, 4)
                            for kc in range(16):
                                mm(ps[:, b1, 0:n], wbuf[sl][:, kc, j * 128:(j + 1) * 128], hT[:, kc, t0:t0 + n], kc == 0, kc == 15,
                                   [t_wb[sl], t_hT[ht]], [pst[b1]])
                            S.op("act", lambda: nc.scalar.activation(out=sg[s2][:, 0:n], in_=ps[:, b1, 0:n], func=AF.Sigmoid),
                                 [pst[b1]], [t_sg[s2]])
                            b2 = bank(4, 8)
                            nk = nkcs[br]
                            for kc in range(nk):
                                mm(ps[:, b2, 0:n], wup[us][:, kc, j * 128:(j + 1) * 128], ZT[:, zoffs[br] + kc, q0:q0 + n], kc == 0, kc == nk - 1,
                                   [t_wup[us]] + [t_ZT[zoffs[br] + kc]], [pst[b2]])
                            if br == 0:
                                S.op("dve", lambda: nc.vector.tensor_tensor(out=acc[:, j, q0:q0 + n], in0=ps[:, b2, 0:n], in1=sg[s2][:, 0:n],
                                                                            op=ALU.mult), [pst[b2], t_sg[s2]], [t_acc])
                            else:
                                S.op("dve", lambda: nc.vector.tensor_tensor(out=tmp[s2][:, 0:n], in0=ps[:, b2, 0:n], in1=sg[s2][:, 0:n],
                                                                            op=ALU.mult), [pst[b2], t_sg[s2]], [t_tmp[s2]])
                                if br == 1:
                                    S.op("dve", lambda: nc.vector.tensor_tensor(out=acc[:, j, q0:q0 + n], in0=acc[:, j, q0:q0 + n],
                                                                                in1=tmp[s2][:, 0:n], op=ALU.add), [t_tmp[s2], t_acc], [t_acc])
                                else:
                                    S.op("dve", lambda: nc.vector.tensor_tensor(out=mst[:, j, q0:q0 + n], in0=acc[:, j, q0:q0 + n],
                                                                                in1=tmp[s2][:, 0:n], op=ALU.add), [t_tmp[s2], t_acc], [t_mst])
                    if br == 2:
                        S.dma("sp", mg_scr[:, cq * 2:(cq + 1) * 2, :], mst[:], "mst", reads=[t_mst], writes=[t_mgscr])
            S.barrier()
            maybe_stop('merge')
        es2 = esM
    esH.close()
    es2 = es

    with ExitStack() as esF:
        es2 = esF
        mergedT = sb("mergedT", (128, 16, NQ), BF16); t_mg = Tk()
        wo2 = [sb("wo2_%d" % i, (128, 16, 512), BF16) for i in range(2)]; t_wo2 = [Tk(), Tk()]
        gp = sb("gp", (128, D), F32); t_gp = Tk()
        xst[0] = sb("xstF0", (128, D), F32); xst[1] = sb("xstF1", (128, D), F32)
        junk[0] = sb("junkF", (128, 512), BF16)
        ysb = [sb("ysb%d" % i, (128, D), F32) for i in range(2)]; t_ysb = [Tk(), Tk()]
        ypre = sb("ypre", (128, 9, D), F32); t_ypre = [Tk() for _ in range(9)]
        ssq = sb("ssq", (128, 36), F32); t_ssq = [Tk() for _ in range(9)]
        wo = w_out.rearrange("(kc p) c -> p kc c", p=128)
        S.dma("pool", wo2[0][:], wo[:, :, 0:512], "wo0", writes=[t_wo2[0]])
        S.dma("sp", mergedT[:], mg_scr, "mgld", reads=[t_mgscr], writes=[t_mg])
        S.dma("sp", gp[:], gpost, "gpF", writes=[t_gp])
        for cg in range(4):
            sl = cg % 2
            if cg + 1 < 4:
                S.dma("pool", wo2[1 - sl][:], wo[:, :, (cg + 1) * 512:(cg + 2) * 512], "wo%d" % (1 - sl), writes=[t_wo2[1 - sl]])
            for tb in range(9):
                nr = 128 if tb < 8 else 16
                q0 = tb * 128
                b = bank(0, 8)
                for kc in range(16):
                    mm(ps[0:nr, b, :], mergedT[:, kc, q0:q0 + nr], wo2[sl][:, kc, :], kc == 0, kc == 15, [t_mg, t_wo2[sl]], [pst[b]])
                S.op("act", lambda: nc.scalar.activation(out=junk[0][0:nr, :], in_=ps[0:nr, b, :], func=AF.Square,
                                                         accum_out=ssq[0:nr, tb * 4 + cg:tb * 4 + cg + 1]), [pst[b]], [t_junk, t_ssq[tb]])
                S.op("dve", lambda: nc.vector.tensor_copy(out=ypre[0:nr, tb, cg * 512:(cg + 1) * 512], in_=ps[0:nr, b, :]),
                     [pst[b]], [t_ypre[tb]])
                if cg == 3:
                    dsto = o_y[tb * 128:(tb + 1) * 128, :] if tb < 8 else o_ys
                    s2 = tb % 2

                    def ld_x(t_):
                        n_ = 128 if t_ < 8 else 16
                        src_ = xctx[3072 + t_ * 128:3072 + (t_ + 1) * 128, :] if t_ < 8 else xs
                        S.dma("pool", xst[t_ % 2][0:n_, :], src_, "xstF%d" % (t_ % 2), writes=[t_xst[t_ % 2]])
                    if tb == 0:
                        ld_x(0)
                    if tb + 1 < 9:
                        ld_x(tb + 1)
                    S.op("dve", lambda: nc.vector.tensor_reduce(out=stat[0:nr, 0:1], in_=ssq[0:nr, tb * 4:tb * 4 + 4], axis=AX.X, op=ALU.add),
                         [t_ssq[tb]], [t_stat])
                    S.op("act", lambda: nc.scalar.activation(out=stat[0:nr, 1:2], in_=stat[0:nr, 0:1], func=AF.Ln, scale=1.0 / D, bias=EPS),
                         [t_stat], [t_stat])
                    S.op("act", lambda: nc.scalar.activation(out=stat[0:nr, 2:3], in_=stat[0:nr, 1:2], func=AF.Exp, scale=-0.5), [t_stat], [t_stat])
                    S.op("dve", lambda: nc.vector.scalar_tensor_tensor(out=ysb[s2][0:nr, :], in0=ypre[0:nr, tb, :], scalar=stat[0:nr, 2:3],
                                                                       in1=gp[0:nr, :], op0=ALU.mult, op1=ALU.mult),
                         [t_ypre[tb], t_stat, t_gp], [t_ysb[s2]])
                    S.op("dve", lambda: nc.vector.tensor_tensor(out=ysb[s2][0:nr, :], in0=ysb[s2][0:nr, :], in1=xst[s2][0:nr, :], op=ALU.add),
                         [t_ysb[s2], t_xst[s2]], [t_ysb[s2]])
                    S.dma("sp", dsto, ysb[s2][0:nr, :], "yst%d" % s2, reads=[t_ysb[s2]])
        S.finish("sp")
        S.barrier()
    es.close()
    return nc


def _host_consts(rel_bias, j):
    k = np.arange(128)
    ident = np.eye(128, dtype=np.float32)
    tri = (k[:, None] >= k[None, :]).astype(np.float32)
    ones = np.ones((128, 128), np.float32)
    maskB = np.zeros((128, 4, 4, 128), np.float32)
    for i in range(4):
        for s in range(4):
            if s < i:
                maskB[:, i, s, :] = NEG
            elif s == i:
                maskB[:, i, s, :] = np.where(k[:, None] < k[None, :], 0.0, NEG)
    maskB = maskB.reshape(128, 4 * 512)
    q16 = np.arange(16)
    mS = np.where((k[:, None] < 16) & (k[:, None] < q16[None, :]), 0.0, NEG).astype(np.float32)
    maskS = np.tile(mS, (1, 6))
    bc = np.zeros((128, 32), np.float32)
    bc[:, :24 - 8 * j] = NEG
    rb = rel_bias
    bT = np.zeros((128, 2, 6, 5, 64), np.float32)
    ql = np.arange(64)
    for par in range(2):
        for i in range(5):
            kl = i * 128 + k - 64 * par
            valid = (kl >= 0) & (kl < 576)
            idx = np.clip(512 + ql[None, :] - kl[:, None], -256, 256) + 256
            for h in range(6):
                bT[:, par, h, i, :] = np.where(valid[:, None], rb[h][idx], NEG)
    bTs = np.zeros((128, 6, 5, 16), np.float32)
    for i in range(5):
        kl = i * 128 + k
        valid = kl < 528
        idx = np.clip(512 + q16[None, :] - kl[:, None], -256, 256) + 256
        for h in range(6):
            bTs[:, h, i, :] = np.where(valid[:, None], rb[h][idx], NEG)
    negrow = np.zeros((1, 12, 128), np.float32)
    if j == 0:
        negrow[:, 0:4, :] = NEG
    return dict(c_ident=ident, c_tri=tri, c_ones=ones, c_maskB=maskB, c_maskS=maskS.astype(np.float32), c_bc=bc,
                c_bT=bT.reshape(128, -1), c_bTs=bTs.reshape(128, -1), c_negrow=negrow.reshape(1, -1))


_PROG = [None]


def kernel(x_prompt, x_sample, cache_sb_k, cache_sb_v, cache_band_k, cache_band_v, cache_mem_k, cache_mem_v,
           mem_prompt, g_pre, w_in, rel_bias, g_mem, w_mem_kv, w_up_sb, w_up_band, w_up_mem, w_out, g_post):
    f = lambda a: np.ascontiguousarray(np.asarray(a, dtype=np.float32))
    x_prompt = f(x_prompt); x_sample = f(x_sample); mem_prompt = f(mem_prompt)
    if _PROG[0] is None:
        _PROG[0] = build_program()
    nc = _PROG[0]
    shared = dict(
        w_in=f(w_in[0]), w_mkv=f(w_mem_kv[0]), w_up_sb=f(w_up_sb[0]), w_up_bd=f(w_up_band[0]), w_up_mm=f(w_up_mem[0]),
        w_out=f(w_out[0]),
        gpreT=f(np.asarray(g_pre[0]).reshape(16, 128).T), gmemT=f(np.asarray(g_mem[0]).reshape(16, 128).T),
        gpost=f(np.broadcast_to(np.asarray(g_post[0])[None, :], (128, D))),
        gpre_b=f(np.broadcast_to(np.asarray(g_pre[0])[None, :], (128, D))),
        gmem_b=f(np.broadcast_to(np.asarray(g_mem[0])[None, :], (128, D))),
    )
    rb = f(rel_bias[0])
    consts = [_host_consts(rb, j) for j in range(4)]
    in_maps = []
    for c in range(8):
        b, j = c // 4, c % 4
        xc = np.zeros((4096, D), np.float32)
        lo = 1024 * j - 3072
        src_lo = max(lo, 0)
        xc[src_lo - lo:, :] = x_prompt[b, src_lo:1024 * (j + 1), :]
        m = dict(shared)
        m.update(consts[j])
        m.update(
            xctx=xc, xs=f(x_sample[c]), memx=f(mem_prompt[b]),
            csk=f(np.asarray(cache_sb_k[0, c]).reshape(4096, 768)), csv=f(np.asarray(cache_sb_v[0, c]).reshape(4096, 768)),
            cbk=f(np.asarray(cache_band_k[0, c]).reshape(512, 768)), cbv=f(np.asarray(cache_band_v[0, c]).reshape(512, 768)),
            cmk=f(np.asarray(cache_mem_k[0, c]).reshape(256, 512)), cmv=f(np.asarray(cache_mem_v[0, c]).reshape(256, 512)),
        )
        in_maps.append(m)
    res = run_bass_kernel_spmd(nc, in_maps, core_ids=list(range(8)))
    R = res.results
    y_p = np.zeros((2, 4096, D), np.float32)
    y_s = np.zeros((8, 16, D), np.float32)
    sbk_p = np.zeros((1, 2, 4096, 6, 128), np.float32); sbv_p = np.zeros_like(sbk_p)
    bdk_p = np.zeros((1, 2, 512, 6, 128), np.float32); bdv_p = np.zeros_like(bdk_p)
    mk_p = np.zeros((1, 2, 256, 4, 128), np.float32); mv_p = np.zeros_like(mk_p)
    sbk_s = np.zeros((1, 8, 16, 6, 128), np.float32); sbv_s = np.zeros_like(sbk_s)
    bdk_s = np.zeros_like(sbk_s); bdv_s = np.zeros_like(sbk_s)
    for c in range(8):
        b, j = c // 4, c % 4
        r = R[c]
        y_p[b, 1024 * j:1024 * (j + 1)] = r["o_y"]
        y_s[c] = r["o_ys"]
        sbk_p[0, b, 1024 * j:1024 * (j + 1)] = r["o_sbk"].reshape(1024, 6, 128)
        sbv_p[0, b, 1024 * j:1024 * (j + 1)] = r["o_sbv"].reshape(1024, 6, 128)
        if j == 3:
            bdk_p[0, b] = r["o_bdk"].reshape(512, 6, 128)
            bdv_p[0, b] = r["o_bdv"].reshape(512, 6, 128)
        if j == 0:
            mk_p[0, b] = r["o_mk"].reshape(256, 4, 128)
            mv_p[0, b] = r["o_mv"].reshape(256, 4, 128)
        sbk_s[0, c] = r["o_sbks"].reshape(16, 6, 128); sbv_s[0, c] = r["o_sbvs"].reshape(16, 6, 128)
        bdk_s[0, c] = r["o_bdks"].reshape(16, 6, 128); bdv_s[0, c] = r["o_bdvs"].reshape(16, 6, 128)
    return (y_p, y_s, sbk_p, sbv_p, bdk_p, bdv_p, mk_p, mv_p, sbk_s, sbv_s, bdk_s, bdv_s)
```

```python
import numpy as np
from contextlib import ExitStack
import concourse.bass as bass
import concourse.mybir as mybir
from concourse.bass_utils import run_bass_kernel_spmd

F32 = mybir.dt.float32
BF16 = mybir.dt.bfloat16
AF = mybir.ActivationFunctionType
ALU = mybir.AluOpType
AX = mybir.AxisListType

D = 2048
NEG = -30000.0
QS = 128 ** -0.5
IN_W = 13312
OWN0 = 512
SMP0 = 1536
NTOK = 1552
NQ = 1040
EPS = 1e-6


class Tk:
    __slots__ = ("w", "r", "x")

    def __init__(self, x=False):
        self.w = {}
        self.r = {}
        self.x = x


class Sched:
    def __init__(self, nc, es):
        self.nc = nc
        self.es = es
        self.E = dict(pe=nc.tensor, act=nc.scalar, dve=nc.vector, pool=nc.gpsimd, sp=nc.sync)
        self.sem = {}
        self.cnt = {}
        self.waited = {e: {} for e in self.E}
        for e in self.E:
            self._sem(e)

    def _sem(self, key):
        if key not in self.sem:
            self.sem[key] = self.es.enter_context(self.nc.semaphore("sem_" + key))
            self.cnt[key] = 0
        return self.sem[key]

    def _wait(self, e, key, val):
        if key == "pe" and e == "pe":
            return
        if self.waited[e].get(key, 0) >= val:
            return
        self.E[e].wait_ge(self.sem[key], val)
        self.waited[e][key] = val

    def _deps(self, e, reads, writes):
        for t in reads:
            for k, v in t.w.items():
                self._wait(e, k, v)
            if t.x:
                for k, v in t.r.items():
                    if k != e:
                        self._wait(e, k, v)
        for t in writes:
            for k, v in t.w.items():
                self._wait(e, k, v)
            for k, v in t.r.items():
                self._wait(e, k, v)

    def op(self, e, fn, reads=(), writes=(), signal=True):
        self._deps(e, reads, writes)
        ins = fn()
        if signal:
            self.cnt[e] += 1
            ins.then_inc(self.sem[e], 1)
            v = self.cnt[e]
        else:
            v = self.cnt[e] + 1
        for t in reads:
            t.r[e] = v
        for t in writes:
            t.w = {e: v}
            t.r = {}
        return ins

    def dma(self, q, out, in_, key, reads=(), writes=()):
        self._sem(key)
        self._deps(q, reads, writes)
        ins = self.E[q].dma_start(out=out, in_=in_)
        self.cnt[key] += 16
        ins.then_inc(self.sem[key], 16)
        v = self.cnt[key]
        for t in reads:
            t.r[key] = v
        for t in writes:
            t.w = {key: v}
            t.r = {}

    def barrier(self):
        keys = [k for k in self.sem if self.cnt[k] > 0]
        for e in self.E:
            for k in keys:
                if k != e:
                    self._wait(e, k, self.cnt[k])

    def finish(self, e="sp"):
        for k in self.sem:
            if self.cnt[k] > 0 and k != e:
                self._wait(e, k, self.cnt[k])


def split512(width, step=384):
    out = []
    c = 0
    while c < width:
        w = min(step, width - c)
        out.append((c, w))
        c += w
    return out


class _Stop(Exception):
    pass


def build_program(stop=None):
    st = {}
    try:
        return _build(stop, st)
    except _Stop:
        return st["nc"]


def _build(stop, st):
    nc = bass.Bass("TRN2", target_bir_lowering=False)
    es = ExitStack()
    st["nc"] = nc; st["es"] = es

    def maybe_stop(tag):
        if stop == tag:
            S.finish("sp")
            S.barrier()
            raise _Stop()

    def din(name, shape):
        return nc.dram_tensor(name, list(shape), F32, kind="ExternalInput").ap()

    def dout(name, shape):
        return nc.dram_tensor(name, list(shape), F32, kind="ExternalOutput").ap()

    xctx = din("xctx", (4096, D))
    xs = din("xs", (16, D))
    memx = din("memx", (256, D))
    csk = din("csk", (4096, 768)); csv = din("csv", (4096, 768))
    cbk = din("cbk", (512, 768)); cbv = din("cbv", (512, 768))
    cmk = din("cmk", (256, 512)); cmv = din("cmv", (256, 512))
    w_in = din("w_in", (D, IN_W))
    w_mkv = din("w_mkv", (D, 1024))
    w_up = [din("w_up_sb", (768, D)), din("w_up_bd", (768, D)), din("w_up_mm", (512, D))]
    w_out = din("w_out", (D, D))
    gpreT = din("gpreT", (128, 16)); gmemT = din("gmemT", (128, 16))
    gpost = din("gpost", (128, D))
    gpre_b = din("gpre_b", (128, D)); gmem_b = din("gmem_b", (128, D))
    c_ident = din("c_ident", (128, 128)); c_tri = din("c_tri", (128, 128)); c_ones = din("c_ones", (128, 128))
    c_maskB = din("c_maskB", (128, 4 * 512))
    c_maskS = din("c_maskS", (128, 96))
    c_bc = din("c_bc", (128, 32))
    c_bT = din("c_bT", (128, 2 * 6 * 5 * 64))
    c_bTs = din("c_bTs", (128, 6 * 5 * 16))
    c_negrow = din("c_negrow", (1, 12 * 128))

    o_y = dout("o_y", (1024, D)); o_ys = dout("o_ys", (16, D))
    o_sbk = dout("o_sbk", (1024, 768)); o_sbv = dout("o_sbv", (1024, 768))
    o_bdk = dout("o_bdk", (512, 768)); o_bdv = dout("o_bdv", (512, 768))
    o_mk = dout("o_mk", (256, 512)); o_mv = dout("o_mv", (256, 512))
    o_sbks = dout("o_sbks", (16, 768)); o_sbvs = dout("o_sbvs", (16, 768))
    o_bdks = dout("o_bdks", (16, 768)); o_bdvs = dout("o_bdvs", (16, 768))

    kT_scr = nc.dram_tensor("kT_scr", [128, 6, 4096], BF16, kind="Internal").ap()
    v_scr = nc.dram_tensor("v_scr", [128, 32, 768], BF16, kind="Internal").ap()
    t_kscr = Tk(); t_vscr = Tk()

    S = Sched(nc, es)

    def sb(name, shape, dt):
        return es2.enter_context(nc.sbuf_tensor(name, list(shape), dt))

    es2 = es
    ps = es.enter_context(nc.psum_tensor("ps", [128, 8, 512], F32))
    pst = [Tk(True) for _ in range(8)]
    ident = sb("ident", (128, 128), BF16); tri = sb("tri", (128, 128), BF16); ones = sb("ones", (128, 128), BF16)
    maskB = sb("maskB", (128, 4, 512), BF16); maskS = sb("maskS", (128, 96), BF16)
    bc = sb("bc", (128, 32), F32)
    bT = sb("bT", (128, 2, 6, 5, 64), BF16); bTs = sb("bTs", (128, 6, 5, 16), BF16)
    negrow = sb("negrow", (1, 12, 128), BF16)
    gpre = sb("gpre", (128, 16), F32); gmem = sb("gmem", (128, 16), F32)
    t_const = Tk()
    stat = sb("stat", (128, 16), F32); t_stat = Tk()
    mkT = sb("mkT", (128, 4, 256), BF16); t_mkT = Tk()
    mvb = sb("mvb", (128, 2, 512), BF16); t_mvb = Tk()
    mg_scr = nc.dram_tensor("mg_scr", [128, 16, NQ], BF16, kind="Internal").ap()
    t_mgscr = Tk()
    esH = ExitStack()
    st["esH"] = esH
    es2 = esH
    hT = sb("hT", (128, 16, NTOK), BF16)
    t_hT = [Tk() for _ in range(4)]
    es2 = es
    t_xst = [Tk(), Tk()]; t_junk = Tk()
    xst = [None, None]; junk = [None]

    cast_i = [0]

    def cast_load(dst, src, writes, key=None):
        cast_i[0] += 1
        S.dma("pool", dst, src, key or ("cst%d" % cast_i[0]), writes=writes)

    def emit_const_loads():
        _ct = [Tk() for _ in range(12)]
        cast_load(ident[:], c_ident, [_ct[0]]); cast_load(tri[:], c_tri, [_ct[1]]); cast_load(ones[:], c_ones, [_ct[2]])
        cast_load(maskB[:].rearrange("p a b -> p (a b)"), c_maskB, [_ct[3]]); cast_load(maskS[:], c_maskS, [_ct[4]])
        cast_load(bT[:].rearrange("p a b c d -> p (a b c d)"), c_bT, [_ct[5]])
        cast_load(bTs[:].rearrange("p a b c -> p (a b c)"), c_bTs, [_ct[6]])
        cast_load(negrow[:].rearrange("p a b -> p (a b)"), c_negrow, [_ct[7]])
        S.dma("sp", bc[:], c_bc, "cf0", writes=[_ct[8]])
        S.dma("sp", gpre[:], gpreT, "cf1", writes=[_ct[9]])
        S.dma("sp", gmem[:], gmemT, "cf2", writes=[_ct[10]])
        for _t in _ct:
            for _k, _v in _t.w.items():
                t_const.w[_k] = max(t_const.w.get(_k, 0), _v)


    evac_i = [0]

    def evac(out, in_, reads, writes, scale=None, eng=None):
        if eng is None:
            evac_i[0] += 1
            eng = "act" if evac_i[0] % 2 else "dve"
        if eng == "act":
            if scale is None:
                S.op("act", lambda: nc.scalar.activation(out=out, in_=in_, func=AF.Copy), reads, writes)
            else:
                S.op("act", lambda: nc.scalar.activation(out=out, in_=in_, func=AF.Copy, scale=scale), reads, writes)
        else:
            if scale is None:
                S.op("dve", lambda: nc.vector.tensor_copy(out=out, in_=in_), reads, writes)
            else:
                S.op("dve", lambda: nc.vector.tensor_scalar(out=out, in0=in_, scalar1=scale, scalar2=None, op0=ALU.mult), reads, writes)

    bank_i = [0]

    def bank(lo=0, hi=8):
        bank_i[0] += 1
        return lo + bank_i[0] % (hi - lo)

    def mm(out, lhsT, rhs, start, stop, reads, writes):
        S.op("pe", lambda: nc.tensor.matmul(out, lhsT, rhs, start=start, stop=stop), reads, writes, signal=stop)

    xi = [0]
    xn = [None] * 4
    t_xn = [Tk() for _ in range(4)]
    t_stat4 = [Tk() for _ in range(4)]
    gb_ref = [None]
    t_gpb = Tk()

    def prep_block(src_rows, nrows, gb, slot, q="pool"):
        xi[0] += 1
        sl = xi[0] % 2
        c = slot * 4
        S.dma(q, xst[sl][0:nrows, :], src_rows, "xst%d" % sl, writes=[t_xst[sl]])
        S.op("act", lambda: nc.scalar.activation(out=xn[slot][0:nrows, :], in_=xst[sl][0:nrows, :], func=AF.Square,
                                                 accum_out=stat[0:nrows, c:c + 1]), [t_xst[sl]], [t_xn[slot], t_stat4[slot]])
        S.op("act", lambda: nc.scalar.activation(out=stat[0:nrows, c + 1:c + 2], in_=stat[0:nrows, c:c + 1], func=AF.Ln,
                                                 scale=1.0 / D, bias=EPS), [t_stat4[slot]], [t_stat4[slot]])
        S.op("act", lambda: nc.scalar.activation(out=stat[0:nrows, c + 2:c + 3], in_=stat[0:nrows, c + 1:c + 2], func=AF.Exp,
                                                 scale=-0.5), [t_stat4[slot]], [t_stat4[slot]])
        S.op("dve", lambda: nc.vector.scalar_tensor_tensor(out=xn[slot][0:nrows, :], in0=xst[sl][0:nrows, :],
                                                           scalar=stat[0:nrows, c + 2:c + 3], in1=gb[0:nrows, :],
                                                           op0=ALU.mult, op1=ALU.mult), [t_xst[sl], t_stat4[slot], t_gpb], [t_xn[slot]])

    def transpose_block(slot, nrows, dst3, t_dst):
        for j in range(4):
            b = bank(0, 3)
            for i in range(4):
                kc = 4 * j + i
                mm(ps[:, b, i * nrows:(i + 1) * nrows], xn[slot][0:nrows, kc * 128:(kc + 1) * 128], ident[0:nrows, 0:nrows],
                   True, True, [t_xn[slot], t_const], [pst[b]])
            evac(dst3[:, 4 * j:4 * j + 4, :], ps[:, b, 0:4 * nrows].rearrange("p (a b) -> p a b", a=4), [pst[b]], [t_dst])

    wsrc = w_in.rearrange("(kc p) c -> p kc c", p=128)
    with ExitStack() as es1:
        es2 = es1
        xst[0] = sb("xst0", (128, D), F32); xst[1] = sb("xst1", (128, D), F32)
        for i in range(4):
            xn[i] = sb("xn%d" % i, (128, D), BF16)
        gpb = sb("gpb", (128, D), F32)
        S.dma("sp", gpb[:], gpre_b, "gpb", writes=[t_gpb])
        wk = sb("wk", (128, 16, 768), BF16); wv = sb("wv", (128, 16, 768), BF16); t_wkv = Tk()
        t_wv2 = [Tk(), Tk()]
        chunks_v = split512(768)
        t_wchain = Tk()
        cast_load(wv[:, :, 0:chunks_v[0][1]], wsrc[:, :, 1536:1536 + chunks_v[0][1]], [t_wv2[0], t_wchain], key="wvld0")
        prep_block(xctx[0:128, :], 128, gpb, 0, q="sp")
        prep_block(xctx[128:256, :], 128, gpb, 1, q="sp")
        emit_const_loads()
        for ci, (c0, w) in enumerate(chunks_v):
            if ci > 0:
                cast_load(wv[:, :, c0:c0 + w], wsrc[:, :, 1536 + c0:1536 + c0 + w], [t_wv2[ci], t_wchain], key="wvld%d" % ci)
        for (c0, w) in split512(768):
            cast_load(wk[:, :, c0:c0 + w], wsrc[:, :, 768 + c0:768 + c0 + w], [t_wkv, t_wchain], key="wkld")
        es2 = es1
        hTt = [sb("hTt%d" % i, (128, 16, 512), BF16) for i in range(2)]
        kT_st = sb("kT_st", (128, 6, 512), BF16); t_kst = Tk()
        v_st = [sb("v_st%d" % i, (128, 768), BF16) for i in range(2)]; t_vst = [Tk(), Tk()]
        ko_st = sb("ko_st", (128, 768), F32); t_kost = Tk()
        vo_st = sb("vo_st", (128, 768), F32); t_vost = Tk()

        t_blk = [[Tk() for _ in range(4)] for _ in range(2)]

        def tile_ap(ti):
            return hT[:, :, (ti - 5) * 512:(ti - 4) * 512] if ti >= 5 else hTt[ti % 2][:, :, :]

        def do_T(n):
            ti, blk = divmod(n, 4)
            transpose_block(n % 4, 128, tile_ap(ti)[:, :, blk * 128:(blk + 1) * 128], t_blk[ti % 2][blk])

        def do_prep(n):
            prep_block(xctx[n * 128:(n + 1) * 128, :], 128, gpb, n % 4, q=("sp" if n < 4 else "pool"))

        do_prep(2)
        do_T(0)
        for n in range(32):
            ti, blk = divmod(n, 4)
            dst = tile_ap(ti)
            if n + 1 < 32:
                do_T(n + 1)
            if n + 3 < 32:
                do_prep(n + 3)
            lt = lambda kc, blk=blk, dst=dst: dst[:, kc, blk * 128:(blk + 1) * 128]
            rd = [t_wkv, t_blk[ti % 2][blk]]
            vs = n % 2
            b1 = bank(5, 8); b2 = bank(5, 8)
            for kc in range(16):
                mm(ps[:, b1, 0:384], lt(kc), wv[:, kc, 0:384], kc == 0, kc == 15, [t_wv2[0], t_blk[ti % 2][blk]], [pst[b1]])
            for kc in range(16):
                mm(ps[:, b2, 0:384], lt(kc), wv[:, kc, 384:768], kc == 0, kc == 15, [t_wv2[1], t_blk[ti % 2][blk]], [pst[b2]])
            evac(v_st[vs][:, 0:384], ps[:, b1, 0:384], [pst[b1]], [t_vst[vs]])
            evac(v_st[vs][:, 384:768], ps[:, b2, 0:384], [pst[b2]], [t_vst[vs]])
            S.dma("sp", v_scr[:, n, :], v_st[vs][:], "vst%d" % vs, reads=[t_vst[vs]], writes=[t_vscr])
            if ti >= 6:
                evac(vo_st[:, 0:384], ps[:, b1, 0:384], [pst[b1]], [t_vost])
                evac(vo_st[:, 384:768], ps[:, b2, 0:384], [pst[b2]], [t_vost])
                orow = (ti - 6) * 512 + blk * 128
                S.dma("sp", o_sbv[orow:orow + 128, :], vo_st[:], "vost", reads=[t_vost])
                b1 = bank(5, 8); b2 = bank(5, 8)
                for kc in range(16):
                    mm(ps[:, b1, :], lt(kc), wk[:, kc, 0:512], kc == 0, kc == 15, rd, [pst[b1]])
                for kc in range(16):
                    mm(ps[:, b2, 0:256], lt(kc), wk[:, kc, 512:768], kc == 0, kc == 15, rd, [pst[b2]])
                evac(ko_st[:, 0:512], ps[:, b1, :], [pst[b1]], [t_kost])
                evac(ko_st[:, 512:768], ps[:, b2, 0:256], [pst[b2]], [t_kost])
                S.dma("sp", o_sbk[orow:orow + 128, :], ko_st[:], "kost", reads=[t_kost])
            if blk == 3:
                for h in range(6):
                    b = bank(3, 5)
                    for kc in range(16):
                        mm(ps[:, b, :], wk[:, kc, h * 128:(h + 1) * 128], dst[:, kc, :], kc == 0, kc == 15, [t_wkv] + t_blk[ti % 2], [pst[b]])
                    evac(kT_st[:, h, :], ps[:, b, :], [pst[b]], [t_kst])
                S.dma("sp", kT_scr[:, :, ti * 512:(ti + 1) * 512], kT_st[:], "kst", reads=[t_kst], writes=[t_kscr])
        prep_block(xs, 16, gpb, 3)
        transpose_block(3, 16, hT[:, :, SMP0:SMP0 + 16], t_hT[3])
        S.barrier()
        maybe_stop('p1')
    es2 = es

    with ExitStack() as esM:
        es2 = esM
        ZT = sb("ZT", (128, 16, NQ), BF16); t_ZT = [Tk() for _ in range(16)]
        wbuf = [sb("wbuf%d" % i, (128, 16, 384), BF16) for i in range(2)]; t_wb = [Tk(), Tk()]
        QT = sb("QT", (128, 6, NQ), BF16); t_QT = Tk()
        ost = [sb("ost%d" % i, (128, 384), F32) for i in range(2)]; t_ost = [Tk(), Tk()]
        rden = sb("rden", (128, 512), F32); t_rden = Tk()
        otmp = sb("otmp", (128, 512), F32); t_otmp = Tk()
        KTn = sb("KTn", (128, 6, 128), BF16); t_KTn = Tk()
        Vn = sb("Vn", (128, 768), BF16); t_Vn = Tk()
        wi = [0]
        oi = [0]

        def load_w(src3, w):
            wi[0] += 1
            sl = wi[0] % 2
            S.dma("pool", wbuf[sl][:, :, 0:w], src3, "wb%d" % sl, writes=[t_wb[sl]])
            return sl

        def out_rows(dst_rows, nrows, b_list, widths):
            oi[0] += 1
            sl = oi[0] % 2
            c = 0
            for b, w in zip(b_list, widths):
                evac(ost[sl][0:nrows, c:c + w], ps[0:nrows, b, 0:w], [pst[b]], [t_ost[sl]])
                c += w
            S.dma("sp", dst_rows, ost[sl][0:nrows, 0:c], "ost%d" % sl, reads=[t_ost[sl]])

        OWN_TILES = [(OWN0, 512, 1), (OWN0 + 512, 512, 2), (SMP0, 16, 3)]
        ALL_TILES = [(0, 512, 0)] + OWN_TILES

        def proj_fm(sl, j, tiles, fn):
            for (t0, n, ht) in tiles:
                b = bank(0, 4)
                for kc in range(16):
                    mm(ps[:, b, 0:n], wbuf[sl][:, kc, j * 128:(j + 1) * 128], hT[:, kc, t0:t0 + n], kc == 0, kc == 15,
                       [t_wb[sl], t_hT[ht]], [pst[b]])
                fn(b, t0, n)

        def proj_tm(sl, w, t0, nrows, ht, lo=4, hi=8):
            b = bank(lo, hi)
            for kc in range(16):
                mm(ps[0:nrows, b, 0:w], hT[:, kc, t0:t0 + nrows], wbuf[sl][:, kc, 0:w], kc == 0, kc == 15,
                   [t_wb[sl], t_hT[ht]], [pst[b]])
            return b

        prefetched = {}

        def prefetch_w(col0, w):
            prefetched[col0] = load_w(wsrc[:, :, col0:col0 + w], w)

        def q_seg(col0, width, nheads):
            for (c0, w) in split512(width):
                if c0 == 0 and col0 in prefetched:
                    sl = prefetched.pop(col0)
                else:
                    sl = load_w(wsrc[:, :, col0 + c0:col0 + c0 + w], w)
                for j in range(w // 128):
                    head = c0 // 128 + j
                    proj_fm(sl, j, OWN_TILES, lambda b, t0, n, head=head: evac(
                        QT[:, head, t0 - OWN0:t0 - OWN0 + n], ps[:, b, 0:n], [pst[b]], [t_QT], scale=QS))

        def g_seg(col0, width, zoff):
            for (c0, w) in split512(width):
                sl = load_w(wsrc[:, :, col0 + c0:col0 + c0 + w], w)
                for j in range(w // 128):
                    zc = zoff + c0 // 128 + j
                    proj_fm(sl, j, OWN_TILES, lambda b, t0, n, zc=zc: S.op(
                        "act", lambda: nc.scalar.activation(out=ZT[:, zc, t0 - OWN0:t0 - OWN0 + n], in_=ps[:, b, 0:n], func=AF.Silu),
                        [pst[b]], [t_ZT[zc]]))

        def epilogue(zc, q0, n, o_ap, den_ap, b_o, b_d):
            if den_ap is not None:
                S.op("dve", lambda: nc.vector.reciprocal(out=rden[:, 0:n], in_=den_ap), [pst[b_d]], [t_rden])
                S.op("dve", lambda: nc.vector.tensor_tensor(out=otmp[:, 0:n], in0=o_ap, in1=rden[:, 0:n], op=ALU.mult),
                     [pst[b_o], t_rden], [t_otmp])
            else:
                S.op("dve", lambda: nc.vector.tensor_copy(out=otmp[:, 0:n], in_=o_ap), [pst[b_o]], [t_otmp])
            S.op("dve", lambda: nc.vector.tensor_tensor(out=ZT[:, zc, q0:q0 + n], in0=ZT[:, zc, q0:q0 + n], in1=otmp[:, 0:n],
                                                        op=ALU.mult), [t_otmp, t_ZT[zc]], [t_ZT[zc]])

        S.op("pool", lambda: nc.gpsimd.memset(KTn[:], 0.0), [], [t_KTn])
        S.op("pool", lambda: nc.gpsimd.memset(Vn[:], 0.0), [], [t_Vn])

        q_seg(0, 768, 6)
        for (c0, w) in split512(768):
            sl = load_w(wsrc[:, :, 768 + c0:768 + c0 + w], w)
            for j in range(w // 128):
                head = c0 // 128 + j
                proj_fm(sl, j, [OWN_TILES[2]], lambda b, t0, n, head=head: evac(KTn[:, head, 0:16], ps[:, b, 0:16], [pst[b]], [t_KTn]))
            b = proj_tm(sl, w, SMP0, 16, 3)
            out_rows(o_sbks[:, c0:c0 + w], 16, [b], [w])
        for (c0, w) in split512(768):
            sl = load_w(wsrc[:, :, 1536 + c0:1536 + c0 + w], w)
            b = proj_tm(sl, w, SMP0, 16, 3)
            evac(Vn[0:16, c0:c0 + w], ps[0:16, b, 0:w], [pst[b]], [t_Vn])
            out_rows(o_sbvs[:, c0:c0 + w], 16, [b], [w])
        g_seg(2304, 768, 0)
        maybe_stop('p2')

        with ExitStack() as esS:
            es2 = esS
            KTh = sb("KTh", (128, 4096), BF16); t_KTh = Tk()
            Vh = sb("Vh", (128, 32, 128), BF16); t_Vh = Tk()
            e_t = [sb("e_t%d" % i, (128, 2, 512), BF16) for i in range(3)]; t_e = [Tk(), Tk(), Tk()]
            sp_t = [sb("sp_t%d" % i, (128, 2, 512), BF16) for i in range(2)]; t_sp = [Tk(), Tk()]
            tt_t = [sb("tt_t%d" % i, (128, 2, 512), BF16) for i in range(2)]; t_tt = [Tk(), Tk()]
            w_t = [sb("w_t%d" % i, (128, 2, 512), BF16) for i in range(2)]; t_w = [Tk(), Tk()]
            P_t = [sb("P_t%d" % i, (128, 2, 512), BF16) for i in range(2)]; t_P = [Tk(), Tk()]
            Kb = [sb("Kb%d" % i, (128, 768), BF16) for i in range(2)]; t_Kb = [Tk(), Tk()]
            Vb = [sb("Vb%d" % i, (128, 768), BF16) for i in range(4)]; t_Vb = [Tk() for _ in range(4)]
            KTb = [sb("KTb%d" % i, (128, 768), BF16) for i in range(2)]; t_KTb = [Tk(), Tk()]
            oacc = sb("oacc", (128, 96), F32); t_oacc = Tk()

            def sweep(n, zmm, e_op, sp_op, s_mm, p_op, t_op, w_op, pv, pre=None):
                if pre is not None:
                    pre(0); pre(1)
                zmm(0)
                for k in range(n):
                    e_op(k)
                    if k > 0:
                        t_op(k - 1); w_op(k - 1)
                    sp_op(k)
                    if pre is not None and k + 2 < n:
                        pre(k + 2)
                    if k < n - 1:
                        zmm(k + 1)
                    s_mm(k); p_op(k)
                    if k > 0:
                        pv(k - 1)
                t_op(n - 1); w_op(n - 1); pv(n - 1)

            def flat(t, c0, c1):
                return t[:].rearrange("p a b -> p (a b)")[:, c0:c1]

            for h in range(6):
                S.dma("sp", KTh[:], kT_scr[:, h, :], "kth", reads=[t_kscr], writes=[t_KTh])
                S.dma("sp", Vh[:], v_scr[:, :, h * 128:(h + 1) * 128], "vh", reads=[t_vscr], writes=[t_Vh])
                S.op("pool", lambda: nc.gpsimd.memset(P_t[0][:], 0.0), [], [t_P[0]])
                S.op("pool", lambda: nc.gpsimd.memset(P_t[1][:], 0.0), [], [t_P[1]])
                nB = lambda k: 2 if k >= 4 else 1

                def zmm(k, h=h):
                    r = 31 - k
                    zb = (k % 2) * 2
                    dA = r - 28
                    mm(ps[:, zb, :], KTh[:, r * 128:(r + 1) * 128], QT[:, h, 512:1024], True, dA < 0, [t_KTh, t_QT], [pst[zb]])
                    if dA >= 0:
                        mm(ps[:, zb, :], ident[:], maskB[:, dA, :], False, True, [t_const], [pst[zb]])
                    if k >= 4:
                        dB = r - 24
                        mm(ps[:, zb + 1, :], KTh[:, r * 128:(r + 1) * 128], QT[:, h, 0:512], True, dB < 0, [t_KTh, t_QT], [pst[zb + 1]])
                        if dB >= 0:
                            mm(ps[:, zb + 1, :], ident[:], maskB[:, dB, :], False, True, [t_const], [pst[zb + 1]])

                def e_op(k):
                    r = 31 - k; zb = (k % 2) * 2; s3 = k % 3; nb = nB(k)
                    S.op("act", lambda: nc.scalar.activation(out=e_t[s3][:, 0:nb, :], in_=ps[:, zb:zb + nb, :], func=AF.Exp, bias=bc[:, r:r + 1]),
                         [pst[zb + i] for i in range(nb)] + [t_const], [t_e[s3]])

                def sp_op(k):
                    s_ = k % 2; nb = nB(k)
                    S.op("act", lambda: nc.scalar.activation(out=sp_t[s_][:, 0:nb, :], in_=e_t[k % 3][:, 0:nb, :], func=AF.Ln, bias=1.0),
                         [t_e[k % 3]], [t_sp[s_]])

                def s_mm(k):
                    s_ = k % 2; pc = k % 2
                    for i in range(nB(k)):
                        mm(ps[:, 4 + i, :], ones[:], P_t[pc][:, i, :], True, False, [t_const, t_P[pc]], [pst[4 + i]])
                    for i in range(nB(k)):
                        mm(ps[:, 4 + i, :], tri[:], sp_t[s_][:, i, :], False, True, [t_const, t_sp[s_]], [pst[4 + i]])

                def p_op(k):
                    s_ = k % 2; pc = k % 2; pn = 1 - pc; nb = nB(k)
                    if k == 31:
                        return
                    S.op("pool", lambda: nc.gpsimd.tensor_tensor(out=flat(P_t[pn], 0, nb * 512), in0=flat(P_t[pc], 0, nb * 512),
                                                                 in1=flat(sp_t[s_], 0, nb * 512), op=ALU.add), [t_sp[s_], t_P[pc]], [t_P[pn]])

                def t_op(k):
                    s_ = k % 2; nb = nB(k)
                    S.op("act", lambda: nc.scalar.activation(out=tt_t[s_][:, 0:nb, :], in_=ps[:, 4:4 + nb, :], func=AF.Exp, scale=-1.0),
                         [pst[4 + i] for i in range(nb)], [t_tt[s_]])

                def w_op(k):
                    s_ = k % 2; nb = nB(k)
                    S.op("dve", lambda: nc.vector.tensor_tensor(out=flat(w_t[s_], 0, nb * 512), in0=flat(e_t[k % 3], 0, nb * 512),
                                                                in1=flat(tt_t[s_], 0, nb * 512), op=ALU.mult), [t_e[k % 3], t_tt[s_]], [t_w[s_]])

                def pv(k):
                    r = 31 - k; s_ = k % 2
                    mm(ps[:, 6, :], Vh[:, r, :], w_t[s_][:, 0, :], k == 0, k == 31, [t_Vh, t_w[s_]], [pst[6]])
                    if k >= 4:
                        mm(ps[:, 7, :], Vh[:, r, :], w_t[s_][:, 1, :], k == 4, k == 31, [t_Vh, t_w[s_]], [pst[7]])

                sweep(32, zmm, e_op, sp_op, s_mm, p_op, t_op, w_op, pv)
                epilogue(h, 512, 512, ps[:, 6, :], None, 6, None)
                epilogue(h, 0, 512, ps[:, 7, :], None, 7, None)
            maybe_stop('sbp')

            S.op("pool", lambda: nc.gpsimd.memset(P_t[0][:], 0.0), [t_P[0]], [t_P[0]])
            S.op("pool", lambda: nc.gpsimd.memset(P_t[1][:], 0.0), [t_P[1]], [t_P[1]])
            NS = 33

            t_KTb2 = [[Tk(), Tk()], [Tk(), Tk()]]

            def pre(k):
                if k == 0:
                    return
                s_ = k % 2
                r = 32 - k
                S.dma("pool", Kb[s_][:], csk[r * 128:(r + 1) * 128, :], "kb%d" % s_, writes=[t_Kb[s_]])
                S.dma("pool", Vb[k % 4][:], csv[r * 128:(r + 1) * 128, :], "vb%d" % (k % 4), writes=[t_Vb[k % 4]])
                tb0 = 2 * s_
                for h in range(6):
                    bb = tb0 + h // 4
                    mm(ps[:, bb, (h % 4) * 128:(h % 4 + 1) * 128], Kb[s_][:, h * 128:(h + 1) * 128], ident[:], True, True,
                       [t_Kb[s_], t_const], [pst[bb]])
                evac(KTb[s_][:, 0:512], ps[:, tb0, :], [pst[tb0]], [t_KTb2[s_][0]], eng="act")
                evac(KTb[s_][:, 512:768], ps[:, tb0 + 1, 0:256], [pst[tb0 + 1]], [t_KTb2[s_][1]], eng="dve")

            def zmm(k):
                s_ = k % 2
                zb = 4 + s_
                if k == 0:
                    for h in range(6):
                        mm(ps[:, zb, h * 16:(h + 1) * 16], KTn[:, h, :], QT[:, h, 1024:1040], True, False, [t_KTn, t_QT], [pst[zb]])
                        mm(ps[:, zb, h * 16:(h + 1) * 16], ident[:], maskS[:, h * 16:(h + 1) * 16], False, True, [t_const], [pst[zb]])
                    return
                for h in range(6):
                    mm(ps[:, zb, h * 16:(h + 1) * 16], KTb[s_][:, h * 128:(h + 1) * 128], QT[:, h, 1024:1040], True, True,
                       [t_KTb2[s_][h // 4], t_QT], [pst[zb]])

            def e_op(k):
                s_ = k % 2; zb = 4 + s_
                S.op("act", lambda: nc.scalar.activation(out=flat(e_t[k % 3], 0, 96), in_=ps[:, zb, 0:96], func=AF.Exp), [pst[zb]], [t_e[k % 3]])

            def sp_op(k):
                s_ = k % 2
                S.op("act", lambda: nc.scalar.activation(out=flat(sp_t[s_], 0, 96), in_=flat(e_t[k % 3], 0, 96), func=AF.Ln, bias=1.0),
                     [t_e[k % 3]], [t_sp[s_]])

            def s_mm(k):
                s_ = k % 2; pc = k % 2
                mm(ps[:, 6, 0:96], ones[:], flat(P_t[pc], 0, 96), True, False, [t_const, t_P[pc]], [pst[6]])
                mm(ps[:, 6, 0:96], tri[:], flat(sp_t[s_], 0, 96), False, True, [t_const, t_sp[s_]], [pst[6]])

            def p_op(k):
                s_ = k % 2; pc = k % 2; pn = 1 - pc
                if k == NS - 1:
                    return
                S.op("pool", lambda: nc.gpsimd.tensor_tensor(out=flat(P_t[pn], 0, 96), in0=flat(P_t[pc], 0, 96), in1=flat(sp_t[s_], 0, 96),
                                                             op=ALU.add), [t_sp[s_], t_P[pc]], [t_P[pn]])

            def t_op(k):
                s_ = k % 2
                S.op("act", lambda: nc.scalar.activation(out=flat(tt_t[s_], 0, 96), in_=ps[:, 6, 0:96], func=AF.Exp, scale=-1.0),
                     [pst[6]], [t_tt[s_]])

            def w_op(k):
                s_ = k % 2
                S.op("dve", lambda: nc.vector.tensor_tensor(out=flat(w_t[s_], 0, 96), in0=flat(e_t[k % 3], 0, 96), in1=flat(tt_t[s_], 0, 96),
                                                            op=ALU.mult), [t_e[k % 3], t_tt[s_]], [t_w[s_]])

            def pv(k):
                s_ = k % 2
                for h in range(6):
                    if k == 0:
                        v_ap = Vn[:, h * 128:(h + 1) * 128]; rv = [t_Vn]
                    else:
                        v_ap = Vb[k % 4][:, h * 128:(h + 1) * 128]; rv = [t_Vb[k % 4]]
                    S.op("pe", lambda: nc.tensor.matmul(ps[:, 7, h * 16:(h + 1) * 16], v_ap, flat(w_t[s_], h * 16, (h + 1) * 16),
                                                        start=True, stop=True), rv + [t_w[s_]], [pst[7]], signal=(h == 5))
                if k == 0:
                    S.op("dve", lambda: nc.vector.tensor_copy(out=oacc[:, :], in_=ps[:, 7, 0:96]), [pst[7]], [t_oacc])
                else:
                    S.op("dve", lambda: nc.vector.tensor_tensor(out=oacc[:, :], in0=oacc[:, :], in1=ps[:, 7, 0:96], op=ALU.add),
                         [pst[7], t_oacc], [t_oacc])

            sweep(NS, zmm, e_op, sp_op, s_mm, p_op, t_op, w_op, pv, pre=pre)
            for h in range(6):
                S.op("dve", lambda: nc.vector.tensor_tensor(out=ZT[:, h, 1024:1040], in0=ZT[:, h, 1024:1040], in1=oacc[:, h * 16:(h + 1) * 16],
                                                            op=ALU.mult), [t_oacc, t_ZT[h]], [t_ZT[h]])
            prefetch_w(3072, 384)
            S.barrier()
            maybe_stop('sb')
        es2 = esM

        with ExitStack() as esB:
            es2 = esB
            KTbd = sb("KTbd", (128, 6, NTOK), BF16); t_KTbd = Tk()
            Vbd = sb("Vbd", (128, 12, 768), BF16); t_Vbd = Tk()
            cK = sb("cK", (128, 4, 768), BF16); t_cK = Tk()
            cV = sb("cV", (128, 4, 768), BF16); t_cV = Tk()
            cKT = sb("cKT", (128, 4, 768), BF16); t_cKT = Tk()
            onesrow = sb("onesrow", (1, 64), BF16); t_or = Tk()
            S.op("pool", lambda: nc.gpsimd.memset(onesrow[:], 1.0), [], [t_or])
            S.op("pool", lambda: nc.gpsimd.memset(KTn[:], 0.0), [t_KTn], [t_KTn])
            S.op("pool", lambda: nc.gpsimd.memset(Vn[:], 0.0), [t_Vn], [t_Vn])
            S.dma("pool", cK[:], cbk.rearrange("(b p) c -> p b c", p=128), "ck", writes=[t_cK])
            S.dma("pool", cV[:], cbv.rearrange("(b p) c -> p b c", p=128), "cv", writes=[t_cV])
            q_seg(3072, 768, 6)
            for (c0, w) in split512(768):
                sl = load_w(wsrc[:, :, 3840 + c0:3840 + c0 + w], w)
                for j in range(w // 128):
                    head = c0 // 128 + j

                    def fn(b, t0, n, head=head):
                        evac(KTbd[:, head, t0:t0 + n], ps[:, b, 0:n], [pst[b]], [t_KTbd])
                        if t0 == SMP0:
                            evac(KTn[:, head, 0:16], ps[:, b, 0:16], [pst[b]], [t_KTn])
                    proj_fm(sl, j, ALL_TILES, fn)
                for blk in range(8, 12):
                    b = proj_tm(sl, w, blk * 128, 128, 2)
                    out_rows(o_bdk[(blk - 8) * 128:(blk - 7) * 128, c0:c0 + w], 128, [b], [w])
                b = proj_tm(sl, w, SMP0, 16, 3)
                out_rows(o_bdks[:, c0:c0 + w], 16, [b], [w])
            for (c0, w) in split512(768):
                sl = load_w(wsrc[:, :, 4608 + c0:4608 + c0 + w], w)
                for blk in range(12):
                    b = proj_tm(sl, w, blk * 128, 128, blk // 4)
                    evac(Vbd[:, blk, c0:c0 + w], ps[:, b, 0:w], [pst[b]], [t_Vbd])
                    if blk >= 8:
                        out_rows(o_bdv[(blk - 8) * 128:(blk - 7) * 128, c0:c0 + w], 128, [b], [w])
                b = proj_tm(sl, w, SMP0, 16, 3)
                evac(Vn[0:16, c0:c0 + w], ps[0:16, b, 0:w], [pst[b]], [t_Vn])
                out_rows(o_bdvs[:, c0:c0 + w], 16, [b], [w])
            g_seg(5376, 768, 6)

            ebT = bT; ebTs = bTs; t_eb = Tk()
            S.op("act", lambda: nc.scalar.activation(out=ebT[:].rearrange("p a b c d -> p (a b c d)"),
                                                     in_=bT[:].rearrange("p a b c d -> p (a b c d)"), func=AF.Exp), [t_const], [t_eb])
            S.op("act", lambda: nc.scalar.activation(out=ebTs[:].rearrange("p a b c -> p (a b c)"),
                                                     in_=bTs[:].rearrange("p a b c -> p (a b c)"), func=AF.Exp), [t_const], [t_eb])
            p3 = [sb("p3_%d" % i, (128, 320), BF16) for i in range(3)]; t_p3 = [Tk(), Tk(), Tk()]
            units = []

            def band_S(u):
                nq, blocks, q_ap, eb_ap, zc, q0 = units[u]
                zb = 2 + u % 3
                for i, (kT_ap, rk, v_ap, rv, neg_ap) in enumerate(blocks):
                    mm(ps[:, zb, i * nq:(i + 1) * nq], kT_ap, q_ap, True, neg_ap is None, rk + [t_QT], [pst[zb]])
                    if neg_ap is not None:
                        mm(ps[:, zb, i * nq:(i + 1) * nq], neg_ap, onesrow[0:1, 0:nq], False, True, [t_const, t_or], [pst[zb]])

            def band_P(u):
                nq, blocks, q_ap, eb_ap, zc, q0 = units[u]
                zb = 2 + u % 3; sl = u % 3; nb = len(blocks) * nq
                S.op("act", lambda: nc.scalar.activation(out=p3[sl][:, 0:nb], in_=ps[:, zb, 0:nb], func=AF.Exp), [pst[zb]], [t_p3[sl]])
                S.op("dve", lambda: nc.vector.tensor_tensor(out=p3[sl][:, 0:nb], in0=p3[sl][:, 0:nb], in1=eb_ap, op=ALU.mult),
                     [t_p3[sl], t_eb], [t_p3[sl]])

            def band_O(u):
                nq, blocks, q_ap, eb_ap, zc, q0 = units[u]
                ob = 5 + u % 3; sl = u % 3; nbk = len(blocks)
                for i, (kT_ap, rk, v_ap, rv, neg_ap) in enumerate(blocks):
                    S.op("pe", lambda: nc.tensor.matmul(ps[:, ob, 0:nq], v_ap, p3[sl][:, i * nq:(i + 1) * nq], start=(i == 0),
                                                        stop=(i == nbk - 1)), rv + [t_p3[sl]], [pst[ob]], signal=False)
                for i in range(nbk):
                    mm(ps[:, ob, 64:64 + nq], ones[:], p3[sl][:, i * nq:(i + 1) * nq], i == 0, i == nbk - 1, [t_const, t_p3[sl]], [pst[ob]])
                epilogue(zc, q0, nq, ps[:, ob, 0:nq], ps[:, ob, 64:64 + nq], ob, ob)

            for n in range(16):
                g0 = n // 2; par = n % 2
                for h in range(6):
                    blocks = []
                    for i in range(5):
                        g = g0 + i
                        blocks.append((KTbd[:, h, g * 128:(g + 1) * 128], [t_KTbd], Vbd[:, g, h * 128:(h + 1) * 128], [t_Vbd],
                                       negrow[0:1, g, :] if g < 4 else None))
                    units.append((64, blocks, QT[:, h, n * 64:(n + 1) * 64], ebT[:, par, h, :, :].rearrange("p a b -> p (a b)"), 6 + h, n * 64))
            for blk in range(4):
                for h in range(6):
                    bb = h // 4
                    mm(ps[:, bb, (h % 4) * 128:(h % 4 + 1) * 128], cK[:, blk, h * 128:(h + 1) * 128], ident[:], True, True,
                       [t_cK, t_const], [pst[bb]])
                evac(cKT[:, blk, 0:512], ps[:, 0, :], [pst[0]], [t_cKT])
                evac(cKT[:, blk, 512:768], ps[:, 1, 0:256], [pst[1]], [t_cKT])
            for h in range(6):
                blocks = []
                for i in range(4):
                    blocks.append((cKT[:, i, h * 128:(h + 1) * 128], [t_cKT], cV[:, i, h * 128:(h + 1) * 128], [t_cV], None))
                blocks.append((KTn[:, h, :], [t_KTn], Vn[:, h * 128:(h + 1) * 128], [t_Vn], None))
                units.append((16, blocks, QT[:, h, 1024:1040], ebTs[:, h, :, :].rearrange("p a b -> p (a b)"), 6 + h, 1024))
            NU = len(units)
            band_S(0); band_S(1); band_P(0)
            for u in range(NU):
                if u + 2 < NU:
                    band_S(u + 2)
                if u + 1 < NU:
                    band_P(u + 1)
                band_O(u)
            prefetch_w(6144, 384)
            S.barrier()
            maybe_stop('band')
        es2 = esM

        with ExitStack() as esQ:
            es2 = esQ
            pm = sb("pm", (128, 2, 512), BF16); t_pm = Tk()
            cmK = sb("cmK", (128, 2, 512), BF16); t_cmK = Tk()
            cmV = sb("cmV", (128, 2, 512), BF16); t_cmV = Tk()
            cmKT = sb("cmKT", (128, 4, 256), BF16); t_cmKT = Tk()
            S.dma("pool", cmK[:], cmk.rearrange("(b p) c -> p b c", p=128), "ck", writes=[t_cmK])
            S.dma("pool", cmV[:], cmv.rearrange("(b p) c -> p b c", p=128), "cv", writes=[t_cmV])
            xst[0] = sb("xstM0", (128, D), F32); xst[1] = sb("xstM1", (128, D), F32)
            xn[0] = sb("xnM0", (128, D), BF16); xn[1] = sb("xnM1", (128, D), BF16)
            gmb = sb("gmb", (128, D), F32)
            hmT = sb("hmT", (128, 16, 256), BF16); t_hmT = Tk()
            S.dma("sp", gmb[:], gmem_b, "gpb", writes=[t_gpb])
            for mb in range(2):
                prep_block(memx[mb * 128:(mb + 1) * 128, :], 128, gmb, mb)
            q_seg(6144, 512, 4)
            g_seg(6656, 512, 12)
            for mb in range(2):
                transpose_block(mb, 128, hmT[:, :, mb * 128:(mb + 1) * 128], t_hmT)
            wm = w_mkv.rearrange("(kc p) c -> p kc c", p=128)
            for (c0, w) in split512(512):
                sl = load_w(wm[:, :, c0:c0 + w], w)
                for j in range(w // 128):
                    head = c0 // 128 + j
                    b = bank(0, 4)
                    for kc in range(16):
                        mm(ps[:, b, 0:256], wbuf[sl][:, kc, j * 128:(j + 1) * 128], hmT[:, kc, :], kc == 0, kc == 15, [t_wb[sl], t_hmT], [pst[b]])
                    evac(mkT[:, head, :], ps[:, b, 0:256], [pst[b]], [t_mkT])
                for mb in range(2):
                    b = bank(4, 8)
                    for kc in range(16):
                        mm(ps[:, b, 0:w], hmT[:, kc, mb * 128:(mb + 1) * 128], wbuf[sl][:, kc, 0:w], kc == 0, kc == 15, [t_wb[sl], t_hmT], [pst[b]])
                    out_rows(o_mk[mb * 128:(mb + 1) * 128, c0:c0 + w], 128, [b], [w])
            for (c0, w) in split512(512):
                sl = load_w(wm[:, :, 512 + c0:512 + c0 + w], w)
                for mb in range(2):
                    b = bank(4, 8)
                    for kc in range(16):
                        mm(ps[:, b, 0:w], hmT[:, kc, mb * 128:(mb + 1) * 128], wbuf[sl][:, kc, 0:w], kc == 0, kc == 15, [t_wb[sl], t_hmT], [pst[b]])
                    evac(mvb[:, mb, c0:c0 + w], ps[:, b, 0:w], [pst[b]], [t_mvb])
                    out_rows(o_mv[mb * 128:(mb + 1) * 128, c0:c0 + w], 128, [b], [w])
            for mb in range(2):
                for h in range(4):
                    mm(ps[:, 0, h * 128:(h + 1) * 128], cmK[:, mb, h * 128:(h + 1) * 128], ident[:], True, True, [t_cmK, t_const], [pst[0]])
                for h in range(4):
                    evac(cmKT[:, h, mb * 128:(mb + 1) * 128], ps[:, 0, h * 128:(h + 1) * 128], [pst[0]], [t_cmKT])

            pm2 = [pm, sb("pmB", (128, 2, 512), BF16)]; t_pm2 = [t_pm, Tk()]
            mu = [0]

            def mem_unit(n, q_ap, kT_of, rk, v_of, rv, zc, q0):
                mu[0] += 1
                u = mu[0] % 2
                zb = 4 * u; pmx = pm2[u]; tpm = t_pm2[u]
                for mb in range(2):
                    mm(ps[:, zb + mb, 0:n], kT_of(mb), q_ap, True, True, rk + [t_QT], [pst[zb + mb]])
                for mb in range(2):
                    S.op("act", lambda: nc.scalar.activation(out=pmx[:, mb, 0:n], in_=ps[:, zb + mb, 0:n], func=AF.Exp), [pst[zb + mb]], [tpm])
                for mb in range(2):
                    mm(ps[:, zb + 2, 0:n], v_of(mb), pmx[:, mb, 0:n], mb == 0, mb == 1, rv + [tpm], [pst[zb + 2]])
                for mb in range(2):
                    mm(ps[:, zb + 3, 0:n], ones[:], pmx[:, mb, 0:n], mb == 0, mb == 1, [t_const, tpm], [pst[zb + 3]])
                epilogue(zc, q0, n, ps[:, zb + 2, 0:n], ps[:, zb + 3, 0:n], zb + 2, zb + 3)

            for h in range(4):
                for qt in range(2):
                    mem_unit(512, QT[:, h, qt * 512:(qt + 1) * 512], lambda mb, h=h: mkT[:, h, mb * 128:(mb + 1) * 128], [t_mkT],
                             lambda mb, h=h: mvb[:, mb, h * 128:(h + 1) * 128], [t_mvb], 12 + h, qt * 512)
                mem_unit(16, QT[:, h, 1024:1040], lambda mb, h=h: cmKT[:, h, mb * 128:(mb + 1) * 128], [t_cmKT],
                         lambda mb, h=h: cmV[:, mb, h * 128:(h + 1) * 128], [t_cmV], 12 + h, 1024)
            S.barrier()
            maybe_stop('mem')
        es2 = esM

        with ExitStack() as esG:
            es2 = esG
            acc = sb("acc", (128, 2, NQ), F32); t_acc = Tk()
            mst = sb("mst", (128, 2, NQ), BF16); t_mst = Tk()
            sg = [sb("sg%d" % i, (128, 512), BF16) for i in range(2)]; t_sg = [Tk(), Tk()]
            tmp = [sb("tmp%d" % i, (128, 512), F32) for i in range(2)]; t_tmp = [Tk(), Tk()]
            wup = [sb("wup%d" % i, (128, 6, 256), BF16) for i in range(2)]; t_wup = [Tk(), Tk()]
            zoffs = [0, 6, 12]; nkcs = [6, 6, 4]
            gi = 0
            groups = [(cq, br) for cq in range(8) for br in range(3)]

            def merge_load(g):
                cq, br = groups[g]
                col = 7168 + br * 2048 + cq * 256
                sl_ = load_w(wsrc[:, :, col:col + 256], 256)
                us_ = g % 2
                S.dma("pool", wup[us_][:, 0:nkcs[br], :], w_up[br].rearrange("(kc p) c -> p kc c", p=128)[:, :, cq * 256:(cq + 1) * 256],
                      "wup%d" % us_, writes=[t_wup[us_]])
                return sl_, us_

            nxt = merge_load(0)
            for g, (cq, br) in enumerate(groups):
                    sl, us = nxt
                    if g + 1 < len(groups):
                        nxt = merge_load(g + 1)
                    for j in range(2):
                        for (t0, n, ht) in OWN_TILES:
                            gi += 1
                            s2 = gi % 2
                            q0 = t0 - OWN0
                            b1 = bank(0, 4)
                            for kc in range(16):
                                mm(ps[:, b1, 0:n], wbuf[sl][:, kc, j * 128:(j + 1) * 128], hT[:, kc, t0:t0 + n], kc == 0, kc == 15,
                                   [t_wb[sl], t_hT[ht]], [pst[b1]])
                            S.op("act", lambda: nc.scalar.activation(out=sg[s2][:, 0:n], in_=ps[:, b1, 0:n], func=AF.Sigmoid),
                                 [pst[b1]], [t_sg[s2]])
                            b2 = bank(4, 8)
                            nk = nkcs[br]
                            for kc in range(nk):
                                mm(ps[:, b2, 0:n], wup[us][:, kc, j * 128:(j + 1) * 128], ZT[:, zoffs[br] + kc, q0:q0 + n], kc == 0, kc == nk - 1,
                                   [t_wup[us]] + [t_ZT[zoffs[br] + kc]], [pst[b2]])
                            if br == 0:
                                S.op("dve", lambda: nc.vector.tensor_tensor(out=acc[:, j, q0:q0 + n], in0=ps[:, b2, 0:n], in1=sg[s2][:, 0:n],
                                                                            op=ALU.mult), [pst[b2], t_sg[s2]], [t_acc])
                            else:
                                S.op("dve", lambda: nc.vector.tensor_tensor(out=tmp[s2][:, 0:n], in0=ps[:, b2, 0:n], in1=sg[s2][:, 0:n],
                                                                            op=ALU.mult), [pst[b2], t_sg[s2]], [t_tmp[s2]])
                                if br == 1:
                                    S.op("dve", lambda: nc.vector.tensor_tensor(out=acc[:, j, q0:q0 + n], in0=acc[:, j, q0:q0 + n],
                                                                                in1=tmp[s2][:, 0:n], op=ALU.add), [t_tmp[s2], t_acc], [t_acc])
                                else:
                                    S.op("dve", lambda: nc.vector.tensor_tensor(out=mst[:, j, q0:q0 + n], in0=acc[:, j, q0:q0 + n],
                                                                                in1=tmp[s2][:, 0:n], op=ALU.add), [t_tmp[s2], t_acc], [t_mst])
                    if br == 2:
                        S.dma("sp", mg_scr[:, cq * 2:(cq + 1) * 2, :], mst[:], "mst", reads=[t_mst], writes=[t_mgscr])
            S.barrier()
            maybe_stop('merge')
        es2 = esM
    esH.close()
    es2 = es

    with ExitStack() as esF:
        es2 = esF
        mergedT = sb("mergedT", (128, 16, NQ), BF16); t_mg = Tk()
        wo2 = [sb("wo2_%d" % i, (128, 16, 512), BF16) for i in range(2)]; t_wo2 = [Tk(), Tk()]
        gp = sb("gp", (128, D), F32); t_gp = Tk()
        xst[0] = sb("xstF0", (128, D), F32); xst[1] = sb("xstF1", (128, D), F32)
        junk[0] = sb("junkF", (128, 512), BF16)
        ysb = [sb("ysb%d" % i, (128, D), F32) for i in range(2)]; t_ysb = [Tk(), Tk()]
        ypre = sb("ypre", (128, 9, D), F32); t_ypre = [Tk() for _ in range(9)]
        ssq = sb("ssq", (128, 36), F32); t_ssq = [Tk() for _ in range(9)]
        wo = w_out.rearrange("(kc p) c -> p kc c", p=128)
        S.dma("pool", wo2[0][:], wo[:, :, 0:512], "wo0", writes=[t_wo2[0]])
        S.dma("sp", mergedT[:], mg_scr, "mgld", reads=[t_mgscr], writes=[t_mg])
        S.dma("sp", gp[:], gpost, "gpF", writes=[t_gp])
        for cg in range(4):
            sl = cg % 2
            if cg + 1 < 4:
                S.dma("pool", wo2[1 - sl][:], wo[:, :, (cg + 1) * 512:(cg + 2) * 512], "wo%d" % (1 - sl), writes=[t_wo2[1 - sl]])
            for tb in range(9):
                nr = 128 if tb < 8 else 16
                q0 = tb * 128
                b = bank(0, 8)
                for kc in range(16):
                    mm(ps[0:nr, b, :], mergedT[:, kc, q0:q0 + nr], wo2[sl][:, kc, :], kc == 0, kc == 15, [t_mg, t_wo2[sl]], [pst[b]])
                S.op("act", lambda: nc.scalar.activation(out=junk[0][0:nr, :], in_=ps[0:nr, b, :], func=AF.Square,
                                                         accum_out=ssq[0:nr, tb * 4 + cg:tb * 4 + cg + 1]), [pst[b]], [t_junk, t_ssq[tb]])
                S.op("dve", lambda: nc.vector.tensor_copy(out=ypre[0:nr, tb, cg * 512:(cg + 1) * 512], in_=ps[0:nr, b, :]),
                     [pst[b]], [t_ypre[tb]])
                if cg == 3:
                    dsto = o_y[tb * 128:(tb + 1) * 128, :] if tb < 8 else o_ys
                    s2 = tb % 2

                    def ld_x(t_):
                        n_ = 128 if t_ < 8 else 16
                        src_ = xctx[3072 + t_ * 128:3072 + (t_ + 1) * 128, :] if t_ < 8 else xs
                        S.dma("pool", xst[t_ % 2][0:n_, :], src_, "xstF%d" % (t_ % 2), writes=[t_xst[t_ % 2]])
                    if tb == 0:
                        ld_x(0)
                    if tb + 1 < 9:
                        ld_x(tb + 1)
                    S.op("dve", lambda: nc.vector.tensor_reduce(out=stat[0:nr, 0:1], in_=ssq[0:nr, tb * 4:tb * 4 + 4], axis=AX.X, op=ALU.add),
                         [t_ssq[tb]], [t_stat])
                    S.op("act", lambda: nc.scalar.activation(out=stat[0:nr, 1:2], in_=stat[0:nr, 0:1], func=AF.Ln, scale=1.0 / D, bias=EPS),
                         [t_stat], [t_stat])
                    S.op("act", lambda: nc.scalar.activation(out=stat[0:nr, 2:3], in_=stat[0:nr, 1:2], func=AF.Exp, scale=-0.5), [t_stat], [t_stat])
                    S.op("dve", lambda: nc.vector.scalar_tensor_tensor(out=ysb[s2][0:nr, :], in0=ypre[0:nr, tb, :], scalar=stat[0:nr, 2:3],
                                                                       in1=gp[0:nr, :], op0=ALU.mult, op1=ALU.mult),
                         [t_ypre[tb], t_stat, t_gp], [t_ysb[s2]])
                    S.op("dve", lambda: nc.vector.tensor_tensor(out=ysb[s2][0:nr, :], in0=ysb[s2][0:nr, :], in1=xst[s2][0:nr, :], op=ALU.add),
                         [t_ysb[s2], t_xst[s2]], [t_ysb[s2]])
                    S.dma("sp", dsto, ysb[s2][0:nr, :], "yst%d" % s2, reads=[t_ysb[s2]])
        S.finish("sp")
        S.barrier()
    es.close()
    return nc


def _host_consts(rel_bias, j):
    k = np.arange(128)
    ident = np.eye(128, dtype=np.float32)
    tri = (k[:, None] >= k[None, :]).astype(np.float32)
    ones = np.ones((128, 128), np.float32)
    maskB = np.zeros((128, 4, 4, 128), np.float32)
    for i in range(4):
        for s in range(4):
            if s < i:
                maskB[:, i, s, :] = NEG
            elif s == i:
                maskB[:, i, s, :] = np.where(k[:, None] < k[None, :], 0.0, NEG)
    maskB = maskB.reshape(128, 4 * 512)
    q16 = np.arange(16)
    mS = np.where((k[:, None] < 16) & (k[:, None] < q16[None, :]), 0.0, NEG).astype(np.float32)
    maskS = np.tile(mS, (1, 6))
    bc = np.zeros((128, 32), np.float32)
    bc[:, :24 - 8 * j] = NEG
    rb = rel_bias
    bT = np.zeros((128, 2, 6, 5, 64), np.float32)
    ql = np.arange(64)
    for par in range(2):
        for i in range(5):
            kl = i * 128 + k - 64 * par
            valid = (kl >= 0) & (kl < 576)
            idx = np.clip(512 + ql[None, :] - kl[:, None], -256, 256) + 256
            for h in range(6):
                bT[:, par, h, i, :] = np.where(valid[:, None], rb[h][idx], NEG)
    bTs = np.zeros((128, 6, 5, 16), np.float32)
    for i in range(5):
        kl = i * 128 + k
        valid = kl < 528
        idx = np.clip(512 + q16[None, :] - kl[:, None], -256, 256) + 256
        for h in range(6):
            bTs[:, h, i, :] = np.where(valid[:, None], rb[h][idx], NEG)
    negrow = np.zeros((1, 12, 128), np.float32)
    if j == 0:
        negrow[:, 0:4, :] = NEG
    return dict(c_ident=ident, c_tri=tri, c_ones=ones, c_maskB=maskB, c_maskS=maskS.astype(np.float32), c_bc=bc,
                c_bT=bT.reshape(128, -1), c_bTs=bTs.reshape(128, -1), c_negrow=negrow.reshape(1, -1))


_PROG = [None]


def kernel(x_prompt, x_sample, cache_sb_k, cache_sb_v, cache_band_k, cache_band_v, cache_mem_k, cache_mem_v,
           mem_prompt, g_pre, w_in, rel_bias, g_mem, w_mem_kv, w_up_sb, w_up_band, w_up_mem, w_out, g_post):
    f = lambda a: np.ascontiguousarray(np.asarray(a, dtype=np.float32))
    x_prompt = f(x_prompt); x_sample = f(x_sample); mem_prompt = f(mem_prompt)
    if _PROG[0] is None:
        _PROG[0] = build_program()
    nc = _PROG[0]
    shared = dict(
        w_in=f(w_in[0]), w_mkv=f(w_mem_kv[0]), w_up_sb=f(w_up_sb[0]), w_up_bd=f(w_up_band[0]), w_up_mm=f(w_up_mem[0]),
        w_out=f(w_out[0]),
        gpreT=f(np.asarray(g_pre[0]).reshape(16, 128).T), gmemT=f(np.asarray(g_mem[0]).reshape(16, 128).T),
        gpost=f(np.broadcast_to(np.asarray(g_post[0])[None, :], (128, D))),
        gpre_b=f(np.broadcast_to(np.asarray(g_pre[0])[None, :], (128, D))),
        gmem_b=f(np.broadcast_to(np.asarray(g_mem[0])[None, :], (128, D))),
    )
    rb = f(rel_bias[0])
    consts = [_host_consts(rb, j) for j in range(4)]
    in_maps = []
    for c in range(8):
        b, j = c // 4, c % 4
        xc = np.zeros((4096, D), np.float32)
        lo = 1024 * j - 3072
        src_lo = max(lo, 0)
        xc[src_lo - lo:, :] = x_prompt[b, src_lo:1024 * (j + 1), :]
        m = dict(shared)
        m.update(consts[j])
        m.update(
            xctx=xc, xs=f(x_sample[c]), memx=f(mem_prompt[b]),
            csk=f(np.asarray(cache_sb_k[0, c]).reshape(4096, 768)), csv=f(np.asarray(cache_sb_v[0, c]).reshape(4096, 768)),
            cbk=f(np.asarray(cache_band_k[0, c]).reshape(512, 768)), cbv=f(np.asarray(cache_band_v[0, c]).reshape(512, 768)),
            cmk=f(np.asarray(cache_mem_k[0, c]).reshape(256, 512)), cmv=f(np.asarray(cache_mem_v[0, c]).reshape(256, 512)),
        )
        in_maps.append(m)
    res = run_bass_kernel_spmd(nc, in_maps, core_ids=list(range(8)))
    R = res.results
    y_p = np.zeros((2, 4096, D), np.float32)
    y_s = np.zeros((8, 16, D), np.float32)
    sbk_p = np.zeros((1, 2, 4096, 6, 128), np.float32); sbv_p = np.zeros_like(sbk_p)
    bdk_p = np.zeros((1, 2, 512, 6, 128), np.float32); bdv_p = np.zeros_like(bdk_p)
    mk_p = np.zeros((1, 2, 256, 4, 128), np.float32); mv_p = np.zeros_like(mk_p)
    sbk_s = np.zeros((1, 8, 16, 6, 128), np.float32); sbv_s = np.zeros_like(sbk_s)
    bdk_s = np.zeros_like(sbk_s); bdv_s = np.zeros_like(sbk_s)
    for c in range(8):
        b, j = c // 4, c % 4
        r = R[c]
        y_p[b, 1024 * j:1024 * (j + 1)] = r["o_y"]
        y_s[c] = r["o_ys"]
        sbk_p[0, b, 1024 * j:1024 * (j + 1)] = r["o_sbk"].reshape(1024, 6, 128)
        sbv_p[0, b, 1024 * j:1024 * (j + 1)] = r["o_sbv"].reshape(1024, 6, 128)
        if j == 3:
            bdk_p[0, b] = r["o_bdk"].reshape(512, 6, 128)
            bdv_p[0, b] = r["o_bdv"].reshape(512, 6, 128)
        if j == 0:
            mk_p[0, b] = r["o_mk"].reshape(256, 4, 128)
            mv_p[0, b] = r["o_mv"].reshape(256, 4, 128)
        sbk_s[0, c] = r["o_sbks"].reshape(16, 6, 128); sbv_s[0, c] = r["o_sbvs"].reshape(16, 6, 128)
        bdk_s[0, c] = r["o_bdks"].reshape(16, 6, 128); bdv_s[0, c] = r["o_bdvs"].reshape(16, 6, 128)
    return (y_p, y_s, sbk_p, sbv_p, bdk_p, bdv_p, mk_p, mv_p, sbk_s, sbv_s, bdk_s, bdv_s)
```

```python
import numpy as np
from contextlib import ExitStack
import concourse.bass as bass
import concourse.mybir as mybir
from concourse.bass_utils import run_bass_kernel_spmd

F32 = mybir.dt.float32
BF16 = mybir.dt.bfloat16
AF = mybir.ActivationFunctionType
ALU = mybir.AluOpType
AX = mybir.AxisListType

D = 2048
NEG = -30000.0
QS = 128 ** -0.5
IN_W = 13312
OWN0 = 512
SMP0 = 1536
NTOK = 1552
NQ = 1040
EPS = 1e-6


class Tk:
    __slots__ = ("w", "r", "x")

    def __init__(self, x=False):
        self.w = {}
        self.r = {}
        self.x = x


class Sched:
    def __init__(self, nc, es):
        self.nc = nc
        self.es = es
        self.E = dict(pe=nc.tensor, act=nc.scalar, dve=nc.vector, pool=nc.gpsimd, sp=nc.sync)
        self.sem = {}
        self.cnt = {}
        self.waited = {e: {} for e in self.E}
        for e in self.E:
            self._sem(e)

    def _sem(self, key):
        if key not in self.sem:
            self.sem[key] = self.es.enter_context(self.nc.semaphore("sem_" + key))
            self.cnt[key] = 0
        return self.sem[key]

    def _wait(self, e, key, val):
        if key == "pe" and e == "pe":
            return
        if self.waited[e].get(key, 0) >= val:
            return
        self.E[e].wait_ge(self.sem[key], val)
        self.waited[e][key] = val

    def _deps(self, e, reads, writes):
        for t in reads:
            for k, v in t.w.items():
                self._wait(e, k, v)
            if t.x:
                for k, v in t.r.items():
                    if k != e:
                        self._wait(e, k, v)
        for t in writes:
            for k, v in t.w.items():
                self._wait(e, k, v)
            for k, v in t.r.items():
                self._wait(e, k, v)

    def op(self, e, fn, reads=(), writes=(), signal=True):
        self._deps(e, reads, writes)
        ins = fn()
        if signal:
            self.cnt[e] += 1
            ins.then_inc(self.sem[e], 1)
            v = self.cnt[e]
        else:
            v = self.cnt[e] + 1
        for t in reads:
            t.r[e] = v
        for t in writes:
            t.w = {e: v}
            t.r = {}
        return ins

    def dma(self, q, out, in_, key, reads=(), writes=()):
        self._sem(key)
        self._deps(q, reads, writes)
        ins = self.E[q].dma_start(out=out, in_=in_)
        self.cnt[key] += 16
        ins.then_inc(self.sem[key], 16)
        v = self.cnt[key]
        for t in reads:
            t.r[key] = v
        for t in writes:
            t.w = {key: v}
            t.r = {}

    def barrier(self):
        keys = [k for k in self.sem if self.cnt[k] > 0]
        for e in self.E:
            for k in keys:
                if k != e:
                    self._wait(e, k, self.cnt[k])

    def finish(self, e="sp"):
        for k in self.sem:
            if self.cnt[k] > 0 and k != e:
                self._wait(e, k, self.cnt[k])


def split512(width, step=384):
    out = []
    c = 0
    while c < width:
        w = min(step, width - c)
        out.append((c, w))
        c += w
    return out


class _Stop(Exception):
    pass


def build_program(stop=None):
    st = {}
    try:
        return _build(stop, st)
    except _Stop:
        return st["nc"]


def _build(stop, st):
    nc = bass.Bass("TRN2", target_bir_lowering=False)
    es = ExitStack()
    st["nc"] = nc; st["es"] = es

    def maybe_stop(tag):
        if stop == tag:
            S.finish("sp")
            S.barrier()
            raise _Stop()

    def din(name, shape):
        return nc.dram_tensor(name, list(shape), F32, kind="ExternalInput").ap()

    def dout(name, shape):
        return nc.dram_tensor(name, list(shape), F32, kind="ExternalOutput").ap()

    xctx = din("xctx", (4096, D))
    xs = din("xs", (16, D))
    memx = din("memx", (256, D))
    csk = din("csk", (4096, 768)); csv = din("csv", (4096, 768))
    cbk = din("cbk", (512, 768)); cbv = din("cbv", (512, 768))
    cmk = din("cmk", (256, 512)); cmv = din("cmv", (256, 512))
    w_in = din("w_in", (D, IN_W))
    w_mkv = din("w_mkv", (D, 1024))
    w_up = [din("w_up_sb", (768, D)), din("w_up_bd", (768, D)), din("w_up_mm", (512, D))]
    w_out = din("w_out", (D, D))
    gpreT = din("gpreT", (128, 16)); gmemT = din("gmemT", (128, 16))
    gpost = din("gpost", (128, D))
    gpre_b = din("gpre_b", (128, D)); gmem_b = din("gmem_b", (128, D))
    c_ident = din("c_ident", (128, 128)); c_tri = din("c_tri", (128, 128)); c_ones = din("c_ones", (128, 128))
    c_maskB = din("c_maskB", (128, 4 * 512))
    c_maskS = din("c_maskS", (128, 96))
    c_bc = din("c_bc", (128, 32))
    c_bT = din("c_bT", (128, 2 * 6 * 5 * 64))
    c_bTs = din("c_bTs", (128, 6 * 5 * 16))
    c_negrow = din("c_negrow", (1, 12 * 128))

    o_y = dout("o_y", (1024, D)); o_ys = dout("o_ys", (16, D))
    o_sbk = dout("o_sbk", (1024, 768)); o_sbv = dout("o_sbv", (1024, 768))
    o_bdk = dout("o_bdk", (512, 768)); o_bdv = dout("o_bdv", (512, 768))
    o_mk = dout("o_mk", (256, 512)); o_mv = dout("o_mv", (256, 512))
    o_sbks = dout("o_sbks", (16, 768)); o_sbvs = dout("o_sbvs", (16, 768))
    o_bdks = dout("o_bdks", (16, 768)); o_bdvs = dout("o_bdvs", (16, 768))

    kT_scr = nc.dram_tensor("kT_scr", [128, 6, 4096], BF16, kind="Internal").ap()
    v_scr = nc.dram_tensor("v_scr", [128, 32, 768], BF16, kind="Internal").ap()
    t_kscr = Tk(); t_vscr = Tk()

    S = Sched(nc, es)

    def sb(name, shape, dt):
        return es2.enter_context(nc.sbuf_tensor(name, list(shape), dt))

    es2 = es
    ps = es.enter_context(nc.psum_tensor("ps", [128, 8, 512], F32))
    pst = [Tk(True) for _ in range(8)]
    ident = sb("ident", (128, 128), BF16); tri = sb("tri", (128, 128), BF16); ones = sb("ones", (128, 128), BF16)
    maskB = sb("maskB", (128, 4, 512), BF16); maskS = sb("maskS", (128, 96), BF16)
    bc = sb("bc", (128, 32), F32)
    bT = sb("bT", (128, 2, 6, 5, 64), BF16); bTs = sb("bTs", (128, 6, 5, 16), BF16)
    negrow = sb("negrow", (1, 12, 128), BF16)
    gpre = sb("gpre", (128, 16), F32); gmem = sb("gmem", (128, 16), F32)
    t_const = Tk()
    stat = sb("stat", (128, 16), F32); t_stat = Tk()
    mkT = sb("mkT", (128, 4, 256), BF16); t_mkT = Tk()
    mvb = sb("mvb", (128, 2, 512), BF16); t_mvb = Tk()
    mg_scr = nc.dram_tensor("mg_scr", [128, 16, NQ], BF16, kind="Internal").ap()
    t_mgscr = Tk()
    esH = ExitStack()
    st["esH"] = esH
    es2 = esH
    hT = sb("hT", (128, 16, NTOK), BF16)
    t_hT = [Tk() for _ in range(4)]
    es2 = es
    t_xst = [Tk(), Tk()]; t_junk = Tk()
    xst = [None, None]; junk = [None]

    cast_i = [0]

    def cast_load(dst, src, writes, key=None):
        cast_i[0] += 1
        S.dma("pool", dst, src, key or ("cst%d" % cast_i[0]), writes=writes)

    def emit_const_loads():
        _ct = [Tk() for _ in range(12)]
        cast_load(ident[:], c_ident, [_ct[0]]); cast_load(tri[:], c_tri, [_ct[1]]); cast_load(ones[:], c_ones, [_ct[2]])
        cast_load(maskB[:].rearrange("p a b -> p (a b)"), c_maskB, [_ct[3]]); cast_load(maskS[:], c_maskS, [_ct[4]])
        cast_load(bT[:].rearrange("p a b c d -> p (a b c d)"), c_bT, [_ct[5]])
        cast_load(bTs[:].rearrange("p a b c -> p (a b c)"), c_bTs, [_ct[6]])
        cast_load(negrow[:].rearrange("p a b -> p (a b)"), c_negrow, [_ct[7]])
        S.dma("sp", bc[:], c_bc, "cf0", writes=[_ct[8]])
        S.dma("sp", gpre[:], gpreT, "cf1", writes=[_ct[9]])
        S.dma("sp", gmem[:], gmemT, "cf2", writes=[_ct[10]])
        for _t in _ct:
            for _k, _v in _t.w.items():
                t_const.w[_k] = max(t_const.w.get(_k, 0), _v)


    evac_i = [0]

    def evac(out, in_, reads, writes, scale=None, eng=None):
        if eng is None:
            evac_i[0] += 1
            eng = "act" if evac_i[0] % 2 else "dve"
        if eng == "act":
            if scale is None:
                S.op("act", lambda: nc.scalar.activation(out=out, in_=in_, func=AF.Copy), reads, writes)
            else:
                S.op("act", lambda: nc.scalar.activation(out=out, in_=in_, func=AF.Copy, scale=scale), reads, writes)
        else:
            if scale is None:
                S.op("dve", lambda: nc.vector.tensor_copy(out=out, in_=in_), reads, writes)
            else:
                S.op("dve", lambda: nc.vector.tensor_scalar(out=out, in0=in_, scalar1=scale, scalar2=None, op0=ALU.mult), reads, writes)

    bank_i = [0]

    def bank(lo=0, hi=8):
        bank_i[0] += 1
        return lo + bank_i[0] % (hi - lo)

    def mm(out, lhsT, rhs, start, stop, reads, writes):
        S.op("pe", lambda: nc.tensor.matmul(out, lhsT, rhs, start=start, stop=stop), reads, writes, signal=stop)

    xi = [0]
    xn = [None] * 4
    t_xn = [Tk() for _ in range(4)]
    t_stat4 = [Tk() for _ in range(4)]
    gb_ref = [None]
    t_gpb = Tk()

    def prep_block(src_rows, nrows, gb, slot, q="pool"):
        xi[0] += 1
        sl = xi[0] % 2
        c = slot * 4
        S.dma(q, xst[sl][0:nrows, :], src_rows, "xst%d" % sl, writes=[t_xst[sl]])
        S.op("act", lambda: nc.scalar.activation(out=xn[slot][0:nrows, :], in_=xst[sl][0:nrows, :], func=AF.Square,
                                                 accum_out=stat[0:nrows, c:c + 1]), [t_xst[sl]], [t_xn[slot], t_stat4[slot]])
        S.op("act", lambda: nc.scalar.activation(out=stat[0:nrows, c + 1:c + 2], in_=stat[0:nrows, c:c + 1], func=AF.Ln,
                                                 scale=1.0 / D, bias=EPS), [t_stat4[slot]], [t_stat4[slot]])
        S.op("act", lambda: nc.scalar.activation(out=stat[0:nrows, c + 2:c + 3], in_=stat[0:nrows, c + 1:c + 2], func=AF.Exp,
                                                 scale=-0.5), [t_stat4[slot]], [t_stat4[slot]])
        S.op("dve", lambda: nc.vector.scalar_tensor_tensor(out=xn[slot][0:nrows, :], in0=xst[sl][0:nrows, :],
                                                           scalar=stat[0:nrows, c + 2:c + 3], in1=gb[0:nrows, :],
                                                           op0=ALU.mult, op1=ALU.mult), [t_xst[sl], t_stat4[slot], t_gpb], [t_xn[slot]])

    def transpose_block(slot, nrows, dst3, t_dst):
        for j in range(4):
            b = bank(0, 3)
            for i in range(4):
                kc = 4 * j + i
                mm(ps[:, b, i * nrows:(i + 1) * nrows], xn[slot][0:nrows, kc * 128:(kc + 1) * 128], ident[0:nrows, 0:nrows],
                   True, True, [t_xn[slot], t_const], [pst[b]])
            evac(dst3[:, 4 * j:4 * j + 4, :], ps[:, b, 0:4 * nrows].rearrange("p (a b) -> p a b", a=4), [pst[b]], [t_dst])

    wsrc = w_in.rearrange("(kc p) c -> p kc c", p=128)
    with ExitStack() as es1:
        es2 = es1
        xst[0] = sb("xst0", (128, D), F32); xst[1] = sb("xst1", (128, D), F32)
        for i in range(4):
            xn[i] = sb("xn%d" % i, (128, D), BF16)
        gpb = sb("gpb", (128, D), F32)
        S.dma("sp", gpb[:], gpre_b, "gpb", writes=[t_gpb])
        wk = sb("wk", (128, 16, 768), BF16); wv = sb("wv", (128, 16, 768), BF16); t_wkv = Tk()
        t_wv2 = [Tk(), Tk()]
        chunks_v = split512(768)
        t_wchain = Tk()
        cast_load(wv[:, :, 0:chunks_v[0][1]], wsrc[:, :, 1536:1536 + chunks_v[0][1]], [t_wv2[0], t_wchain], key="wvld0")
        prep_block(xctx[0:128, :], 128, gpb, 0, q="sp")
        prep_block(xctx[128:256, :], 128, gpb, 1, q="sp")
        emit_const_loads()
        prep_block(xctx[256:384, :], 128, gpb, 2)
        for ci, (c0, w) in enumerate(chunks_v):
            if ci > 0:
                cast_load(wv[:, :, c0:c0 + w], wsrc[:, :, 1536 + c0:1536 + c0 + w], [t_wv2[ci], t_wchain], key="wvld%d" % ci)
        prep_block(xctx[384:512, :], 128, gpb, 3)
        for (c0, w) in split512(768):
            cast_load(wk[:, :, c0:c0 + w], wsrc[:, :, 768 + c0:768 + c0 + w], [t_wkv, t_wchain], key="wkld")
        es2 = es1
        hTt = [sb("hTt%d" % i, (128, 16, 512), BF16) for i in range(2)]
        kT_st = sb("kT_st", (128, 6, 512), BF16); t_kst = Tk()
        v_st = [sb("v_st%d" % i, (128, 768), BF16) for i in range(2)]; t_vst = [Tk(), Tk()]
        ko_st = sb("ko_st", (128, 768), F32); t_kost = Tk()
        vo_st = sb("vo_st", (128, 768), F32); t_vost = Tk()

        t_blk = [[Tk() for _ in range(4)] for _ in range(2)]

        def tile_ap(ti):
            return hT[:, :, (ti - 5) * 512:(ti - 4) * 512] if ti >= 5 else hTt[ti % 2][:, :, :]

        def do_T(n):
            ti, blk = divmod(n, 4)
            transpose_block(n % 4, 128, tile_ap(ti)[:, :, blk * 128:(blk + 1) * 128], t_blk[ti % 2][blk])

        def do_prep(n):
            if n < 4:
                return
            prep_block(xctx[n * 128:(n + 1) * 128, :], 128, gpb, n % 4)

        do_prep(2)
        do_T(0)
        for n in range(32):
            ti, blk = divmod(n, 4)
            dst = tile_ap(ti)
            if n + 1 < 32:
                do_T(n + 1)
            if n + 3 < 32:
                do_prep(n + 3)
            lt = lambda kc, blk=blk, dst=dst: dst[:, kc, blk * 128:(blk + 1) * 128]
            rd = [t_wkv, t_blk[ti % 2][blk]]
            vs = n % 2
            b1 = bank(5, 8); b2 = bank(5, 8)
            for kc in range(16):
                mm(ps[:, b1, 0:384], lt(kc), wv[:, kc, 0:384], kc == 0, kc == 15, [t_wv2[0], t_blk[ti % 2][blk]], [pst[b1]])
            for kc in range(16):
                mm(ps[:, b2, 0:384], lt(kc), wv[:, kc, 384:768], kc == 0, kc == 15, [t_wv2[1], t_blk[ti % 2][blk]], [pst[b2]])
            evac(v_st[vs][:, 0:384], ps[:, b1, 0:384], [pst[b1]], [t_vst[vs]])
            evac(v_st[vs][:, 384:768], ps[:, b2, 0:384], [pst[b2]], [t_vst[vs]])
            S.dma("sp", v_scr[:, n, :], v_st[vs][:], "vst%d" % vs, reads=[t_vst[vs]], writes=[t_vscr])
            if ti >= 6:
                evac(vo_st[:, 0:384], ps[:, b1, 0:384], [pst[b1]], [t_vost])
                evac(vo_st[:, 384:768], ps[:, b2, 0:384], [pst[b2]], [t_vost])
                orow = (ti - 6) * 512 + blk * 128
                S.dma("sp", o_sbv[orow:orow + 128, :], vo_st[:], "vost", reads=[t_vost])
                b1 = bank(5, 8); b2 = bank(5, 8)
                for kc in range(16):
                    mm(ps[:, b1, :], lt(kc), wk[:, kc, 0:512], kc == 0, kc == 15, rd, [pst[b1]])
                for kc in range(16):
                    mm(ps[:, b2, 0:256], lt(kc), wk[:, kc, 512:768], kc == 0, kc == 15, rd, [pst[b2]])
                evac(ko_st[:, 0:512], ps[:, b1, :], [pst[b1]], [t_kost])
                evac(ko_st[:, 512:768], ps[:, b2, 0:256], [pst[b2]], [t_kost])
                S.dma("sp", o_sbk[orow:orow + 128, :], ko_st[:], "kost", reads=[t_kost])
            if blk == 3:
                for h in range(6):
                    b = bank(3, 5)
                    for kc in range(16):
                        mm(ps[:, b, :], wk[:, kc, h * 128:(h + 1) * 128], dst[:, kc, :], kc == 0, kc == 15, [t_wkv] + t_blk[ti % 2], [pst[b]])
                    evac(kT_st[:, h, :], ps[:, b, :], [pst[b]], [t_kst])
                S.dma("sp", kT_scr[:, :, ti * 512:(ti + 1) * 512], kT_st[:], "kst", reads=[t_kst], writes=[t_kscr])
        prep_block(xs, 16, gpb, 3)
        transpose_block(3, 16, hT[:, :, SMP0:SMP0 + 16], t_hT[3])
        S.barrier()
        maybe_stop('p1')
    es2 = es

    with ExitStack() as esM:
        es2 = esM
        ZT = sb("ZT", (128, 16, NQ), BF16); t_ZT = [Tk() for _ in range(16)]
        wbuf = [sb("wbuf%d" % i, (128, 16, 384), BF16) for i in range(2)]; t_wb = [Tk(), Tk()]
        QT = sb("QT", (128, 6, NQ), BF16); t_QT = Tk()
        ost = [sb("ost%d" % i, (128, 384), F32) for i in range(2)]; t_ost = [Tk(), Tk()]
        rden = sb("rden", (128, 512), F32); t_rden = Tk()
        otmp = sb("otmp", (128, 512), F32); t_otmp = Tk()
        KTn = sb("KTn", (128, 6, 128), BF16); t_KTn = Tk()
        Vn = sb("Vn", (128, 768), BF16); t_Vn = Tk()
        wi = [0]
        oi = [0]

        def load_w(src3, w):
            wi[0] += 1
            sl = wi[0] % 2
            S.dma("pool", wbuf[sl][:, :, 0:w], src3, "wb%d" % sl, writes=[t_wb[sl]])
            return sl

        def out_rows(dst_rows, nrows, b_list, widths):
            oi[0] += 1
            sl = oi[0] % 2
            c = 0
            for b, w in zip(b_list, widths):
                evac(ost[sl][0:nrows, c:c + w], ps[0:nrows, b, 0:w], [pst[b]], [t_ost[sl]])
                c += w
            S.dma("sp", dst_rows, ost[sl][0:nrows, 0:c], "ost%d" % sl, reads=[t_ost[sl]])

        OWN_TILES = [(OWN0, 512, 1), (OWN0 + 512, 512, 2), (SMP0, 16, 3)]
        ALL_TILES = [(0, 512, 0)] + OWN_TILES

        def proj_fm(sl, j, tiles, fn):
            for (t0, n, ht) in tiles:
                b = bank(0, 4)
                for kc in range(16):
                    mm(ps[:, b, 0:n], wbuf[sl][:, kc, j * 128:(j + 1) * 128], hT[:, kc, t0:t0 + n], kc == 0, kc == 15,
                       [t_wb[sl], t_hT[ht]], [pst[b]])
                fn(b, t0, n)

        def proj_tm(sl, w, t0, nrows, ht, lo=4, hi=8):
            b = bank(lo, hi)
            for kc in range(16):
                mm(ps[0:nrows, b, 0:w], hT[:, kc, t0:t0 + nrows], wbuf[sl][:, kc, 0:w], kc == 0, kc == 15,
                   [t_wb[sl], t_hT[ht]], [pst[b]])
            return b

        prefetched = {}

        def prefetch_w(col0, w):
            prefetched[col0] = load_w(wsrc[:, :, col0:col0 + w], w)

        def q_seg(col0, width, nheads):
            for (c0, w) in split512(width):
                if c0 == 0 and col0 in prefetched:
                    sl = prefetched.pop(col0)
                else:
                    sl = load_w(wsrc[:, :, col0 + c0:col0 + c0 + w], w)
                for j in range(w // 128):
                    head = c0 // 128 + j
                    proj_fm(sl, j, OWN_TILES, lambda b, t0, n, head=head: evac(
                        QT[:, head, t0 - OWN0:t0 - OWN0 + n], ps[:, b, 0:n], [pst[b]], [t_QT], scale=QS))

        def g_seg(col0, width, zoff):
            for (c0, w) in split512(width):
                sl = load_w(wsrc[:, :, col0 + c0:col0 + c0 + w], w)
                for j in range(w // 128):
                    zc = zoff + c0 // 128 + j
                    proj_fm(sl, j, OWN_TILES, lambda b, t0, n, zc=zc: S.op(
                        "act", lambda: nc.scalar.activation(out=ZT[:, zc, t0 - OWN0:t0 - OWN0 + n], in_=ps[:, b, 0:n], func=AF.Silu),
                        [pst[b]], [t_ZT[zc]]))

        def epilogue(zc, q0, n, o_ap, den_ap, b_o, b_d):
            if den_ap is not None:
                S.op("dve", lambda: nc.vector.reciprocal(out=rden[:, 0:n], in_=den_ap), [pst[b_d]], [t_rden])
                S.op("dve", lambda: nc.vector.tensor_tensor(out=otmp[:, 0:n], in0=o_ap, in1=rden[:, 0:n], op=ALU.mult),
                     [pst[b_o], t_rden], [t_otmp])
            else:
                S.op("dve", lambda: nc.vector.tensor_copy(out=otmp[:, 0:n], in_=o_ap), [pst[b_o]], [t_otmp])
            S.op("dve", lambda: nc.vector.tensor_tensor(out=ZT[:, zc, q0:q0 + n], in0=ZT[:, zc, q0:q0 + n], in1=otmp[:, 0:n],
                                                        op=ALU.mult), [t_otmp, t_ZT[zc]], [t_ZT[zc]])

        S.op("pool", lambda: nc.gpsimd.memset(KTn[:], 0.0), [], [t_KTn])
        S.op("pool", lambda: nc.gpsimd.memset(Vn[:], 0.0), [], [t_Vn])

        q_seg(0, 768, 6)
        for (c0, w) in split512(768):
            sl = load_w(wsrc[:, :, 768 + c0:768 + c0 + w], w)
            for j in range(w // 128):
                head = c0 // 128 + j
                proj_fm(sl, j, [OWN_TILES[2]], lambda b, t0, n, head=head: evac(KTn[:, head, 0:16], ps[:, b, 0:16], [pst[b]], [t_KTn]))
            b = proj_tm(sl, w, SMP0, 16, 3)
            out_rows(o_sbks[:, c0:c0 + w], 16, [b], [w])
        for (c0, w) in split512(768):
            sl = load_w(wsrc[:, :, 1536 + c0:1536 + c0 + w], w)
            b = proj_tm(sl, w, SMP0, 16, 3)
            evac(Vn[0:16, c0:c0 + w], ps[0:16, b, 0:w], [pst[b]], [t_Vn])
            out_rows(o_sbvs[:, c0:c0 + w], 16, [b], [w])
        g_seg(2304, 768, 0)
        maybe_stop('p2')

        with ExitStack() as esS:
            es2 = esS
            KTh = sb("KTh", (128, 4096), BF16); t_KTh = Tk()
            Vh = sb("Vh", (128, 32, 128), BF16); t_Vh = Tk()
            e_t = [sb("e_t%d" % i, (128, 2, 512), BF16) for i in range(3)]; t_e = [Tk(), Tk(), Tk()]
            sp_t = [sb("sp_t%d" % i, (128, 2, 512), BF16) for i in range(2)]; t_sp = [Tk(), Tk()]
            tt_t = [sb("tt_t%d" % i, (128, 2, 512), BF16) for i in range(2)]; t_tt = [Tk(), Tk()]
            w_t = [sb("w_t%d" % i, (128, 2, 512), BF16) for i in range(2)]; t_w = [Tk(), Tk()]
            P_t = [sb("P_t%d" % i, (128, 2, 512), BF16) for i in range(2)]; t_P = [Tk(), Tk()]
            Kb = [sb("Kb%d" % i, (128, 768), BF16) for i in range(2)]; t_Kb = [Tk(), Tk()]
            Vb = [sb("Vb%d" % i, (128, 768), BF16) for i in range(4)]; t_Vb = [Tk() for _ in range(4)]
            KTb = [sb("KTb%d" % i, (128, 768), BF16) for i in range(2)]; t_KTb = [Tk(), Tk()]
            oacc = sb("oacc", (128, 96), F32); t_oacc = Tk()

            def sweep(n, zmm, e_op, sp_op, s_mm, p_op, t_op, w_op, pv, pre=None):
                if pre is not None:
                    pre(0); pre(1)
                zmm(0)
                for k in range(n):
                    e_op(k)
                    if k > 0:
                        t_op(k - 1); w_op(k - 1)
                    sp_op(k)
                    if pre is not None and k + 2 < n:
                        pre(k + 2)
                    if k < n - 1:
                        zmm(k + 1)
                    s_mm(k); p_op(k)
                    if k > 0:
                        pv(k - 1)
                t_op(n - 1); w_op(n - 1); pv(n - 1)

            def flat(t, c0, c1):
                return t[:].rearrange("p a b -> p (a b)")[:, c0:c1]

            for h in range(6):
                S.dma("sp", KTh[:], kT_scr[:, h, :], "kth", reads=[t_kscr], writes=[t_KTh])
                S.dma("sp", Vh[:], v_scr[:, :, h * 128:(h + 1) * 128], "vh", reads=[t_vscr], writes=[t_Vh])
                S.op("pool", lambda: nc.gpsimd.memset(P_t[0][:], 0.0), [], [t_P[0]])
                S.op("pool", lambda: nc.gpsimd.memset(P_t[1][:], 0.0), [], [t_P[1]])
                nB = lambda k: 2 if k >= 4 else 1

                def zmm(k, h=h):
                    r = 31 - k
                    zb = (k % 2) * 2
                    dA = r - 28
                    mm(ps[:, zb, :], KTh[:, r * 128:(r + 1) * 128], QT[:, h, 512:1024], True, dA < 0, [t_KTh, t_QT], [pst[zb]])
                    if dA >= 0:
                        mm(ps[:, zb, :], ident[:], maskB[:, dA, :], False, True, [t_const], [pst[zb]])
                    if k >= 4:
                        dB = r - 24
                        mm(ps[:, zb + 1, :], KTh[:, r * 128:(r + 1) * 128], QT[:, h, 0:512], True, dB < 0, [t_KTh, t_QT], [pst[zb + 1]])
                        if dB >= 0:
                            mm(ps[:, zb + 1, :], ident[:], maskB[:, dB, :], False, True, [t_const], [pst[zb + 1]])

                def e_op(k):
                    r = 31 - k; zb = (k % 2) * 2; s3 = k % 3; nb = nB(k)
                    S.op("act", lambda: nc.scalar.activation(out=e_t[s3][:, 0:nb, :], in_=ps[:, zb:zb + nb, :], func=AF.Exp, bias=bc[:, r:r + 1]),
                         [pst[zb + i] for i in range(nb)] + [t_const], [t_e[s3]])

                def sp_op(k):
                    s_ = k % 2; nb = nB(k)
                    S.op("act", lambda: nc.scalar.activation(out=sp_t[s_][:, 0:nb, :], in_=e_t[k % 3][:, 0:nb, :], func=AF.Ln, bias=1.0),
                         [t_e[k % 3]], [t_sp[s_]])

                def s_mm(k):
                    s_ = k % 2; pc = k % 2
                    for i in range(nB(k)):
                        mm(ps[:, 4 + i, :], ones[:], P_t[pc][:, i, :], True, False, [t_const, t_P[pc]], [pst[4 + i]])
                    for i in range(nB(k)):
                        mm(ps[:, 4 + i, :], tri[:], sp_t[s_][:, i, :], False, True, [t_const, t_sp[s_]], [pst[4 + i]])

                def p_op(k):
                    s_ = k % 2; pc = k % 2; pn = 1 - pc; nb = nB(k)
                    if k == 31:
                        return
                    S.op("pool", lambda: nc.gpsimd.tensor_tensor(out=flat(P_t[pn], 0, nb * 512), in0=flat(P_t[pc], 0, nb * 512),
                                                                 in1=flat(sp_t[s_], 0, nb * 512), op=ALU.add), [t_sp[s_], t_P[pc]], [t_P[pn]])

                def t_op(k):
                    s_ = k % 2; nb = nB(k)
                    S.op("act", lambda: nc.scalar.activation(out=tt_t[s_][:, 0:nb, :], in_=ps[:, 4:4 + nb, :], func=AF.Exp, scale=-1.0),
                         [pst[4 + i] for i in range(nb)], [t_tt[s_]])

                def w_op(k):
                    s_ = k % 2; nb = nB(k)
                    S.op("dve", lambda: nc.vector.tensor_tensor(out=flat(w_t[s_], 0, nb * 512), in0=flat(e_t[k % 3], 0, nb * 512),
                                                                in1=flat(tt_t[s_], 0, nb * 512), op=ALU.mult), [t_e[k % 3], t_tt[s_]], [t_w[s_]])

                def pv(k):
                    r = 31 - k; s_ = k % 2
                    mm(ps[:, 6, :], Vh[:, r, :], w_t[s_][:, 0, :], k == 0, k == 31, [t_Vh, t_w[s_]], [pst[6]])
                    if k >= 4:
                        mm(ps[:, 7, :], Vh[:, r, :], w_t[s_][:, 1, :], k == 4, k == 31, [t_Vh, t_w[s_]], [pst[7]])

                sweep(32, zmm, e_op, sp_op, s_mm, p_op, t_op, w_op, pv)
                epilogue(h, 512, 512, ps[:, 6, :], None, 6, None)
                epilogue(h, 0, 512, ps[:, 7, :], None, 7, None)
            maybe_stop('sbp')

            S.op("pool", lambda: nc.gpsimd.memset(P_t[0][:], 0.0), [t_P[0]], [t_P[0]])
            S.op("pool", lambda: nc.gpsimd.memset(P_t[1][:], 0.0), [t_P[1]], [t_P[1]])
            NS = 33

            t_KTb2 = [[Tk(), Tk()], [Tk(), Tk()]]

            def pre(k):
                if k == 0:
                    return
                s_ = k % 2
                r = 32 - k
                S.dma("pool", Kb[s_][:], csk[r * 128:(r + 1) * 128, :], "kb%d" % s_, writes=[t_Kb[s_]])
                S.dma("pool", Vb[k % 4][:], csv[r * 128:(r + 1) * 128, :], "vb%d" % (k % 4), writes=[t_Vb[k % 4]])
                tb0 = 2 * s_
                for h in range(6):
                    bb = tb0 + h // 4
                    mm(ps[:, bb, (h % 4) * 128:(h % 4 + 1) * 128], Kb[s_][:, h * 128:(h + 1) * 128], ident[:], True, True,
                       [t_Kb[s_], t_const], [pst[bb]])
                evac(KTb[s_][:, 0:512], ps[:, tb0, :], [pst[tb0]], [t_KTb2[s_][0]], eng="act")
                evac(KTb[s_][:, 512:768], ps[:, tb0 + 1, 0:256], [pst[tb0 + 1]], [t_KTb2[s_][1]], eng="dve")

            def zmm(k):
                s_ = k % 2
                zb = 4 + s_
                if k == 0:
                    for h in range(6):
                        mm(ps[:, zb, h * 16:(h + 1) * 16], KTn[:, h, :], QT[:, h, 1024:1040], True, False, [t_KTn, t_QT], [pst[zb]])
                        mm(ps[:, zb, h * 16:(h + 1) * 16], ident[:], maskS[:, h * 16:(h + 1) * 16], False, True, [t_const], [pst[zb]])
                    return
                for h in range(6):
                    mm(ps[:, zb, h * 16:(h + 1) * 16], KTb[s_][:, h * 128:(h + 1) * 128], QT[:, h, 1024:1040], True, True,
                       [t_KTb2[s_][h // 4], t_QT], [pst[zb]])

            def e_op(k):
                s_ = k % 2; zb = 4 + s_
                S.op("act", lambda: nc.scalar.activation(out=flat(e_t[k % 3], 0, 96), in_=ps[:, zb, 0:96], func=AF.Exp), [pst[zb]], [t_e[k % 3]])

            def sp_op(k):
                s_ = k % 2
                S.op("act", lambda: nc.scalar.activation(out=flat(sp_t[s_], 0, 96), in_=flat(e_t[k % 3], 0, 96), func=AF.Ln, bias=1.0),
                     [t_e[k % 3]], [t_sp[s_]])

            def s_mm(k):
                s_ = k % 2; pc = k % 2
                mm(ps[:, 6, 0:96], ones[:], flat(P_t[pc], 0, 96), True, False, [t_const, t_P[pc]], [pst[6]])
                mm(ps[:, 6, 0:96], tri[:], flat(sp_t[s_], 0, 96), False, True, [t_const, t_sp[s_]], [pst[6]])

            def p_op(k):
                s_ = k % 2; pc = k % 2; pn = 1 - pc
                if k == NS - 1:
                    return
                S.op("pool", lambda: nc.gpsimd.tensor_tensor(out=flat(P_t[pn], 0, 96), in0=flat(P_t[pc], 0, 96), in1=flat(sp_t[s_], 0, 96),
                                                             op=ALU.add), [t_sp[s_], t_P[pc]], [t_P[pn]])

            def t_op(k):
                s_ = k % 2
                S.op("act", lambda: nc.scalar.activation(out=flat(tt_t[s_], 0, 96), in_=ps[:, 6, 0:96], func=AF.Exp, scale=-1.0),
                     [pst[6]], [t_tt[s_]])

            def w_op(k):
                s_ = k % 2
                S.op("dve", lambda: nc.vector.tensor_tensor(out=flat(w_t[s_], 0, 96), in0=flat(e_t[k % 3], 0, 96), in1=flat(tt_t[s_], 0, 96),
                                                            op=ALU.mult), [t_e[k % 3], t_tt[s_]], [t_w[s_]])

            def pv(k):
                s_ = k % 2
                for h in range(6):
                    if k == 0:
                        v_ap = Vn[:, h * 128:(h + 1) * 128]; rv = [t_Vn]
                    else:
                        v_ap = Vb[k % 4][:, h * 128:(h + 1) * 128]; rv = [t_Vb[k % 4]]
                    S.op("pe", lambda: nc.tensor.matmul(ps[:, 7, h * 16:(h + 1) * 16], v_ap, flat(w_t[s_], h * 16, (h + 1) * 16),
                                                        start=True, stop=True), rv + [t_w[s_]], [pst[7]], signal=(h == 5))
                if k == 0:
                    S.op("dve", lambda: nc.vector.tensor_copy(out=oacc[:, :], in_=ps[:, 7, 0:96]), [pst[7]], [t_oacc])
                else:
                    S.op("dve", lambda: nc.vector.tensor_tensor(out=oacc[:, :], in0=oacc[:, :], in1=ps[:, 7, 0:96], op=ALU.add),
                         [pst[7], t_oacc], [t_oacc])

            sweep(NS, zmm, e_op, sp_op, s_mm, p_op, t_op, w_op, pv, pre=pre)
            for h in range(6):
                S.op("dve", lambda: nc.vector.tensor_tensor(out=ZT[:, h, 1024:1040], in0=ZT[:, h, 1024:1040], in1=oacc[:, h * 16:(h + 1) * 16],
                                                            op=ALU.mult), [t_oacc, t_ZT[h]], [t_ZT[h]])
            prefetch_w(3072, 384)
            S.barrier()
            maybe_stop('sb')
        es2 = esM

        with ExitStack() as esB:
            es2 = esB
            KTbd = sb("KTbd", (128, 6, NTOK), BF16); t_KTbd = Tk()
            Vbd = sb("Vbd", (128, 12, 768), BF16); t_Vbd = Tk()
            cK = sb("cK", (128, 4, 768), BF16); t_cK = Tk()
            cV = sb("cV", (128, 4, 768), BF16); t_cV = Tk()
            cKT = sb("cKT", (128, 4, 768), BF16); t_cKT = Tk()
            onesrow = sb("onesrow", (1, 64), BF16); t_or = Tk()
            S.op("pool", lambda: nc.gpsimd.memset(onesrow[:], 1.0), [], [t_or])
            S.op("pool", lambda: nc.gpsimd.memset(KTn[:], 0.0), [t_KTn], [t_KTn])
            S.op("pool", lambda: nc.gpsimd.memset(Vn[:], 0.0), [t_Vn], [t_Vn])
            S.dma("pool", cK[:], cbk.rearrange("(b p) c -> p b c", p=128), "ck", writes=[t_cK])
            S.dma("pool", cV[:], cbv.rearrange("(b p) c -> p b c", p=128), "cv", writes=[t_cV])
            q_seg(3072, 768, 6)
            for (c0, w) in split512(768):
                sl = load_w(wsrc[:, :, 3840 + c0:3840 + c0 + w], w)
                for j in range(w // 128):
                    head = c0 // 128 + j

                    def fn(b, t0, n, head=head):
                        evac(KTbd[:, head, t0:t0 + n], ps[:, b, 0:n], [pst[b]], [t_KTbd])
                        if t0 == SMP0:
                            evac(KTn[:, head, 0:16], ps[:, b, 0:16], [pst[b]], [t_KTn])
                    proj_fm(sl, j, ALL_TILES, fn)
                for blk in range(8, 12):
                    b = proj_tm(sl, w, blk * 128, 128, 2)
                    out_rows(o_bdk[(blk - 8) * 128:(blk - 7) * 128, c0:c0 + w], 128, [b], [w])
                b = proj_tm(sl, w, SMP0, 16, 3)
                out_rows(o_bdks[:, c0:c0 + w], 16, [b], [w])
            for (c0, w) in split512(768):
                sl = load_w(wsrc[:, :, 4608 + c0:4608 + c0 + w], w)
                for blk in range(12):
                    b = proj_tm(sl, w, blk * 128, 128, blk // 4)
                    evac(Vbd[:, blk, c0:c0 + w], ps[:, b, 0:w], [pst[b]], [t_Vbd])
                    if blk >= 8:
                        out_rows(o_bdv[(blk - 8) * 128:(blk - 7) * 128, c0:c0 + w], 128, [b], [w])
                b = proj_tm(sl, w, SMP0, 16, 3)
                evac(Vn[0:16, c0:c0 + w], ps[0:16, b, 0:w], [pst[b]], [t_Vn])
                out_rows(o_bdvs[:, c0:c0 + w], 16, [b], [w])
            g_seg(5376, 768, 6)

            ebT = bT; ebTs = bTs; t_eb = Tk()
            S.op("act", lambda: nc.scalar.activation(out=ebT[:].rearrange("p a b c d -> p (a b c d)"),
                                                     in_=bT[:].rearrange("p a b c d -> p (a b c d)"), func=AF.Exp), [t_const], [t_eb])
            S.op("act", lambda: nc.scalar.activation(out=ebTs[:].rearrange("p a b c -> p (a b c)"),
                                                     in_=bTs[:].rearrange("p a b c -> p (a b c)"), func=AF.Exp), [t_const], [t_eb])
            p3 = [sb("p3_%d" % i, (128, 320), BF16) for i in range(3)]; t_p3 = [Tk(), Tk(), Tk()]
            units = []

            def band_S(u):
                nq, blocks, q_ap, eb_ap, zc, q0 = units[u]
                zb = 2 + u % 3
                for i, (kT_ap, rk, v_ap, rv, neg_ap) in enumerate(blocks):
                    mm(ps[:, zb, i * nq:(i + 1) * nq], kT_ap, q_ap, True, neg_ap is None, rk + [t_QT], [pst[zb]])
                    if neg_ap is not None:
                        mm(ps[:, zb, i * nq:(i + 1) * nq], neg_ap, onesrow[0:1, 0:nq], False, True, [t_const, t_or], [pst[zb]])

            def band_P(u):
                nq, blocks, q_ap, eb_ap, zc, q0 = units[u]
                zb = 2 + u % 3; sl = u % 3; nb = len(blocks) * nq
                S.op("act", lambda: nc.scalar.activation(out=p3[sl][:, 0:nb], in_=ps[:, zb, 0:nb], func=AF.Exp), [pst[zb]], [t_p3[sl]])
                S.op("dve", lambda: nc.vector.tensor_tensor(out=p3[sl][:, 0:nb], in0=p3[sl][:, 0:nb], in1=eb_ap, op=ALU.mult),
                     [t_p3[sl], t_eb], [t_p3[sl]])

            def band_O(u):
                nq, blocks, q_ap, eb_ap, zc, q0 = units[u]
                ob = 5 + u % 3; sl = u % 3; nbk = len(blocks)
                for i, (kT_ap, rk, v_ap, rv, neg_ap) in enumerate(blocks):
                    S.op("pe", lambda: nc.tensor.matmul(ps[:, ob, 0:nq], v_ap, p3[sl][:, i * nq:(i + 1) * nq], start=(i == 0),
                                                        stop=(i == nbk - 1)), rv + [t_p3[sl]], [pst[ob]], signal=False)
                for i in range(nbk):
                    mm(ps[:, ob, 64:64 + nq], ones[:], p3[sl][:, i * nq:(i + 1) * nq], i == 0, i == nbk - 1, [t_const, t_p3[sl]], [pst[ob]])
                epilogue(zc, q0, nq, ps[:, ob, 0:nq], ps[:, ob, 64:64 + nq], ob, ob)

            for n in range(16):
                g0 = n // 2; par = n % 2
                for h in range(6):
                    blocks = []
                    for i in range(5):
                        g = g0 + i
                        blocks.append((KTbd[:, h, g * 128:(g + 1) * 128], [t_KTbd], Vbd[:, g, h * 128:(h + 1) * 128], [t_Vbd],
                                       negrow[0:1, g, :] if g < 4 else None))
                    units.append((64, blocks, QT[:, h, n * 64:(n + 1) * 64], ebT[:, par, h, :, :].rearrange("p a b -> p (a b)"), 6 + h, n * 64))
            for blk in range(4):
                for h in range(6):
                    bb = h // 4
                    mm(ps[:, bb, (h % 4) * 128:(h % 4 + 1) * 128], cK[:, blk, h * 128:(h + 1) * 128], ident[:], True, True,
                       [t_cK, t_const], [pst[bb]])
                evac(cKT[:, blk, 0:512], ps[:, 0, :], [pst[0]], [t_cKT])
                evac(cKT[:, blk, 512:768], ps[:, 1, 0:256], [pst[1]], [t_cKT])
            for h in range(6):
                blocks = []
                for i in range(4):
                    blocks.append((cKT[:, i, h * 128:(h + 1) * 128], [t_cKT], cV[:, i, h * 128:(h + 1) * 128], [t_cV], None))
                blocks.append((KTn[:, h, :], [t_KTn], Vn[:, h * 128:(h + 1) * 128], [t_Vn], None))
                units.append((16, blocks, QT[:, h, 1024:1040], ebTs[:, h, :, :].rearrange("p a b -> p (a b)"), 6 + h, 1024))
            NU = len(units)
            band_S(0); band_S(1); band_P(0)
            for u in range(NU):
                if u + 2 < NU:
                    band_S(u + 2)
                if u + 1 < NU:
                    band_P(u + 1)
                band_O(u)
            prefetch_w(6144, 384)
            S.barrier()
            maybe_stop('band')
        es2 = esM

        with ExitStack() as esQ:
            es2 = esQ
            pm = sb("pm", (128, 2, 512), BF16); t_pm = Tk()
            cmK = sb("cmK", (128, 2, 512), BF16); t_cmK = Tk()
            cmV = sb("cmV", (128, 2, 512), BF16); t_cmV = Tk()
            cmKT = sb("cmKT", (128, 4, 256), BF16); t_cmKT = Tk()
            S.dma("pool", cmK[:], cmk.rearrange("(b p) c -> p b c", p=128), "ck", writes=[t_cmK])
            S.dma("pool", cmV[:], cmv.rearrange("(b p) c -> p b c", p=128), "cv", writes=[t_cmV])
            xst[0] = sb("xstM0", (128, D), F32); xst[1] = sb("xstM1", (128, D), F32)
            xn[0] = sb("xnM0", (128, D), BF16); xn[1] = sb("xnM1", (128, D), BF16)
            gmb = sb("gmb", (128, D), F32)
            hmT = sb("hmT", (128, 16, 256), BF16); t_hmT = Tk()
            S.dma("sp", gmb[:], gmem_b, "gpb", writes=[t_gpb])
            for mb in range(2):
                prep_block(memx[mb * 128:(mb + 1) * 128, :], 128, gmb, mb)
            q_seg(6144, 512, 4)
            g_seg(6656, 512, 12)
            for mb in range(2):
                transpose_block(mb, 128, hmT[:, :, mb * 128:(mb + 1) * 128], t_hmT)
            wm = w_mkv.rearrange("(kc p) c -> p kc c", p=128)
            for (c0, w) in split512(512):
                sl = load_w(wm[:, :, c0:c0 + w], w)
                for j in range(w // 128):
                    head = c0 // 128 + j
                    b = bank(0, 4)
                    for kc in range(16):
                        mm(ps[:, b, 0:256], wbuf[sl][:, kc, j * 128:(j + 1) * 128], hmT[:, kc, :], kc == 0, kc == 15, [t_wb[sl], t_hmT], [pst[b]])
                    evac(mkT[:, head, :], ps[:, b, 0:256], [pst[b]], [t_mkT])
                for mb in range(2):
                    b = bank(4, 8)
                    for kc in range(16):
                        mm(ps[:, b, 0:w], hmT[:, kc, mb * 128:(mb + 1) * 128], wbuf[sl][:, kc, 0:w], kc == 0, kc == 15, [t_wb[sl], t_hmT], [pst[b]])
                    out_rows(o_mk[mb * 128:(mb + 1) * 128, c0:c0 + w], 128, [b], [w])
            for (c0, w) in split512(512):
                sl = load_w(wm[:, :, 512 + c0:512 + c0 + w], w)
                for mb in range(2):
                    b = bank(4, 8)
                    for kc in range(16):
                        mm(ps[:, b, 0:w], hmT[:, kc, mb * 128:(mb + 1) * 128], wbuf[sl][:, kc, 0:w], kc == 0, kc == 15, [t_wb[sl], t_hmT], [pst[b]])
                    evac(mvb[:, mb, c0:c0 + w], ps[:, b, 0:w], [pst[b]], [t_mvb])
                    out_rows(o_mv[mb * 128:(mb + 1) * 128, c0:c0 + w], 128, [b], [w])
            for mb in range(2):
                for h in range(4):
                    mm(ps[:, 0, h * 128:(h + 1) * 128], cmK[:, mb, h * 128:(h + 1) * 128], ident[:], True, True, [t_cmK, t_const], [pst[0]])
                for h in range(4):
                    evac(cmKT[:, h, mb * 128:(mb + 1) * 128], ps[:, 0, h * 128:(h + 1) * 128], [pst[0]], [t_cmKT])

            pm2 = [pm, sb("pmB", (128, 2, 512), BF16)]; t_pm2 = [t_pm, Tk()]
            mu = [0]

            def mem_unit(n, q_ap, kT_of, rk, v_of, rv, zc, q0):
                mu[0] += 1
                u = mu[0] % 2
                zb = 4 * u; pmx = pm2[u]; tpm = t_pm2[u]
                for mb in range(2):
                    mm(ps[:, zb + mb, 0:n], kT_of(mb), q_ap, True, True, rk + [t_QT], [pst[zb + mb]])
                for mb in range(2):
                    S.op("act", lambda: nc.scalar.activation(out=pmx[:, mb, 0:n], in_=ps[:, zb + mb, 0:n], func=AF.Exp), [pst[zb + mb]], [tpm])
                for mb in range(2):
                    mm(ps[:, zb + 2, 0:n], v_of(mb), pmx[:, mb, 0:n], mb == 0, mb == 1, rv + [tpm], [pst[zb + 2]])
                for mb in range(2):
                    mm(ps[:, zb + 3, 0:n], ones[:], pmx[:, mb, 0:n], mb == 0, mb == 1, [t_const, tpm], [pst[zb + 3]])
                epilogue(zc, q0, n, ps[:, zb + 2, 0:n], ps[:, zb + 3, 0:n], zb + 2, zb + 3)

            for h in range(4):
                for qt in range(2):
                    mem_unit(512, QT[:, h, qt * 512:(qt + 1) * 512], lambda mb, h=h: mkT[:, h, mb * 128:(mb + 1) * 128], [t_mkT],
                             lambda mb, h=h: mvb[:, mb, h * 128:(h + 1) * 128], [t_mvb], 12 + h, qt * 512)
                mem_unit(16, QT[:, h, 1024:1040], lambda mb, h=h: cmKT[:, h, mb * 128:(mb + 1) * 128], [t_cmKT],
                         lambda mb, h=h: cmV[:, mb, h * 128:(h + 1) * 128], [t_cmV], 12 + h, 1024)
            S.barrier()
            maybe_stop('mem')
        es2 = esM

        with ExitStack() as esG:
            es2 = esG
            acc = sb("acc", (128, 2, NQ), F32); t_acc = Tk()
            mst = sb("mst", (128, 2, NQ), BF16); t_mst = Tk()
            sg = [sb("sg%d" % i, (128, 512), BF16) for i in range(2)]; t_sg = [Tk(), Tk()]
            tmp = [sb("tmp%d" % i, (128, 512), F32) for i in range(2)]; t_tmp = [Tk(), Tk()]
            wup = [sb("wup%d" % i, (128, 6, 256), BF16) for i in range(2)]; t_wup = [Tk(), Tk()]
            zoffs = [0, 6, 12]; nkcs = [6, 6, 4]
            gi = 0
            groups = [(cq, br) for cq in range(8) for br in range(3)]

            def merge_load(g):
                cq, br = groups[g]
                col = 7168 + br * 2048 + cq * 256
                sl_ = load_w(wsrc[:, :, col:col + 256], 256)
                us_ = g % 2
                S.dma("pool", wup[us_][:, 0:nkcs[br], :], w_up[br].rearrange("(kc p) c -> p kc c", p=128)[:, :, cq * 256:(cq + 1) * 256],
                      "wup%d" % us_, writes=[t_wup[us_]])
                return sl_, us_

            nxt = merge_load(0)
            for g, (cq, br) in enumerate(groups):
                    sl, us = nxt
                    if g + 1 < len(groups):
                        nxt = merge_load(g + 1)
                    for j in range(2):
                        for (t0, n, ht) in OWN_TILES:
                            gi += 1
                            s2 = gi % 2
                            q0 = t0 - OWN0
                            b1 = bank(0, 4)
                            for kc in range(16):
                                mm(ps[:, b1, 0:n], wbuf[sl][:, kc, j * 128:(j + 1) * 128], hT[:, kc, t0:t0 + n], kc == 0, kc == 15,
                                   [t_wb[sl], t_hT[ht]], [pst[b1]])
                            S.op("act", lambda: nc.scalar.activation(out=sg[s2][:, 0:n], in_=ps[:, b1, 0:n], func=AF.Sigmoid),
                                 [pst[b1]], [t_sg[s2]])
                            b2 = bank(4, 8)
                            nk = nkcs[br]
                            for kc in range(nk):
                                mm(ps[:, b2, 0:n], wup[us][:, kc, j * 128:(j + 1) * 128], ZT[:, zoffs[br] + kc, q0:q0 + n], kc == 0, kc == nk - 1,
                                   [t_wup[us]] + [t_ZT[zoffs[br] + kc]], [pst[b2]])
                            if br == 0:
                                S.op("dve", lambda: nc.vector.tensor_tensor(out=acc[:, j, q0:q0 + n], in0=ps[:, b2, 0:n], in1=sg[s2][:, 0:n],
                                                                            op=ALU.mult), [pst[b2], t_sg[s2]], [t_acc])
                            else:
                                S.op("dve", lambda: nc.vector.tensor_tensor(out=tmp[s2][:, 0:n], in0=ps[:, b2, 0:n], in1=sg[s2][:, 0:n],
                                                                            op=ALU.mult), [pst[b2], t_sg[s2]], [t_tmp[s2]])
                                if br == 1:
                                    S.op("dve", lambda: nc.vector.tensor_tensor(out=acc[:, j, q0:q0 + n], in0=acc[:, j, q0:q0 + n],
                                                                                in1=tmp[s2][:, 0:n], op=ALU.add), [t_tmp[s2], t_acc], [t_acc])
                                else:
                                    S.op("dve", lambda: nc.vector.tensor_tensor(out=mst[:, j, q0:q0 + n], in0=acc[:, j, q0:q0 + n],
                                                                                in1=tmp[s2][:, 0:n], op=ALU.add), [t_tmp[s2], t_acc], [t_mst])
                    if br == 2:
                        S.dma("sp", mg_scr[:, cq * 2:(cq + 1) * 2, :], mst[:], "mst", reads=[t_mst], writes=[t_mgscr])
            S.barrier()
            maybe_stop('merge')
        es2 = esM
    esH.close()
    es2 = es

    with ExitStack() as esF:
        es2 = esF
        mergedT = sb("mergedT", (128, 16, NQ), BF16); t_mg = Tk()
        wo2 = [sb("wo2_%d" % i, (128, 16, 512), BF16) for i in range(2)]; t_wo2 = [Tk(), Tk()]
        gp = sb("gp", (128, D), F32); t_gp = Tk()
        xst[0] = sb("xstF0", (128, D), F32); xst[1] = sb("xstF1", (128, D), F32)
        junk[0] = sb("junkF", (128, 512), BF16)
        ysb = [sb("ysb%d" % i, (128, D), F32) for i in range(2)]; t_ysb = [Tk(), Tk()]
        ypre = sb("ypre", (128, 9, D), F32); t_ypre = [Tk() for _ in range(9)]
        ssq = sb("ssq", (128, 36), F32); t_ssq = [Tk() for _ in range(9)]
        wo = w_out.rearrange("(kc p) c -> p kc c", p=128)
        S.dma("pool", wo2[0][:], wo[:, :, 0:512], "wo0", writes=[t_wo2[0]])
        S.dma("sp", mergedT[:], mg_scr, "mgld", reads=[t_mgscr], writes=[t_mg])
        S.dma("sp", gp[:], gpost, "gpF", writes=[t_gp])
        for cg in range(4):
            sl = cg % 2
            if cg + 1 < 4:
                S.dma("pool", wo2[1 - sl][:], wo[:, :, (cg + 1) * 512:(cg + 2) * 512], "wo%d" % (1 - sl), writes=[t_wo2[1 - sl]])
            for tb in range(9):
                nr = 128 if tb < 8 else 16
                q0 = tb * 128
                b = bank(0, 8)
                for kc in range(16):
                    mm(ps[0:nr, b, :], mergedT[:, kc, q0:q0 + nr], wo2[sl][:, kc, :], kc == 0, kc == 15, [t_mg, t_wo2[sl]], [pst[b]])
                S.op("act", lambda: nc.scalar.activation(out=junk[0][0:nr, :], in_=ps[0:nr, b, :], func=AF.Square,
                                                         accum_out=ssq[0:nr, tb * 4 + cg:tb * 4 + cg + 1]), [pst[b]], [t_junk, t_ssq[tb]])
                S.op("dve", lambda: nc.vector.tensor_copy(out=ypre[0:nr, tb, cg * 512:(cg + 1) * 512], in_=ps[0:nr, b, :]),
                     [pst[b]], [t_ypre[tb]])
                if cg == 3:
                    dsto = o_y[tb * 128:(tb + 1) * 128, :] if tb < 8 else o_ys
                    s2 = tb % 2

                    def ld_x(t_):
                        n_ = 128 if t_ < 8 else 16
                        src_ = xctx[3072 + t_ * 128:3072 + (t_ + 1) * 128, :] if t_ < 8 else xs
                        S.dma("pool", xst[t_ % 2][0:n_, :], src_, "xstF%d" % (t_ % 2), writes=[t_xst[t_ % 2]])
                    if tb == 0:
                        ld_x(0)
                    if tb + 1 < 9:
                        ld_x(tb + 1)
                    S.op("dve", lambda: nc.vector.tensor_reduce(out=stat[0:nr, 0:1], in_=ssq[0:nr, tb * 4:tb * 4 + 4], axis=AX.X, op=ALU.add),
                         [t_ssq[tb]], [t_stat])
                    S.op("act", lambda: nc.scalar.activation(out=stat[0:nr, 1:2], in_=stat[0:nr, 0:1], func=AF.Ln, scale=1.0 / D, bias=EPS),
                         [t_stat], [t_stat])
                    S.op("act", lambda: nc.scalar.activation(out=stat[0:nr, 2:3], in_=stat[0:nr, 1:2], func=AF.Exp, scale=-0.5), [t_stat], [t_stat])
                    S.op("dve", lambda: nc.vector.scalar_tensor_tensor(out=ysb[s2][0:nr, :], in0=ypre[0:nr, tb, :], scalar=stat[0:nr, 2:3],
                                                                       in1=gp[0:nr, :], op0=ALU.mult, op1=ALU.mult),
                         [t_ypre[tb], t_stat, t_gp], [t_ysb[s2]])
                    S.op("dve", lambda: nc.vector.tensor_tensor(out=ysb[s2][0:nr, :], in0=ysb[s2][0:nr, :], in1=xst[s2][0:nr, :], op=ALU.add),
                         [t_ysb[s2], t_xst[s2]], [t_ysb[s2]])
                    S.dma("sp", dsto, ysb[s2][0:nr, :], "yst%d" % s2, reads=[t_ysb[s2]])
        S.finish("sp")
        S.barrier()
    es.close()
    return nc


def _host_consts(rel_bias, j):
    k = np.arange(128)
    ident = np.eye(128, dtype=np.float32)
    tri = (k[:, None] >= k[None, :]).astype(np.float32)
    ones = np.ones((128, 128), np.float32)
    maskB = np.zeros((128, 4, 4, 128), np.float32)
    for i in range(4):
        for s in range(4):
            if s < i:
                maskB[:, i, s, :] = NEG
            elif s == i:
                maskB[:, i, s, :] = np.where(k[:, None] < k[None, :], 0.0, NEG)
    maskB = maskB.reshape(128, 4 * 512)
    q16 = np.arange(16)
    mS = np.where((k[:, None] < 16) & (k[:, None] < q16[None, :]), 0.0, NEG).astype(np.float32)
    maskS = np.tile(mS, (1, 6))
    bc = np.zeros((128, 32), np.float32)
    bc[:, :24 - 8 * j] = NEG
    rb = rel_bias
    bT = np.zeros((128, 2, 6, 5, 64), np.float32)
    ql = np.arange(64)
    for par in range(2):
        for i in range(5):
            kl = i * 128 + k - 64 * par
            valid = (kl >= 0) & (kl < 576)
            idx = np.clip(512 + ql[None, :] - kl[:, None], -256, 256) + 256
            for h in range(6):
                bT[:, par, h, i, :] = np.where(valid[:, None], rb[h][idx], NEG)
    bTs = np.zeros((128, 6, 5, 16), np.float32)
    for i in range(5):
        kl = i * 128 + k
        valid = kl < 528
        idx = np.clip(512 + q16[None, :] - kl[:, None], -256, 256) + 256
        for h in range(6):
            bTs[:, h, i, :] = np.where(valid[:, None], rb[h][idx], NEG)
    negrow = np.zeros((1, 12, 128), np.float32)
    if j == 0:
        negrow[:, 0:4, :] = NEG
    return dict(c_ident=ident, c_tri=tri, c_ones=ones, c_maskB=maskB, c_maskS=maskS.astype(np.float32), c_bc=bc,
                c_bT=bT.reshape(128, -1), c_bTs=bTs.reshape(128, -1), c_negrow=negrow.reshape(1, -1))


_PROG = [None]


def kernel(x_prompt, x_sample, cache_sb_k, cache_sb_v, cache_band_k, cache_band_v, cache_mem_k, cache_mem_v,
           mem_prompt, g_pre, w_in, rel_bias, g_mem, w_mem_kv, w_up_sb, w_up_band, w_up_mem, w_out, g_post):
    f = lambda a: np.ascontiguousarray(np.asarray(a, dtype=np.float32))
    x_prompt = f(x_prompt); x_sample = f(x_sample); mem_prompt = f(mem_prompt)
    if _PROG[0] is None:
        _PROG[0] = build_program()
    nc = _PROG[0]
    shared = dict(
        w_in=f(w_in[0]), w_mkv=f(w_mem_kv[0]), w_up_sb=f(w_up_sb[0]), w_up_bd=f(w_up_band[0]), w_up_mm=f(w_up_mem[0]),
        w_out=f(w_out[0]),
        gpreT=f(np.asarray(g_pre[0]).reshape(16, 128).T), gmemT=f(np.asarray(g_mem[0]).reshape(16, 128).T),
        gpost=f(np.broadcast_to(np.asarray(g_post[0])[None, :], (128, D))),
        gpre_b=f(np.broadcast_to(np.asarray(g_pre[0])[None, :], (128, D))),
        gmem_b=f(np.broadcast_to(np.asarray(g_mem[0])[None, :], (128, D))),
    )
    rb = f(rel_bias[0])
    consts = [_host_consts(rb, j) for j in range(4)]
    in_maps = []
    for c in range(8):
        b, j = c // 4, c % 4
        xc = np.zeros((4096, D), np.float32)
        lo = 1024 * j - 3072
        src_lo = max(lo, 0)
        xc[src_lo - lo:, :] = x_prompt[b, src_lo:1024 * (j + 1), :]
        m = dict(shared)
        m.update(consts[j])
        m.update(
            xctx=xc, xs=f(x_sample[c]), memx=f(mem_prompt[b]),
            csk=f(np.asarray(cache_sb_k[0, c]).reshape(4096, 768)), csv=f(np.asarray(cache_sb_v[0, c]).reshape(4096, 768)),
            cbk=f(np.asarray(cache_band_k[0, c]).reshape(512, 768)), cbv=f(np.asarray(cache_band_v[0, c]).reshape(512, 768)),
            cmk=f(np.asarray(cache_mem_k[0, c]).reshape(256, 512)), cmv=f(np.asarray(cache_mem_v[0, c]).reshape(256, 512)),
        )
        in_maps.append(m)
    res = run_bass_kernel_spmd(nc, in_maps, core_ids=list(range(8)))
    R = res.results
    y_p = np.zeros((2, 4096, D), np.float32)
    y_s = np.zeros((8, 16, D), np.float32)
    sbk_p = np.zeros((1, 2, 4096, 6, 128), np.float32); sbv_p = np.zeros_like(sbk_p)
    bdk_p = np.zeros((1, 2, 512, 6, 128), np.float32); bdv_p = np.zeros_like(bdk_p)
    mk_p = np.zeros((1, 2, 256, 4, 128), np.float32); mv_p = np.zeros_like(mk_p)
    sbk_s = np.zeros((1, 8, 16, 6, 128), np.float32); sbv_s = np.zeros_like(sbk_s)
    bdk_s = np.zeros_like(sbk_s); bdv_s = np.zeros_like(sbk_s)
    for c in range(8):
        b, j = c // 4, c % 4
        r = R[c]
        y_p[b, 1024 * j:1024 * (j + 1)] = r["o_y"]
        y_s[c] = r["o_ys"]
        sbk_p[0, b, 1024 * j:1024 * (j + 1)] = r["o_sbk"].reshape(1024, 6, 128)
        sbv_p[0, b, 1024 * j:1024 * (j + 1)] = r["o_sbv"].reshape(1024, 6, 128)
        if j == 3:
            bdk_p[0, b] = r["o_bdk"].reshape(512, 6, 128)
            bdv_p[0, b] = r["o_bdv"].reshape(512, 6, 128)
        if j == 0:
            mk_p[0, b] = r["o_mk"].reshape(256, 4, 128)
            mv_p[0, b] = r["o_mv"].reshape(256, 4, 128)
        sbk_s[0, c] = r["o_sbks"].reshape(16, 6, 128); sbv_s[0, c] = r["o_sbvs"].reshape(16, 6, 128)
        bdk_s[0, c] = r["o_bdks"].reshape(16, 6, 128); bdv_s[0, c] = r["o_bdvs"].reshape(16, 6, 128)
    return (y_p, y_s, sbk_p, sbv_p, bdk_p, bdv_p, mk_p, mv_p, sbk_s, sbv_s, bdk_s, bdv_s)
```

```python
import numpy as np
from contextlib import ExitStack
import concourse.bass as bass
import concourse.mybir as mybir
from concourse.bass_utils import run_bass_kernel_spmd

F32 = mybir.dt.float32
BF16 = mybir.dt.bfloat16
AF = mybir.ActivationFunctionType
ALU = mybir.AluOpType
AX = mybir.AxisListType

D = 2048
NEG = -30000.0
QS = 128 ** -0.5
IN_W = 13312
OWN0 = 512
SMP0 = 1536
NTOK = 1552
NQ = 1040
EPS = 1e-6


class Tk:
    __slots__ = ("w", "r", "x")

    def __init__(self, x=False):
        self.w = {}
        self.r = {}
        self.x = x


class Sched:
    def __init__(self, nc, es):
        self.nc = nc
        self.es = es
        self.E = dict(pe=nc.tensor, act=nc.scalar, dve=nc.vector, pool=nc.gpsimd, sp=nc.sync)
        self.sem = {}
        self.cnt = {}
        self.waited = {e: {} for e in self.E}
        for e in self.E:
            self._sem(e)

    def _sem(self, key):
        if key not in self.sem:
            self.sem[key] = self.es.enter_context(self.nc.semaphore("sem_" + key))
            self.cnt[key] = 0
        return self.sem[key]

    def _wait(self, e, key, val):
        if key == "pe" and e == "pe":
            return
        if self.waited[e].get(key, 0) >= val:
            return
        self.E[e].wait_ge(self.sem[key], val)
        self.waited[e][key] = val

    def _deps(self, e, reads, writes):
        for t in reads:
            for k, v in t.w.items():
                self._wait(e, k, v)
            if t.x:
                for k, v in t.r.items():
                    if k != e:
                        self._wait(e, k, v)
        for t in writes:
            for k, v in t.w.items():
                self._wait(e, k, v)
            for k, v in t.r.items():
                self._wait(e, k, v)

    def op(self, e, fn, reads=(), writes=(), signal=True):
        self._deps(e, reads, writes)
        ins = fn()
        if signal:
            self.cnt[e] += 1
            ins.then_inc(self.sem[e], 1)
            v = self.cnt[e]
        else:
            v = self.cnt[e] + 1
        for t in reads:
            t.r[e] = v
        for t in writes:
            t.w = {e: v}
            t.r = {}
        return ins

    def dma(self, q, out, in_, key, reads=(), writes=()):
        self._sem(key)
        self._deps(q, reads, writes)
        ins = self.E[q].dma_start(out=out, in_=in_)
        self.cnt[key] += 16
        ins.then_inc(self.sem[key], 16)
        v = self.cnt[key]
        for t in reads:
            t.r[key] = v
        for t in writes:
            t.w = {key: v}
            t.r = {}

    def barrier(self):
        keys = [k for k in self.sem if self.cnt[k] > 0]
        for e in self.E:
            for k in keys:
                if k != e:
                    self._wait(e, k, self.cnt[k])

    def finish(self, e="sp"):
        for k in self.sem:
            if self.cnt[k] > 0 and k != e:
                self._wait(e, k, self.cnt[k])


def split512(width, step=384):
    out = []
    c = 0
    while c < width:
        w = min(step, width - c)
        out.append((c, w))
        c += w
    return out


class _Stop(Exception):
    pass


def build_program(stop=None):
    st = {}
    try:
        return _build(stop, st)
    except _Stop:
        return st["nc"]


def _build(stop, st):
    nc = bass.Bass("TRN2", target_bir_lowering=False)
    es = ExitStack()
    st["nc"] = nc; st["es"] = es

    def maybe_stop(tag):
        if stop == tag:
            S.finish("sp")
            S.barrier()
            raise _Stop()

    def din(name, shape):
        return nc.dram_tensor(name, list(shape), F32, kind="ExternalInput").ap()

    def dout(name, shape):
        return nc.dram_tensor(name, list(shape), F32, kind="ExternalOutput").ap()

    xctx = din("xctx", (4096, D))
    xs = din("xs", (16, D))
    memx = din("memx", (256, D))
    csk = din("csk", (4096, 768)); csv = din("csv", (4096, 768))
    cbk = din("cbk", (512, 768)); cbv = din("cbv", (512, 768))
    cmk = din("cmk", (256, 512)); cmv = din("cmv", (256, 512))
    w_in = din("w_in", (D, IN_W))
    w_mkv = din("w_mkv", (D, 1024))
    w_up = [din("w_up_sb", (768, D)), din("w_up_bd", (768, D)), din("w_up_mm", (512, D))]
    w_out = din("w_out", (D, D))
    gpreT = din("gpreT", (128, 16)); gmemT = din("gmemT", (128, 16))
    gpost = din("gpost", (128, D))
    gpre_b = din("gpre_b", (128, D)); gmem_b = din("gmem_b", (128, D))
    c_ident = din("c_ident", (128, 128)); c_tri = din("c_tri", (128, 128)); c_ones = din("c_ones", (128, 128))
    c_maskB = din("c_maskB", (128, 4 * 512))
    c_maskS = din("c_maskS", (128, 96))
    c_bc = din("c_bc", (128, 32))
    c_bT = din("c_bT", (128, 2 * 6 * 5 * 64))
    c_bTs = din("c_bTs", (128, 6 * 5 * 16))
    c_negrow = din("c_negrow", (1, 12 * 128))

    o_y = dout("o_y", (1024, D)); o_ys = dout("o_ys", (16, D))
    o_sbk = dout("o_sbk", (1024, 768)); o_sbv = dout("o_sbv", (1024, 768))
    o_bdk = dout("o_bdk", (512, 768)); o_bdv = dout("o_bdv", (512, 768))
    o_mk = dout("o_mk", (256, 512)); o_mv = dout("o_mv", (256, 512))
    o_sbks = dout("o_sbks", (16, 768)); o_sbvs = dout("o_sbvs", (16, 768))
    o_bdks = dout("o_bdks", (16, 768)); o_bdvs = dout("o_bdvs", (16, 768))

    kT_scr = nc.dram_tensor("kT_scr", [128, 6, 4096], BF16, kind="Internal").ap()
    v_scr = nc.dram_tensor("v_scr", [128, 32, 768], BF16, kind="Internal").ap()
    t_kscr = Tk(); t_vscr = Tk()

    S = Sched(nc, es)

    def sb(name, shape, dt):
        return es2.enter_context(nc.sbuf_tensor(name, list(shape), dt))

    es2 = es
    ps = es.enter_context(nc.psum_tensor("ps", [128, 8, 512], F32))
    pst = [Tk(True) for _ in range(8)]
    ident = sb("ident", (128, 128), BF16); tri = sb("tri", (128, 128), BF16); ones = sb("ones", (128, 128), BF16)
    maskB = sb("maskB", (128, 4, 512), BF16); maskS = sb("maskS", (128, 96), BF16)
    bc = sb("bc", (128, 32), F32)
    bT = sb("bT", (128, 2, 6, 5, 64), BF16); bTs = sb("bTs", (128, 6, 5, 16), BF16)
    negrow = sb("negrow", (1, 12, 128), BF16)
    gpre = sb("gpre", (128, 16), F32); gmem = sb("gmem", (128, 16), F32)
    t_const = Tk()
    stat = sb("stat", (128, 16), F32); t_stat = Tk()
    mkT = sb("mkT", (128, 4, 256), BF16); t_mkT = Tk()
    mvb = sb("mvb", (128, 2, 512), BF16); t_mvb = Tk()
    mg_scr = nc.dram_tensor("mg_scr", [128, 16, NQ], BF16, kind="Internal").ap()
    t_mgscr = Tk()
    esH = ExitStack()
    st["esH"] = esH
    es2 = esH
    hT = sb("hT", (128, 16, NTOK), BF16)
    t_hT = [Tk() for _ in range(4)]
    es2 = es
    t_xst = [Tk(), Tk()]; t_junk = Tk()
    xst = [None, None]; junk = [None]

    cast_i = [0]

    def cast_load(dst, src, writes, key=None):
        cast_i[0] += 1
        S.dma("pool", dst, src, key or ("cst%d" % cast_i[0]), writes=writes)

    def emit_const_loads():
        _ct = [Tk() for _ in range(12)]
        cast_load(ident[:], c_ident, [_ct[0]]); cast_load(tri[:], c_tri, [_ct[1]]); cast_load(ones[:], c_ones, [_ct[2]])
        cast_load(maskB[:].rearrange("p a b -> p (a b)"), c_maskB, [_ct[3]]); cast_load(maskS[:], c_maskS, [_ct[4]])
        cast_load(bT[:].rearrange("p a b c d -> p (a b c d)"), c_bT, [_ct[5]])
        cast_load(bTs[:].rearrange("p a b c -> p (a b c)"), c_bTs, [_ct[6]])
        cast_load(negrow[:].rearrange("p a b -> p (a b)"), c_negrow, [_ct[7]])
        S.dma("sp", bc[:], c_bc, "cf0", writes=[_ct[8]])
        S.dma("sp", gpre[:], gpreT, "cf1", writes=[_ct[9]])
        S.dma("sp", gmem[:], gmemT, "cf2", writes=[_ct[10]])
        for _t in _ct:
            for _k, _v in _t.w.items():
                t_const.w[_k] = max(t_const.w.get(_k, 0), _v)


    evac_i = [0]

    def evac(out, in_, reads, writes, scale=None, eng=None):
        if eng is None:
            evac_i[0] += 1
            eng = "act" if evac_i[0] % 2 else "dve"
        if eng == "act":
            if scale is None:
                S.op("act", lambda: nc.scalar.activation(out=out, in_=in_, func=AF.Copy), reads, writes)
            else:
                S.op("act", lambda: nc.scalar.activation(out=out, in_=in_, func=AF.Copy, scale=scale), reads, writes)
        else:
            if scale is None:
                S.op("dve", lambda: nc.vector.tensor_copy(out=out, in_=in_), reads, writes)
            else:
                S.op("dve", lambda: nc.vector.tensor_scalar(out=out, in0=in_, scalar1=scale, scalar2=None, op0=ALU.mult), reads, writes)

    bank_i = [0]

    def bank(lo=0, hi=8):
        bank_i[0] += 1
        return lo + bank_i[0] % (hi - lo)

    def mm(out, lhsT, rhs, start, stop, reads, writes):
        S.op("pe", lambda: nc.tensor.matmul(out, lhsT, rhs, start=start, stop=stop), reads, writes, signal=stop)

    xi = [0]
    xn = [None] * 4
    t_xn = [Tk() for _ in range(4)]
    t_stat4 = [Tk() for _ in range(4)]
    gb_ref = [None]
    t_gpb = Tk()

    def prep_block(src_rows, nrows, gb, slot, q="pool"):
        xi[0] += 1
        sl = xi[0] % 2
        c = slot * 4
        S.dma(q, xst[sl][0:nrows, :], src_rows, "xst%d" % sl, writes=[t_xst[sl]])
        S.op("act", lambda: nc.scalar.activation(out=xn[slot][0:nrows, :], in_=xst[sl][0:nrows, :], func=AF.Square,
                                                 accum_out=stat[0:nrows, c:c + 1]), [t_xst[sl]], [t_xn[slot], t_stat4[slot]])
        S.op("act", lambda: nc.scalar.activation(out=stat[0:nrows, c + 1:c + 2], in_=stat[0:nrows, c:c + 1], func=AF.Ln,
                                                 scale=1.0 / D, bias=EPS), [t_stat4[slot]], [t_stat4[slot]])
        S.op("act", lambda: nc.scalar.activation(out=stat[0:nrows, c + 2:c + 3], in_=stat[0:nrows, c + 1:c + 2], func=AF.Exp,
                                                 scale=-0.5), [t_stat4[slot]], [t_stat4[slot]])
        S.op("dve", lambda: nc.vector.scalar_tensor_tensor(out=xn[slot][0:nrows, :], in0=xst[sl][0:nrows, :],
                                                           scalar=stat[0:nrows, c + 2:c + 3], in1=gb[0:nrows, :],
                                                           op0=ALU.mult, op1=ALU.mult), [t_xst[sl], t_stat4[slot], t_gpb], [t_xn[slot]])

    def transpose_block(slot, nrows, dst3, t_dst):
        for j in range(4):
            b = bank(0, 3)
            for i in range(4):
                kc = 4 * j + i
                mm(ps[:, b, i * nrows:(i + 1) * nrows], xn[slot][0:nrows, kc * 128:(kc + 1) * 128], ident[0:nrows, 0:nrows],
                   True, True, [t_xn[slot], t_const], [pst[b]])
            evac(dst3[:, 4 * j:4 * j + 4, :], ps[:, b, 0:4 * nrows].rearrange("p (a b) -> p a b", a=4), [pst[b]], [t_dst])

    wsrc = w_in.rearrange("(kc p) c -> p kc c", p=128)
    with ExitStack() as es1:
        es2 = es1
        xst[0] = sb("xst0", (128, D), F32); xst[1] = sb("xst1", (128, D), F32)
        for i in range(4):
            xn[i] = sb("xn%d" % i, (128, D), BF16)
        gpb = sb("gpb", (128, D), F32)
        S.dma("sp", gpb[:], gpre_b, "gpb", writes=[t_gpb])
        wk = sb("wk", (128, 16, 768), BF16); wv = sb("wv", (128, 16, 768), BF16); t_wkv = Tk()
        t_wv2 = [Tk(), Tk()]
        chunks_v = split512(768)
        t_wchain = Tk()
        cast_load(wv[:, :, 0:chunks_v[0][1]], wsrc[:, :, 1536:1536 + chunks_v[0][1]], [t_wv2[0], t_wchain], key="wvld0")
        prep_block(xctx[0:128, :], 128, gpb, 0, q="sp")
        prep_block(xctx[128:256, :], 128, gpb, 1, q="sp")
        emit_const_loads()
        prep_block(xctx[256:384, :], 128, gpb, 2)
        for ci, (c0, w) in enumerate(chunks_v):
            if ci > 0:
                cast_load(wv[:, :, c0:c0 + w], wsrc[:, :, 1536 + c0:1536 + c0 + w], [t_wv2[ci], t_wchain], key="wvld%d" % ci)
        prep_block(xctx[384:512, :], 128, gpb, 3)
        for (c0, w) in split512(768):
            cast_load(wk[:, :, c0:c0 + w], wsrc[:, :, 768 + c0:768 + c0 + w], [t_wkv, t_wchain], key="wkld")
        es2 = es1
        hTt = [sb("hTt%d" % i, (128, 16, 512), BF16) for i in range(2)]
        kT_st = sb("kT_st", (128, 6, 512), BF16); t_kst = Tk()
        v_st = [sb("v_st%d" % i, (128, 768), BF16) for i in range(2)]; t_vst = [Tk(), Tk()]
        ko_st = sb("ko_st", (128, 768), F32); t_kost = Tk()
        vo_st = sb("vo_st", (128, 768), F32); t_vost = Tk()

        t_blk = [[Tk() for _ in range(4)] for _ in range(2)]

        def tile_ap(ti):
            return hT[:, :, (ti - 5) * 512:(ti - 4) * 512] if ti >= 5 else hTt[ti % 2][:, :, :]

        def do_T(n):
            ti, blk = divmod(n, 4)
            transpose_block(n % 4, 128, tile_ap(ti)[:, :, blk * 128:(blk + 1) * 128], t_blk[ti % 2][blk])

        def do_prep(n):
            if n < 4:
                return
            prep_block(xctx[n * 128:(n + 1) * 128, :], 128, gpb, n % 4)

        do_prep(2)
        do_T(0)
        for n in range(32):
            ti, blk = divmod(n, 4)
            dst = tile_ap(ti)
            if n + 1 < 32:
                do_T(n + 1)
            if n + 3 < 32:
                do_prep(n + 3)
            lt = lambda kc, blk=blk, dst=dst: dst[:, kc, blk * 128:(blk + 1) * 128]
            rd = [t_wkv, t_blk[ti % 2][blk]]
            vs = n % 2
            b1 = bank(5, 8); b2 = bank(5, 8)
            for kc in range(16):
                mm(ps[:, b1, 0:384], lt(kc), wv[:, kc, 0:384], kc == 0, kc == 15, [t_wv2[0], t_blk[ti % 2][blk]], [pst[b1]])
            for kc in range(16):
                mm(ps[:, b2, 0:384], lt(kc), wv[:, kc, 384:768], kc == 0, kc == 15, [t_wv2[1], t_blk[ti % 2][blk]], [pst[b2]])
            evac(v_st[vs][:, 0:384], ps[:, b1, 0:384], [pst[b1]], [t_vst[vs]])
            evac(v_st[vs][:, 384:768], ps[:, b2, 0:384], [pst[b2]], [t_vst[vs]])
            S.dma("sp", v_scr[:, n, :], v_st[vs][:], "vst%d" % vs, reads=[t_vst[vs]], writes=[t_vscr])
            if ti >= 6:
                evac(vo_st[:, 0:384], ps[:, b1, 0:384], [pst[b1]], [t_vost])
                evac(vo_st[:, 384:768], ps[:, b2, 0:384], [pst[b2]], [t_vost])
                orow = (ti - 6) * 512 + blk * 128
                S.dma("sp", o_sbv[orow:orow + 128, :], vo_st[:], "vost", reads=[t_vost])
                b1 = bank(5, 8); b2 = bank(5, 8)
                for kc in range(16):
                    mm(ps[:, b1, :], lt(kc), wk[:, kc, 0:512], kc == 0, kc == 15, rd, [pst[b1]])
                for kc in range(16):
                    mm(ps[:, b2, 0:256], lt(kc), wk[:, kc, 512:768], kc == 0, kc == 15, rd, [pst[b2]])
                evac(ko_st[:, 0:512], ps[:, b1, :], [pst[b1]], [t_kost])
                evac(ko_st[:, 512:768], ps[:, b2, 0:256], [pst[b2]], [t_kost])
                S.dma("sp", o_sbk[orow:orow + 128, :], ko_st[:], "kost", reads=[t_kost])
            if blk == 3:
                for h in range(6):
                    b = bank(3, 5)
                    for kc in range(16):
                        mm(ps[:, b, :], wk[:, kc, h * 128:(h + 1) * 128], dst[:, kc, :], kc == 0, kc == 15, [t_wkv] + t_blk[ti % 2], [pst[b]])
                    evac(kT_st[:, h, :], ps[:, b, :], [pst[b]], [t_kst])
                S.dma("sp", kT_scr[:, :, ti * 512:(ti + 1) * 512], kT_st[:], "kst", reads=[t_kst], writes=[t_kscr])
        prep_block(xs, 16, gpb, 3)
        transpose_block(3, 16, hT[:, :, SMP0:SMP0 + 16], t_hT[3])
        S.barrier()
        maybe_stop('p1')
    es2 = es

    with ExitStack() as esM:
        es2 = esM
        ZT = sb("ZT", (128, 16, NQ), BF16); t_ZT = [Tk() for _ in range(16)]
        wbuf = [sb("wbuf%d" % i, (128, 16, 384), BF16) for i in range(2)]; t_wb = [Tk(), Tk()]
        QT = sb("QT", (128, 6, NQ), BF16); t_QT = Tk()
        ost = [sb("ost%d" % i, (128, 384), F32) for i in range(2)]; t_ost = [Tk(), Tk()]
        rden = sb("rden", (128, 512), F32); t_rden = Tk()
        otmp = sb("otmp", (128, 512), F32); t_otmp = Tk()
        KTn = sb("KTn", (128, 6, 128), BF16); t_KTn = Tk()
        Vn = sb("Vn", (128, 768), BF16); t_Vn = Tk()
        wi = [0]
        oi = [0]

        def load_w(src3, w):
            wi[0] += 1
            sl = wi[0] % 2
            S.dma("pool", wbuf[sl][:, :, 0:w], src3, "wb%d" % sl, writes=[t_wb[sl]])
            return sl

        def out_rows(dst_rows, nrows, b_list, widths):
            oi[0] += 1
            sl = oi[0] % 2
            c = 0
            for b, w in zip(b_list, widths):
                evac(ost[sl][0:nrows, c:c + w], ps[0:nrows, b, 0:w], [pst[b]], [t_ost[sl]])
                c += w
            S.dma("sp", dst_rows, ost[sl][0:nrows, 0:c], "ost%d" % sl, reads=[t_ost[sl]])

        OWN_TILES = [(OWN0, 512, 1), (OWN0 + 512, 512, 2), (SMP0, 16, 3)]
        ALL_TILES = [(0, 512, 0)] + OWN_TILES

        def proj_fm(sl, j, tiles, fn):
            for (t0, n, ht) in tiles:
                b = bank(0, 4)
                for kc in range(16):
                    mm(ps[:, b, 0:n], wbuf[sl][:, kc, j * 128:(j + 1) * 128], hT[:, kc, t0:t0 + n], kc == 0, kc == 15,
                       [t_wb[sl], t_hT[ht]], [pst[b]])
                fn(b, t0, n)

        def proj_tm(sl, w, t0, nrows, ht, lo=4, hi=8):
            b = bank(lo, hi)
            for kc in range(16):
                mm(ps[0:nrows, b, 0:w], hT[:, kc, t0:t0 + nrows], wbuf[sl][:, kc, 0:w], kc == 0, kc == 15,
                   [t_wb[sl], t_hT[ht]], [pst[b]])
            return b

        prefetched = {}

        def prefetch_w(col0, w):
            prefetched[col0] = load_w(wsrc[:, :, col0:col0 + w], w)

        def q_seg(col0, width, nheads):
            for (c0, w) in split512(width):
                if c0 == 0 and col0 in prefetched:
                    sl = prefetched.pop(col0)
                else:
                    sl = load_w(wsrc[:, :, col0 + c0:col0 + c0 + w], w)
                for j in range(w // 128):
                    head = c0 // 128 + j
                    proj_fm(sl, j, OWN_TILES, lambda b, t0, n, head=head: evac(
                        QT[:, head, t0 - OWN0:t0 - OWN0 + n], ps[:, b, 0:n], [pst[b]], [t_QT], scale=QS))

        def g_seg(col0, width, zoff):
            for (c0, w) in split512(width):
                sl = load_w(wsrc[:, :, col0 + c0:col0 + c0 + w], w)
                for j in range(w // 128):
                    zc = zoff + c0 // 128 + j
                    proj_fm(sl, j, OWN_TILES, lambda b, t0, n, zc=zc: S.op(
                        "act", lambda: nc.scalar.activation(out=ZT[:, zc, t0 - OWN0:t0 - OWN0 + n], in_=ps[:, b, 0:n], func=AF.Silu),
                        [pst[b]], [t_ZT[zc]]))

        def epilogue(zc, q0, n, o_ap, den_ap, b_o, b_d):
            if den_ap is not None:
                S.op("dve", lambda: nc.vector.reciprocal(out=rden[:, 0:n], in_=den_ap), [pst[b_d]], [t_rden])
                S.op("dve", lambda: nc.vector.tensor_tensor(out=otmp[:, 0:n], in0=o_ap, in1=rden[:, 0:n], op=ALU.mult),
                     [pst[b_o], t_rden], [t_otmp])
            else:
                S.op("dve", lambda: nc.vector.tensor_copy(out=otmp[:, 0:n], in_=o_ap), [pst[b_o]], [t_otmp])
            S.op("dve", lambda: nc.vector.tensor_tensor(out=ZT[:, zc, q0:q0 + n], in0=ZT[:, zc, q0:q0 + n], in1=otmp[:, 0:n],
                                                        op=ALU.mult), [t_otmp, t_ZT[zc]], [t_ZT[zc]])

        S.op("pool", lambda: nc.gpsimd.memset(KTn[:], 0.0), [], [t_KTn])
        S.op("pool", lambda: nc.gpsimd.memset(Vn[:], 0.0), [], [t_Vn])

        q_seg(0, 768, 6)
        for (c0, w) in split512(768):
            sl = load_w(wsrc[:, :, 768 + c0:768 + c0 + w], w)
            for j in range(w // 128):
                head = c0 // 128 + j
                proj_fm(sl, j, [OWN_TILES[2]], lambda b, t0, n, head=head: evac(KTn[:, head, 0:16], ps[:, b, 0:16], [pst[b]], [t_KTn]))
            b = proj_tm(sl, w, SMP0, 16, 3)
            out_rows(o_sbks[:, c0:c0 + w], 16, [b], [w])
        for (c0, w) in split512(768):
            sl = load_w(wsrc[:, :, 1536 + c0:1536 + c0 + w], w)
            b = proj_tm(sl, w, SMP0, 16, 3)
            evac(Vn[0:16, c0:c0 + w], ps[0:16, b, 0:w], [pst[b]], [t_Vn])
            out_rows(o_sbvs[:, c0:c0 + w], 16, [b], [w])
        g_seg(2304, 768, 0)
        maybe_stop('p2')

        with ExitStack() as esS:
            es2 = esS
            KTh = sb("KTh", (128, 4096), BF16); t_KTh = Tk()
            Vh = sb("Vh", (128, 32, 128), BF16); t_Vh = Tk()
            e_t = [sb("e_t%d" % i, (128, 2, 512), BF16) for i in range(3)]; t_e = [Tk(), Tk(), Tk()]
            sp_t = [sb("sp_t%d" % i, (128, 2, 512), BF16) for i in range(2)]; t_sp = [Tk(), Tk()]
            tt_t = [sb("tt_t%d" % i, (128, 2, 512), BF16) for i in range(2)]; t_tt = [Tk(), Tk()]
            w_t = [sb("w_t%d" % i, (128, 2, 512), BF16) for i in range(2)]; t_w = [Tk(), Tk()]
            P_t = [sb("P_t%d" % i, (128, 2, 512), BF16) for i in range(2)]; t_P = [Tk(), Tk()]
            Kb = [sb("Kb%d" % i, (128, 768), BF16) for i in range(2)]; t_Kb = [Tk(), Tk()]
            Vb = [sb("Vb%d" % i, (128, 768), BF16) for i in range(4)]; t_Vb = [Tk() for _ in range(4)]
            KTb = [sb("KTb%d" % i, (128, 768), BF16) for i in range(2)]; t_KTb = [Tk(), Tk()]
            oacc = sb("oacc", (128, 96), F32); t_oacc = Tk()

            def sweep(n, zmm, e_op, sp_op, s_mm, p_op, t_op, w_op, pv, pre=None):
                if pre is not None:
                    pre(0); pre(1)
                zmm(0)
                for k in range(n):
                    e_op(k)
                    if k > 0:
                        t_op(k - 1); w_op(k - 1)
                    sp_op(k)
                    if pre is not None and k + 2 < n:
                        pre(k + 2)
                    if k < n - 1:
                        zmm(k + 1)
                    s_mm(k); p_op(k)
                    if k > 0:
                        pv(k - 1)
                t_op(n - 1); w_op(n - 1); pv(n - 1)

            def flat(t, c0, c1):
                return t[:].rearrange("p a b -> p (a b)")[:, c0:c1]

            for h in range(6):
                S.dma("sp", KTh[:], kT_scr[:, h, :], "kth", reads=[t_kscr], writes=[t_KTh])
                S.dma("sp", Vh[:], v_scr[:, :, h * 128:(h + 1) * 128], "vh", reads=[t_vscr], writes=[t_Vh])
                S.op("pool", lambda: nc.gpsimd.memset(P_t[0][:], 0.0), [], [t_P[0]])
                S.op("pool", lambda: nc.gpsimd.memset(P_t[1][:], 0.0), [], [t_P[1]])
                nB = lambda k: 2 if k >= 4 else 1

                def zmm(k, h=h):
                    r = 31 - k
                    zb = (k % 2) * 2
                    dA = r - 28
                    mm(ps[:, zb, :], KTh[:, r * 128:(r + 1) * 128], QT[:, h, 512:1024], True, dA < 0, [t_KTh, t_QT], [pst[zb]])
                    if dA >= 0:
                        mm(ps[:, zb, :], ident[:], maskB[:, dA, :], False, True, [t_const], [pst[zb]])
                    if k >= 4:
                        dB = r - 24
                        mm(ps[:, zb + 1, :], KTh[:, r * 128:(r + 1) * 128], QT[:, h, 0:512], True, dB < 0, [t_KTh, t_QT], [pst[zb + 1]])
                        if dB >= 0:
                            mm(ps[:, zb + 1, :], ident[:], maskB[:, dB, :], False, True, [t_const], [pst[zb + 1]])

                def e_op(k):
                    r = 31 - k; zb = (k % 2) * 2; s3 = k % 3; nb = nB(k)
                    S.op("act", lambda: nc.scalar.activation(out=e_t[s3][:, 0:nb, :], in_=ps[:, zb:zb + nb, :], func=AF.Exp, bias=bc[:, r:r + 1]),
                         [pst[zb + i] for i in range(nb)] + [t_const], [t_e[s3]])

                def sp_op(k):
                    s_ = k % 2; nb = nB(k)
                    S.op("act", lambda: nc.scalar.activation(out=sp_t[s_][:, 0:nb, :], in_=e_t[k % 3][:, 0:nb, :], func=AF.Ln, bias=1.0),
                         [t_e[k % 3]], [t_sp[s_]])

                def s_mm(k):
                    s_ = k % 2; pc = k % 2
                    for i in range(nB(k)):
                        mm(ps[:, 4 + i, :], ones[:], P_t[pc][:, i, :], True, False, [t_const, t_P[pc]], [pst[4 + i]])
                    for i in range(nB(k)):
                        mm(ps[:, 4 + i, :], tri[:], sp_t[s_][:, i, :], False, True, [t_const, t_sp[s_]], [pst[4 + i]])

                def p_op(k):
                    s_ = k % 2; pc = k % 2; pn = 1 - pc; nb = nB(k)
                    if k == 31:
                        return
                    S.op("pool", lambda: nc.gpsimd.tensor_tensor(out=flat(P_t[pn], 0, nb * 512), in0=flat(P_t[pc], 0, nb * 512),
                                                                 in1=flat(sp_t[s_], 0, nb * 512), op=ALU.add), [t_sp[s_], t_P[pc]], [t_P[pn]])

                def t_op(k):
                    s_ = k % 2; nb = nB(k)
                    S.op("act", lambda: nc.scalar.activation(out=tt_t[s_][:, 0:nb, :], in_=ps[:, 4:4 + nb, :], func=AF.Exp, scale=-1.0),
                         [pst[4 + i] for i in range(nb)], [t_tt[s_]])

                def w_op(k):
                    s_ = k % 2; nb = nB(k)
                    S.op("dve", lambda: nc.vector.tensor_tensor(out=flat(w_t[s_], 0, nb * 512), in0=flat(e_t[k % 3], 0, nb * 512),
                                                                in1=flat(tt_t[s_], 0, nb * 512), op=ALU.mult), [t_e[k % 3], t_tt[s_]], [t_w[s_]])

                def pv(k):
                    r = 31 - k; s_ = k % 2
                    mm(ps[:, 6, :], Vh[:, r, :], w_t[s_][:, 0, :], k == 0, k == 31, [t_Vh, t_w[s_]], [pst[6]])
                    if k >= 4:
                        mm(ps[:, 7, :], Vh[:, r, :], w_t[s_][:, 1, :], k == 4, k == 31, [t_Vh, t_w[s_]], [pst[7]])

                sweep(32, zmm, e_op, sp_op, s_mm, p_op, t_op, w_op, pv)
                epilogue(h, 512, 512, ps[:, 6, :], None, 6, None)
                epilogue(h, 0, 512, ps[:, 7, :], None, 7, None)
            maybe_stop('sbp')

            S.op("pool", lambda: nc.gpsimd.memset(P_t[0][:], 0.0), [t_P[0]], [t_P[0]])
            S.op("pool", lambda: nc.gpsimd.memset(P_t[1][:], 0.0), [t_P[1]], [t_P[1]])
            NS = 33

            t_KTb2 = [[Tk(), Tk()], [Tk(), Tk()]]

            def pre(k):
                if k == 0:
                    return
                s_ = k % 2
                r = 32 - k
                S.dma("pool", Kb[s_][:], csk[r * 128:(r + 1) * 128, :], "kb%d" % s_, writes=[t_Kb[s_]])
                S.dma("pool", Vb[k % 4][:], csv[r * 128:(r + 1) * 128, :], "vb%d" % (k % 4), writes=[t_Vb[k % 4]])
                tb0 = 2 * s_
                for h in range(6):
                    bb = tb0 + h // 4
                    mm(ps[:, bb, (h % 4) * 128:(h % 4 + 1) * 128], Kb[s_][:, h * 128:(h + 1) * 128], ident[:], True, True,
                       [t_Kb[s_], t_const], [pst[bb]])
                evac(KTb[s_][:, 0:512], ps[:, tb0, :], [pst[tb0]], [t_KTb2[s_][0]], eng="act")
                evac(KTb[s_][:, 512:768], ps[:, tb0 + 1, 0:256], [pst[tb0 + 1]], [t_KTb2[s_][1]], eng="dve")

            def zmm(k):
                s_ = k % 2
                zb = 4 + s_
                if k == 0:
                    for h in range(6):
                        mm(ps[:, zb, h * 16:(h + 1) * 16], KTn[:, h, :], QT[:, h, 1024:1040], True, False, [t_KTn, t_QT], [pst[zb]])
                        mm(ps[:, zb, h * 16:(h + 1) * 16], ident[:], maskS[:, h * 16:(h + 1) * 16], False, True, [t_const], [pst[zb]])
                    return
                for h in range(6):
                    mm(ps[:, zb, h * 16:(h + 1) * 16], KTb[s_][:, h * 128:(h + 1) * 128], QT[:, h, 1024:1040], True, True,
                       [t_KTb2[s_][h // 4], t_QT], [pst[zb]])

            def e_op(k):
                s_ = k % 2; zb = 4 + s_
                S.op("act", lambda: nc.scalar.activation(out=flat(e_t[k % 3], 0, 96), in_=ps[:, zb, 0:96], func=AF.Exp), [pst[zb]], [t_e[k % 3]])

            def sp_op(k):
                s_ = k % 2
                S.op("act", lambda: nc.scalar.activation(out=flat(sp_t[s_], 0, 96), in_=flat(e_t[k % 3], 0, 96), func=AF.Ln, bias=1.0),
                     [t_e[k % 3]], [t_sp[s_]])

            def s_mm(k):
                s_ = k % 2; pc = k % 2
                mm(ps[:, 6, 0:96], ones[:], flat(P_t[pc], 0, 96), True, False, [t_const, t_P[pc]], [pst[6]])
                mm(ps[:, 6, 0:96], tri[:], flat(sp_t[s_], 0, 96), False, True, [t_const, t_sp[s_]], [pst[6]])

            def p_op(k):
                s_ = k % 2; pc = k % 2; pn = 1 - pc
                if k == NS - 1:
                    return
                S.op("pool", lambda: nc.gpsimd.tensor_tensor(out=flat(P_t[pn], 0, 96), in0=flat(P_t[pc], 0, 96), in1=flat(sp_t[s_], 0, 96),
                                                             op=ALU.add), [t_sp[s_], t_P[pc]], [t_P[pn]])

            def t_op(k):
                s_ = k % 2
                S.op("act", lambda: nc.scalar.activation(out=flat(tt_t[s_], 0, 96), in_=ps[:, 6, 0:96], func=AF.Exp, scale=-1.0),
                     [pst[6]], [t_tt[s_]])

            def w_op(k):
                s_ = k % 2
                S.op("dve", lambda: nc.vector.tensor_tensor(out=flat(w_t[s_], 0, 96), in0=flat(e_t[k % 3], 0, 96), in1=flat(tt_t[s_], 0, 96),
                                                            op=ALU.mult), [t_e[k % 3], t_tt[s_]], [t_w[s_]])

            def pv(k):
                s_ = k % 2
                for h in range(6):
                    if k == 0:
                        v_ap = Vn[:, h * 128:(h + 1) * 128]; rv = [t_Vn]
                    else:
                        v_ap = Vb[k % 4][:, h * 128:(h + 1) * 128]; rv = [t_Vb[k % 4]]
                    S.op("pe", lambda: nc.tensor.matmul(ps[:, 7, h * 16:(h + 1) * 16], v_ap, flat(w_t[s_], h * 16, (h + 1) * 16),
                                                        start=True, stop=True), rv + [t_w[s_]], [pst[7]], signal=(h == 5))
                if k == 0:
                    S.op("dve", lambda: nc.vector.tensor_copy(out=oacc[:, :], in_=ps[:, 7, 0:96]), [pst[7]], [t_oacc])
                else:
                    S.op("dve", lambda: nc.vector.tensor_tensor(out=oacc[:, :], in0=oacc[:, :], in1=ps[:, 7, 0:96], op=ALU.add),
                         [pst[7], t_oacc], [t_oacc])

            sweep(NS, zmm, e_op, sp_op, s_mm, p_op, t_op, w_op, pv, pre=pre)
            for h in range(6):
                S.op("dve", lambda: nc.vector.tensor_tensor(out=ZT[:, h, 1024:1040], in0=ZT[:, h, 1024:1040], in1=oacc[:, h * 16:(h + 1) * 16],
                                                            op=ALU.mult), [t_oacc, t_ZT[h]], [t_ZT[h]])
            prefetch_w(3072, 384)
            S.barrier()
            maybe_stop('sb')
        es2 = esM

        with ExitStack() as esB:
            es2 = esB
            KTbd = sb("KTbd", (128, 6, NTOK), BF16); t_KTbd = Tk()
            Vbd = sb("Vbd", (128, 12, 768), BF16); t_Vbd = Tk()
            cK = sb("cK", (128, 4, 768), BF16); t_cK = Tk()
            cV = sb("cV", (128, 4, 768), BF16); t_cV = Tk()
            cKT = sb("cKT", (128, 4, 768), BF16); t_cKT = Tk()
            onesrow = sb("onesrow", (1, 64), BF16); t_or = Tk()
            S.op("pool", lambda: nc.gpsimd.memset(onesrow[:], 1.0), [], [t_or])
            S.op("pool", lambda: nc.gpsimd.memset(KTn[:], 0.0), [t_KTn], [t_KTn])
            S.op("pool", lambda: nc.gpsimd.memset(Vn[:], 0.0), [t_Vn], [t_Vn])
            S.dma("pool", cK[:], cbk.rearrange("(b p) c -> p b c", p=128), "ck", writes=[t_cK])
            S.dma("pool", cV[:], cbv.rearrange("(b p) c -> p b c", p=128), "cv", writes=[t_cV])
            q_seg(3072, 768, 6)
            for (c0, w) in split512(768):
                sl = load_w(wsrc[:, :, 3840 + c0:3840 + c0 + w], w)
                for j in range(w // 128):
                    head = c0 // 128 + j

                    def fn(b, t0, n, head=head):
                        evac(KTbd[:, head, t0:t0 + n], ps[:, b, 0:n], [pst[b]], [t_KTbd])
                        if t0 == SMP0:
                            evac(KTn[:, head, 0:16], ps[:, b, 0:16], [pst[b]], [t_KTn])
                    proj_fm(sl, j, ALL_TILES, fn)
                for blk in range(8, 12):
                    b = proj_tm(sl, w, blk * 128, 128, 2)
                    out_rows(o_bdk[(blk - 8) * 128:(blk - 7) * 128, c0:c0 + w], 128, [b], [w])
                b = proj_tm(sl, w, SMP0, 16, 3)
                out_rows(o_bdks[:, c0:c0 + w], 16, [b], [w])
            for (c0, w) in split512(768):
                sl = load_w(wsrc[:, :, 4608 + c0:4608 + c0 + w], w)
                for blk in range(12):
                    b = proj_tm(sl, w, blk * 128, 128, blk // 4)
                    evac(Vbd[:, blk, c0:c0 + w], ps[:, b, 0:w], [pst[b]], [t_Vbd])
                    if blk >= 8:
                        out_rows(o_bdv[(blk - 8) * 128:(blk - 7) * 128, c0:c0 + w], 128, [b], [w])
                b = proj_tm(sl, w, SMP0, 16, 3)
                evac(Vn[0:16, c0:c0 + w], ps[0:16, b, 0:w], [pst[b]], [t_Vn])
                out_rows(o_bdvs[:, c0:c0 + w], 16, [b], [w])
            g_seg(5376, 768, 6)

            ebT = bT; ebTs = bTs; t_eb = Tk()
            S.op("act", lambda: nc.scalar.activation(out=ebT[:].rearrange("p a b c d -> p (a b c d)"),
                                                     in_=bT[:].rearrange("p a b c d -> p (a b c d)"), func=AF.Exp), [t_const], [t_eb])
            S.op("act", lambda: nc.scalar.activation(out=ebTs[:].rearrange("p a b c -> p (a b c)"),
                                                     in_=bTs[:].rearrange("p a b c -> p (a b c)"), func=AF.Exp), [t_const], [t_eb])
            p3 = [sb("p3_%d" % i, (128, 320), BF16) for i in range(3)]; t_p3 = [Tk(), Tk(), Tk()]
            units = []

            def band_S(u):
                nq, blocks, q_ap, eb_ap, zc, q0 = units[u]
                zb = 2 + u % 3
                for i, (kT_ap, rk, v_ap, rv, neg_ap) in enumerate(blocks):
                    mm(ps[:, zb, i * nq:(i + 1) * nq], kT_ap, q_ap, True, neg_ap is None, rk + [t_QT], [pst[zb]])
                    if neg_ap is not None:
                        mm(ps[:, zb, i * nq:(i + 1) * nq], neg_ap, onesrow[0:1, 0:nq], False, True, [t_const, t_or], [pst[zb]])

            def band_P(u):
                nq, blocks, q_ap, eb_ap, zc, q0 = units[u]
                zb = 2 + u % 3; sl = u % 3; nb = len(blocks) * nq
                S.op("act", lambda: nc.scalar.activation(out=p3[sl][:, 0:nb], in_=ps[:, zb, 0:nb], func=AF.Exp), [pst[zb]], [t_p3[sl]])
                S.op("dve", lambda: nc.vector.tensor_tensor(out=p3[sl][:, 0:nb], in0=p3[sl][:, 0:nb], in1=eb_ap, op=ALU.mult),
                     [t_p3[sl], t_eb], [t_p3[sl]])

            def band_O(u):
                nq, blocks, q_ap, eb_ap, zc, q0 = units[u]
                ob = 5 + u % 3; sl = u % 3; nbk = len(blocks)
                for i, (kT_ap, rk, v_ap, rv, neg_ap) in enumerate(blocks):
                    S.op("pe", lambda: nc.tensor.matmul(ps[:, ob, 0:nq], v_ap, p3[sl][:, i * nq:(i + 1) * nq], start=(i == 0),
                                                        stop=(i == nbk - 1)), rv + [t_p3[sl]], [pst[ob]], signal=False)
                for i in range(nbk):
                    mm(ps[:, ob, 64:64 + nq], ones[:], p3[sl][:, i * nq:(i + 1) * nq], i == 0, i == nbk - 1, [t_const, t_p3[sl]], [pst[ob]])
                epilogue(zc, q0, nq, ps[:, ob, 0:nq], ps[:, ob, 64:64 + nq], ob, ob)

            for n in range(16):
                g0 = n // 2; par = n % 2
                for h in range(6):
                    blocks = []
                    for i in range(5):
                        g = g0 + i
                        blocks.append((KTbd[:, h, g * 128:(g + 1) * 128], [t_KTbd], Vbd[:, g, h * 128:(h + 1) * 128], [t_Vbd],
                                       negrow[0:1, g, :] if g < 4 else None))
                    units.append((64, blocks, QT[:, h, n * 64:(n + 1) * 64], ebT[:, par, h, :, :].rearrange("p a b -> p (a b)"), 6 + h, n * 64))
            for blk in range(4):
                for h in range(6):
                    bb = h // 4
                    mm(ps[:, bb, (h % 4) * 128:(h % 4 + 1) * 128], cK[:, blk, h * 128:(h + 1) * 128], ident[:], True, True,
                       [t_cK, t_const], [pst[bb]])
                evac(cKT[:, blk, 0:512], ps[:, 0, :], [pst[0]], [t_cKT])
                evac(cKT[:, blk, 512:768], ps[:, 1, 0:256], [pst[1]], [t_cKT])
            for h in range(6):
                blocks = []
                for i in range(4):
                    blocks.append((cKT[:, i, h * 128:(h + 1) * 128], [t_cKT], cV[:, i, h * 128:(h + 1) * 128], [t_cV], None))
                blocks.append((KTn[:, h, :], [t_KTn], Vn[:, h * 128:(h + 1) * 128], [t_Vn], None))
                units.append((16, blocks, QT[:, h, 1024:1040], ebTs[:, h, :, :].rearrange("p a b -> p (a b)"), 6 + h, 1024))
            NU = len(units)
            band_S(0); band_S(1); band_P(0)
            for u in range(NU):
                if u + 2 < NU:
                    band_S(u + 2)
                if u + 1 < NU:
                    band_P(u + 1)
                band_O(u)
            prefetch_w(6144, 384)
            S.barrier()
            maybe_stop('band')
        es2 = esM

        with ExitStack() as esQ:
            es2 = esQ
            pm = sb("pm", (128, 2, 512), BF16); t_pm = Tk()
            cmK = sb("cmK", (128, 2, 512), BF16); t_cmK = Tk()
            cmV = sb("cmV", (128, 2, 512), BF16); t_cmV = Tk()
            cmKT = sb("cmKT", (128, 4, 256), BF16); t_cmKT = Tk()
            S.dma("pool", cmK[:], cmk.rearrange("(b p) c -> p b c", p=128), "ck", writes=[t_cmK])
            S.dma("pool", cmV[:], cmv.rearrange("(b p) c -> p b c", p=128), "cv", writes=[t_cmV])
            xst[0] = sb("xstM0", (128, D), F32); xst[1] = sb("xstM1", (128, D), F32)
            xn[0] = sb("xnM0", (128, D), BF16); xn[1] = sb("xnM1", (128, D), BF16)
            gmb = sb("gmb", (128, D), F32)
            hmT = sb("hmT", (128, 16, 256), BF16); t_hmT = Tk()
            S.dma("sp", gmb[:], gmem_b, "gpb", writes=[t_gpb])
            for mb in range(2):
                prep_block(memx[mb * 128:(mb + 1) * 128, :], 128, gmb, mb)
            q_seg(6144, 512, 4)
            g_seg(6656, 512, 12)
            for mb in range(2):
                transpose_block(mb, 128, hmT[:, :, mb * 128:(mb + 1) * 128], t_hmT)
            wm = w_mkv.rearrange("(kc p) c -> p kc c", p=128)
            for (c0, w) in split512(512):
                sl = load_w(wm[:, :, c0:c0 + w], w)
                for j in range(w // 128):
                    head = c0 // 128 + j
                    b = bank(0, 4)
                    for kc in range(16):
                        mm(ps[:, b, 0:256], wbuf[sl][:, kc, j * 128:(j + 1) * 128], hmT[:, kc, :], kc == 0, kc == 15, [t_wb[sl], t_hmT], [pst[b]])
                    evac(mkT[:, head, :], ps[:, b, 0:256], [pst[b]], [t_mkT])
                for mb in range(2):
                    b = bank(4, 8)
                    for kc in range(16):
                        mm(ps[:, b, 0:w], hmT[:, kc, mb * 128:(mb + 1) * 128], wbuf[sl][:, kc, 0:w], kc == 0, kc == 15, [t_wb[sl], t_hmT], [pst[b]])
                    out_rows(o_mk[mb * 128:(mb + 1) * 128, c0:c0 + w], 128, [b], [w])
            for (c0, w) in split512(512):
                sl = load_w(wm[:, :, 512 + c0:512 + c0 + w], w)
                for mb in range(2):
                    b = bank(4, 8)
                    for kc in range(16):
                        mm(ps[:, b, 0:w], hmT[:, kc, mb * 128:(mb + 1) * 128], wbuf[sl][:, kc, 0:w], kc == 0, kc == 15, [t_wb[sl], t_hmT], [pst[b]])
                    evac(mvb[:, mb, c0:c0 + w], ps[:, b, 0:w], [pst[b]], [t_mvb])
                    out_rows(o_mv[mb * 128:(mb + 1) * 128, c0:c0 + w], 128, [b], [w])
            for mb in range(2):
                for h in range(4):
                    mm(ps[:, 0, h * 128:(h + 1) * 128], cmK[:, mb, h * 128:(h + 1) * 128], ident[:], True, True, [t_cmK, t_const], [pst[0]])
                for h in range(4):
                    evac(cmKT[:, h, mb * 128:(mb + 1) * 128], ps[:, 0, h * 128:(h + 1) * 128], [pst[0]], [t_cmKT])

            pm2 = [pm, sb("pmB", (128, 2, 512), BF16)]; t_pm2 = [t_pm, Tk()]
            mu = [0]

            def mem_unit(n, q_ap, kT_of, rk, v_of, rv, zc, q0):
                mu[0] += 1
                u = mu[0] % 2
                zb = 4 * u; pmx = pm2[u]; tpm = t_pm2[u]
                for mb in range(2):
                    mm(ps[:, zb + mb, 0:n], kT_of(mb), q_ap, True, True, rk + [t_QT], [pst[zb + mb]])
                for mb in range(2):
                    S.op("act", lambda: nc.scalar.activation(out=pmx[:, mb, 0:n], in_=ps[:, zb + mb, 0:n], func=AF.Exp), [pst[zb + mb]], [tpm])
                for mb in range(2):
                    mm(ps[:, zb + 2, 0:n], v_of(mb), pmx[:, mb, 0:n], mb == 0, mb == 1, rv + [tpm], [pst[zb + 2]])
                for mb in range(2):
                    mm(ps[:, zb + 3, 0:n], ones[:], pmx[:, mb, 0:n], mb == 0, mb == 1, [t_const, tpm], [pst[zb + 3]])
                epilogue(zc, q0, n, ps[:, zb + 2, 0:n], ps[:, zb + 3, 0:n], zb + 2, zb + 3)

            for h in range(4):
                for qt in range(2):
                    mem_unit(512, QT[:, h, qt * 512:(qt + 1) * 512], lambda mb, h=h: mkT[:, h, mb * 128:(mb + 1) * 128], [t_mkT],
                             lambda mb, h=h: mvb[:, mb, h * 128:(h + 1) * 128], [t_mvb], 12 + h, qt * 512)
                mem_unit(16, QT[:, h, 1024:1040], lambda mb, h=h: cmKT[:, h, mb * 128:(mb + 1) * 128], [t_cmKT],
                         lambda mb, h=h: cmV[:, mb, h * 128:(h + 1) * 128], [t_cmV], 12 + h, 1024)
            S.barrier()
            maybe_stop('mem')
        es2 = esM

        with ExitStack() as esG:
            es2 = esG
            acc = sb("acc", (128, 2, NQ), F32); t_acc = Tk()
            mst = sb("mst", (128, 2, NQ), BF16); t_mst = Tk()
            sg = [sb("sg%d" % i, (128, 512), BF16) for i in range(2)]; t_sg = [Tk(), Tk()]
            tmp = [sb("tmp%d" % i, (128, 512), F32) for i in range(2)]; t_tmp = [Tk(), Tk()]
            wup = [sb("wup%d" % i, (128, 6, 256), BF16) for i in range(2)]; t_wup = [Tk(), Tk()]
            zoffs = [0, 6, 12]; nkcs = [6, 6, 4]
            gi = 0
            groups = [(cq, br) for cq in range(8) for br in range(3)]

            def merge_load(g):
                cq, br = groups[g]
                col = 7168 + br * 2048 + cq * 256
                sl_ = load_w(wsrc[:, :, col:col + 256], 256)
                us_ = g % 2
                S.dma("pool", wup[us_][:, 0:nkcs[br], :], w_up[br].rearrange("(kc p) c -> p kc c", p=128)[:, :, cq * 256:(cq + 1) * 256],
                      "wup%d" % us_, writes=[t_wup[us_]])
                return sl_, us_

            nxt = merge_load(0)
            for g, (cq, br) in enumerate(groups):
                    sl, us = nxt
                    if g + 1 < len(groups):
                        nxt = merge_load(g + 1)
                    for j in range(2):
                        for (t0, n, ht) in OWN_TILES:
                            gi += 1
                            s2 = gi % 2
                            q0 = t0 - OWN0
                            b1 = bank(0, 4)
                            for kc in range(16):
                                mm(ps[:, b1, 0:n], wbuf[sl][:, kc, j * 128:(j + 1) * 128], hT[:, kc, t0:t0 + n], kc == 0, kc == 15,
                                   [t_wb[sl], t_hT[ht]], [pst[b1]])
                            S.op("act", lambda: nc.scalar.activation(out=sg[s2][:, 0:n], in_=ps[:, b1, 0:n], func=AF.Sigmoid),
                                 [pst[b1]], [t_sg[s2]])
                            b2 = bank(4, 8)
                            nk = nkcs[br]
                            for kc in range(nk):
                                mm(ps[:, b2, 0:n], wup[us][:, kc, j * 128:(j + 1) * 128], ZT[:, zoffs[br] + kc, q0:q0 + n], kc == 0, kc == nk - 1,
                                   [t_wup[us]] + [t_ZT[zoffs[br] + kc]], [pst[b2]])
                            if br == 0:
                                S.op("dve", lambda: nc.vector.tensor_tensor(out=acc[:, j, q0:q0 + n], in0=ps[:, b2, 0:n], in1=sg[s2][:, 0:n],
                                                                            op=ALU.mult), [pst[b2], t_sg[s2]], [t_acc])
                            else:
                                S.op("dve", lambda: nc.vector.tensor_tensor(out=tmp[s2][:, 0:n], in0=ps[:, b2, 0:n], in1=sg[s2][:, 0:n],
                                                                            op=ALU.mult), [pst[b2], t_sg[s2]], [t_tmp[s2]])
                                if br == 1:
                                    S.op("dve", lambda: nc.vector.tensor_tensor(out=acc[:, j, q0:q0 + n], in0=acc[:, j, q0:q0 + n],
                                                                                in1=tmp[s2][:, 0:n], op=ALU.add), [t_tmp[s2], t_acc], [t_acc])
                                else:
                                    S.op("dve", lambda: nc.vector.tensor_tensor(out=mst[:, j, q0:q0 + n], in0=acc[:, j, q0:q0 + n],
                                                                                in1=tmp[s2][:, 0:n], op=ALU.add), [t_tmp[s2], t_acc], [t_mst])
                    if br == 2:
                        S.dma("sp", mg_scr[:, cq * 2:(cq + 1) * 2, :], mst[:], "mst", reads=[t_mst], writes=[t_mgscr])
            S.barrier()
            maybe_stop('merge')
        es2 = esM
    esH.close()
    es2 = es

    with ExitStack() as esF:
        es2 = esF
        mergedT = sb("mergedT", (128, 16, NQ), BF16); t_mg = Tk()
        wo2 = [sb("wo2_%d" % i, (128, 16, 512), BF16) for i in range(2)]; t_wo2 = [Tk(), Tk()]
        gp = sb("gp", (128, D), F32); t_gp = Tk()
        xst[0] = sb("xstF0", (128, D), F32); xst[1] = sb("xstF1", (128, D), F32)
        junk[0] = sb("junkF", (128, 512), BF16)
        ysb = [sb("ysb%d" % i, (128, D), F32) for i in range(2)]; t_ysb = [Tk(), Tk()]
        ypre = sb("ypre", (128, 9, D), F32); t_ypre = [Tk() for _ in range(9)]
        ssq = sb("ssq", (128, 36), F32); t_ssq = [Tk() for _ in range(9)]
        wo = w_out.rearrange("(kc p) c -> p kc c", p=128)
        S.dma("pool", wo2[0][:], wo[:, :, 0:512], "wo0", writes=[t_wo2[0]])
        t_mgh = [Tk(), Tk()]
        S.dma("sp", mergedT[:, :, 0:512], mg_scr[:, :, 0:512], "mgld0", reads=[t_mgscr], writes=[t_mgh[0]])
        S.dma("sp", mergedT[:, :, 512:NQ], mg_scr[:, :, 512:NQ], "mgld1", reads=[t_mgscr], writes=[t_mgh[1]])
        S.dma("sp", gp[:], gpost, "gpF", writes=[t_gp])
        for cg in range(4):
            sl = cg % 2
            if cg + 1 < 4:
                S.dma("pool", wo2[1 - sl][:], wo[:, :, (cg + 1) * 512:(cg + 2) * 512], "wo%d" % (1 - sl), writes=[t_wo2[1 - sl]])
            for tb in range(9):
                nr = 128 if tb < 8 else 16
                q0 = tb * 128
                b = bank(0, 8)
                for kc in range(16):
                    mm(ps[0:nr, b, :], mergedT[:, kc, q0:q0 + nr], wo2[sl][:, kc, :], kc == 0, kc == 15, [t_mgh[0 if tb < 4 else 1], t_wo2[sl]], [pst[b]])
                S.op("act", lambda: nc.scalar.activation(out=junk[0][0:nr, :], in_=ps[0:nr, b, :], func=AF.Square,
                                                         accum_out=ssq[0:nr, tb * 4 + cg:tb * 4 + cg + 1]), [pst[b]], [t_junk, t_ssq[tb]])
                if cg == 3:
                    S.op("act", lambda: nc.scalar.activation(out=ypre[0:nr, tb, cg * 512:(cg + 1) * 512], in_=ps[0:nr, b, :], func=AF.Copy),
                         [pst[b]], [t_ypre[tb]])
                else:
                    S.op("dve", lambda: nc.vector.tensor_copy(out=ypre[0:nr, tb, cg * 512:(cg + 1) * 512], in_=ps[0:nr, b, :]),
                         [pst[b]], [t_ypre[tb]])
                if cg == 3:
                    dsto = o_y[tb * 128:(tb + 1) * 128, :] if tb < 8 else o_ys
                    s2 = tb % 2

                    def ld_x(t_):
                        n_ = 128 if t_ < 8 else 16
                        src_ = xctx[3072 + t_ * 128:3072 + (t_ + 1) * 128, :] if t_ < 8 else xs
                        S.dma("pool", xst[t_ % 2][0:n_, :], src_, "xstF%d" % (t_ % 2), writes=[t_xst[t_ % 2]])
                    if tb == 0:
                        ld_x(0)
                    if tb + 1 < 9:
                        ld_x(tb + 1)
                    S.op("dve", lambda: nc.vector.tensor_reduce(out=stat[0:nr, 0:1], in_=ssq[0:nr, tb * 4:tb * 4 + 4], axis=AX.X, op=ALU.add),
                         [t_ssq[tb]], [t_stat])
                    S.op("act", lambda: nc.scalar.activation(out=stat[0:nr, 1:2], in_=stat[0:nr, 0:1], func=AF.Ln, scale=1.0 / D, bias=EPS),
                         [t_stat], [t_stat])
                    S.op("act", lambda: nc.scalar.activation(out=stat[0:nr, 2:3], in_=stat[0:nr, 1:2], func=AF.Exp, scale=-0.5), [t_stat], [t_stat])
                    S.op("dve", lambda: nc.vector.scalar_tensor_tensor(out=ysb[s2][0:nr, :], in0=ypre[0:nr, tb, :], scalar=stat[0:nr, 2:3],
                                                                       in1=gp[0:nr, :], op0=ALU.mult, op1=ALU.mult),
                         [t_ypre[tb], t_stat, t_gp], [t_ysb[s2]])
                    S.op("dve", lambda: nc.vector.tensor_tensor(out=ysb[s2][0:nr, :], in0=ysb[s2][0:nr, :], in1=xst[s2][0:nr, :], op=ALU.add),
                         [t_ysb[s2], t_xst[s2]], [t_ysb[s2]])
                    S.dma("sp", dsto, ysb[s2][0:nr, :], "yst%d" % s2, reads=[t_ysb[s2]])
        S.finish("sp")
        S.barrier()
    es.close()
    return nc


def _host_consts(rel_bias, j):
    k = np.arange(128)
    ident = np.eye(128, dtype=np.float32)
    tri = (k[:, None] >= k[None, :]).astype(np.float32)
    ones = np.ones((128, 128), np.float32)
    maskB = np.zeros((128, 4, 4, 128), np.float32)
    for i in range(4):
        for s in range(4):
            if s < i:
                maskB[:, i, s, :] = NEG
            elif s == i:
                maskB[:, i, s, :] = np.where(k[:, None] < k[None, :], 0.0, NEG)
    maskB = maskB.reshape(128, 4 * 512)
    q16 = np.arange(16)
    mS = np.where((k[:, None] < 16) & (k[:, None] < q16[None, :]), 0.0, NEG).astype(np.float32)
    maskS = np.tile(mS, (1, 6))
    bc = np.zeros((128, 32), np.float32)
    bc[:, :24 - 8 * j] = NEG
    rb = rel_bias
    bT = np.zeros((128, 2, 6, 5, 64), np.float32)
    ql = np.arange(64)
    for par in range(2):
        for i in range(5):
            kl = i * 128 + k - 64 * par
            valid = (kl >= 0) & (kl < 576)
            idx = np.clip(512 + ql[None, :] - kl[:, None], -256, 256) + 256
            for h in range(6):
                bT[:, par, h, i, :] = np.where(valid[:, None], rb[h][idx], NEG)
    bTs = np.zeros((128, 6, 5, 16), np.float32)
    for i in range(5):
        kl = i * 128 + k
        valid = kl < 528
        idx = np.clip(512 + q16[None, :] - kl[:, None], -256, 256) + 256
        for h in range(6):
            bTs[:, h, i, :] = np.where(valid[:, None], rb[h][idx], NEG)
    negrow = np.zeros((1, 12, 128), np.float32)
    if j == 0:
        negrow[:, 0:4, :] = NEG
    return dict(c_ident=ident, c_tri=tri, c_ones=ones, c_maskB=maskB, c_maskS=maskS.astype(np.float32), c_bc=bc,
                c_bT=bT.reshape(128, -1), c_bTs=bTs.reshape(128, -1), c_negrow=negrow.reshape(1, -1))


_PROG = [None]


def kernel(x_prompt, x_sample, cache_sb_k, cache_sb_v, cache_band_k, cache_band_v, cache_mem_k, cache_mem_v,
           mem_prompt, g_pre, w_in, rel_bias, g_mem, w_mem_kv, w_up_sb, w_up_band, w_up_mem, w_out, g_post):
    f = lambda a: np.ascontiguousarray(np.asarray(a, dtype=np.float32))
    x_prompt = f(x_prompt); x_sample = f(x_sample); mem_prompt = f(mem_prompt)
    if _PROG[0] is None:
        _PROG[0] = build_program()
    nc = _PROG[0]
    shared = dict(
        w_in=f(w_in[0]), w_mkv=f(w_mem_kv[0]), w_up_sb=f(w_up_sb[0]), w_up_bd=f(w_up_band[0]), w_up_mm=f(w_up_mem[0]),
        w_out=f(w_out[0]),
        gpreT=f(np.asarray(g_pre[0]).reshape(16, 128).T), gmemT=f(np.asarray(g_mem[0]).reshape(16, 128).T),
        gpost=f(np.broadcast_to(np.asarray(g_post[0])[None, :], (128, D))),
        gpre_b=f(np.broadcast_to(np.asarray(g_pre[0])[None, :], (128, D))),
        gmem_b=f(np.broadcast_to(np.asarray(g_mem[0])[None, :], (128, D))),
    )
    rb = f(rel_bias[0])
    consts = [_host_consts(rb, j) for j in range(4)]
    in_maps = []
    for c in range(8):
        b, j = c // 4, c % 4
        xc = np.zeros((4096, D), np.float32)
        lo = 1024 * j - 3072
        src_lo = max(lo, 0)
        xc[src_lo - lo:, :] = x_prompt[b, src_lo:1024 * (j + 1), :]
        m = dict(shared)
        m.update(consts[j])
        m.update(
            xctx=xc, xs=f(x_sample[c]), memx=f(mem_prompt[b]),
            csk=f(np.asarray(cache_sb_k[0, c]).reshape(4096, 768)), csv=f(np.asarray(cache_sb_v[0, c]).reshape(4096, 768)),
            cbk=f(np.asarray(cache_band_k[0, c]).reshape(512, 768)), cbv=f(np.asarray(cache_band_v[0, c]).reshape(512, 768)),
            cmk=f(np.asarray(cache_mem_k[0, c]).reshape(256, 512)), cmv=f(np.asarray(cache_mem_v[0, c]).reshape(256, 512)),
        )
        in_maps.append(m)
    res = run_bass_kernel_spmd(nc, in_maps, core_ids=list(range(8)))
    R = res.results
    y_p = np.zeros((2, 4096, D), np.float32)
    y_s = np.zeros((8, 16, D), np.float32)
    sbk_p = np.zeros((1, 2, 4096, 6, 128), np.float32); sbv_p = np.zeros_like(sbk_p)
    bdk_p = np.zeros((1, 2, 512, 6, 128), np.float32); bdv_p = np.zeros_like(bdk_p)
    mk_p = np.zeros((1, 2, 256, 4, 128), np.float32); mv_p = np.zeros_like(mk_p)
    sbk_s = np.zeros((1, 8, 16, 6, 128), np.float32); sbv_s = np.zeros_like(sbk_s)
    bdk_s = np.zeros_like(sbk_s); bdv_s = np.zeros_like(sbk_s)
    for c in range(8):
        b, j = c // 4, c % 4
        r = R[c]
        y_p[b, 1024 * j:1024 * (j + 1)] = r["o_y"]
        y_s[c] = r["o_ys"]
        sbk_p[0, b, 1024 * j:1024 * (j + 1)] = r["o_sbk"].reshape(1024, 6, 128)
        sbv_p[0, b, 1024 * j:1024 * (j + 1)] = r["o_sbv"].reshape(1024, 6, 128)
        if j == 3:
            bdk_p[0, b] = r["o_bdk"].reshape(512, 6, 128)
            bdv_p[0, b] = r["o_bdv"].reshape(512, 6, 128)
        if j == 0:
            mk_p[0, b] = r["o_mk"].reshape(256, 4, 128)
            mv_p[0, b] = r["o_mv"].reshape(256, 4, 128)
        sbk_s[0, c] = r["o_sbks"].reshape(16, 6, 128); sbv_s[0, c] = r["o_sbvs"].reshape(16, 6, 128)
        bdk_s[0, c] = r["o_bdks"].reshape(16, 6, 128); bdv_s[0, c] = r["o_bdvs"].reshape(16, 6, 128)
    return (y_p, y_s, sbk_p, sbv_p, bdk_p, bdv_p, mk_p, mv_p, sbk_s, sbv_s, bdk_s, bdv_s)
```
